# Optimizing a Trainium2 kernel written in Bass

```python
import math
import jax
import jax.numpy as jnp
from jax import lax
import numpy as np

D_MODEL = 1024
BATCH = 8
SEQ = 2048
DEPTH = 2

GRID_W = 64
CTX_LEN = 256
HEAD_DIM = 64
N_GROUPS = 4
GROUP_HEADS = D_MODEL // (N_GROUPS * HEAD_DIM)
GROUP_WIDTH = GROUP_HEADS * HEAD_DIM
MIX_WIDTH = N_GROUPS * GROUP_WIDTH
A_KV_HEADS = GROUP_HEADS // 2
B_KV_HEADS = GROUP_HEADS // 2
D_SUB = HEAD_DIM // 2
WINDOW = 128
BLOCK = 128
NA_KR = 8
NA_KC = 16
D_FF = 2816
ROPE_BASE = 10000.0
NORM_EPS = 1e-6
N_MOD = 9
NEG_INF = -1e30
IN_SPLITS = (GROUP_WIDTH, A_KV_HEADS * HEAD_DIM, A_KV_HEADS * HEAD_DIM,
             GROUP_WIDTH, B_KV_HEADS * HEAD_DIM, B_KV_HEADS * HEAD_DIM,
             GROUP_WIDTH, GROUP_WIDTH, GROUP_WIDTH,
             GROUP_WIDTH, GROUP_WIDTH, GROUP_WIDTH)
IN_WIDTH = int(sum(IN_SPLITS))
IN_SPLIT_OFFSETS = tuple(int(o) for o in np.cumsum(IN_SPLITS)[:-1])

kernel_name = 'hybrid_parallel_heads_diffusion_block'


def rms_norm(x, g=None):
    xf = x.astype(jnp.float32)
    y = xf * lax.rsqrt(jnp.mean(xf * xf, axis=-1, keepdims=True) + NORM_EPS)
    if g is not None:
        y = y * g.astype(jnp.float32)
    return y.astype(x.dtype)


def modulate(x, shift, scale):
    return rms_norm(x) * (1.0 + scale) + shift


def swiglu(h, w_gate, w_up, w_down):
    return (jax.nn.silu(h @ w_gate) * (h @ w_up)) @ w_down


def axial_rope(n_tokens, dim):
    t = jnp.arange(n_tokens, dtype=jnp.int32)
    row = (t // GRID_W).astype(jnp.float32)
    col = (t % GRID_W).astype(jnp.float32)
    half = dim // 2
    freqs = ROPE_BASE ** (-jnp.arange(0, half, 2, dtype=jnp.float32) / half)
    ang_r = row[:, None] * freqs[None, :]
    ang_c = col[:, None] * freqs[None, :]
    ang = jnp.concatenate([ang_r, ang_r, ang_c, ang_c], axis=-1)
    return jnp.cos(ang), jnp.sin(ang)


def apply_rope(x, cos, sin):
    d = x.shape[-1]
    h = d // 2
    qd = h // 2

    def rot(u):
        return jnp.concatenate([-u[..., qd:], u[..., :qd]], axis=-1)

    rotated = jnp.concatenate([rot(x[..., :h]), rot(x[..., h:])], axis=-1)
    shape = (x.shape[1],) + (1,) * (x.ndim - 3) + (d,)
    out = x.astype(jnp.float32) * cos.reshape(shape) + rotated.astype(jnp.float32) * sin.reshape(shape)
    return out.astype(x.dtype)


def window_attention(q, k, v, k_ctx, v_ctx, sink):
    B, S, H, dh = q.shape
    hkv = k.shape[2]
    G = H // hkv
    L = k_ctx.shape[1]
    nb = S // BLOCK
    span = BLOCK + 2 * WINDOW
    pad = ((0, 0), (WINDOW, WINDOW), (0, 0), (0, 0))
    idx = jnp.arange(nb)[:, None] * BLOCK + jnp.arange(span)[None, :]
    kb = jnp.pad(k, pad)[:, idx]
    vb = jnp.pad(v, pad)[:, idx]
    qb = q.reshape(B, nb, BLOCK, hkv, G, dh)
    scale = dh ** -0.5
    s_loc = jnp.einsum('bnqhgd,bnjhd->bnhgqj', qb, kb, preferred_element_type=jnp.float32) * scale
    q_pos = jnp.arange(nb)[:, None] * BLOCK + jnp.arange(BLOCK)[None, :]
    k_pos = idx - WINDOW
    valid = ((jnp.abs(k_pos[:, None, :] - q_pos[:, :, None]) <= WINDOW)
             & (k_pos[:, None, :] >= 0) & (k_pos[:, None, :] < S))
    s_loc = jnp.where(valid[None, :, None, None], s_loc, NEG_INF)
    s_ctx = jnp.einsum('bnqhgd,bjhd->bnhgqj', qb, k_ctx, preferred_element_type=jnp.float32) * scale
    s_sink = jnp.broadcast_to(sink.astype(jnp.float32).reshape(1, 1, hkv, G, 1, 1), s_loc.shape[:-1] + (1,))
    p = jax.nn.softmax(jnp.concatenate([s_loc, s_ctx, s_sink], axis=-1), axis=-1).astype(v.dtype)
    out = (jnp.einsum('bnhgqj,bnjhd->bnqhgd', p[..., :span], vb)
           + jnp.einsum('bnhgqj,bjhd->bnqhgd', p[..., span:span + L], v_ctx))
    return out.reshape(B, S, H, dh)


def context_sink_attention(q, k, v, sink):
    B, L, H, dh = q.shape
    hkv = k.shape[2]
    G = H // hkv
    qg = q.reshape(B, L, hkv, G, dh)
    s = jnp.einsum('bqhgd,bjhd->bhgqj', qg, k, preferred_element_type=jnp.float32) * (dh ** -0.5)
    s_sink = jnp.broadcast_to(sink.astype(jnp.float32).reshape(1, hkv, G, 1, 1), s.shape[:-1] + (1,))
    p = jax.nn.softmax(jnp.concatenate([s, s_sink], axis=-1), axis=-1).astype(v.dtype)
    out = jnp.einsum('bhgqj,bjhd->bqhgd', p[..., :L], v)
    return out.reshape(B, L, H, dh)


def blocked_gqa(q, k, v):
    B, T, H, dh = q.shape
    hkv = k.shape[2]
    G = H // hkv
    nb = T // BLOCK
    scale = dh ** -0.5
    qb = jnp.moveaxis(q.reshape(B, nb, BLOCK, hkv, G, dh), 1, 0)

    def one_block(qblk):
        s = jnp.einsum('bqhgd,bjhd->bhgqj', qblk, k, preferred_element_type=jnp.float32) * scale
        p = jax.nn.softmax(s, axis=-1).astype(v.dtype)
        return jnp.einsum('bhgqj,bjhd->bqhgd', p, v)

    out = lax.map(one_block, qb)
    return jnp.moveaxis(out, 0, 1).reshape(B, T, H, dh)


def neighbourhood_attention(q, k, v, k_ctx, v_ctx, rel_bias, rows):
    B, S, H, dh = q.shape
    kr = min(NA_KR, rows)
    r = jnp.arange(rows)
    r_start = jnp.clip(r - kr // 2, 0, rows - kr)
    key_rows = r_start[:, None] + jnp.arange(kr)[None, :]
    kg = k.reshape(B, rows, GRID_W, H, dh)[:, key_rows].reshape(B, rows, kr * GRID_W, H, dh)
    vg = v.reshape(B, rows, GRID_W, H, dh)[:, key_rows].reshape(B, rows, kr * GRID_W, H, dh)
    qg = q.reshape(B, rows, GRID_W, H, dh)
    scale = dh ** -0.5
    s_loc = jnp.einsum('brqhd,brjhd->brhqj', qg, kg, preferred_element_type=jnp.float32) * scale
    col = jnp.arange(GRID_W)
    c_start = jnp.clip(col - NA_KC // 2, 0, GRID_W - NA_KC)
    col_ok = (col[None, :] >= c_start[:, None]) & (col[None, :] < c_start[:, None] + NA_KC)
    dr_idx = key_rows - r[:, None] + NA_KR - 1
    dc_idx = jnp.clip(col[None, :] - col[:, None] + NA_KC - 1, 0, 2 * NA_KC - 2)
    bias = rel_bias[:, dr_idx[:, None, :, None], dc_idx[None, :, None, :]]
    bias = bias.reshape(H, rows, GRID_W, kr * GRID_W).transpose(1, 0, 2, 3)
    mask = jnp.broadcast_to(col_ok[:, None, :], (GRID_W, kr, GRID_W)).reshape(GRID_W, kr * GRID_W)
    s_loc = jnp.where(mask, s_loc + bias.astype(jnp.float32), NEG_INF)
    s_ctx = jnp.einsum('brqhd,bjhd->brhqj', qg, k_ctx, preferred_element_type=jnp.float32) * scale
    n_loc = kr * GRID_W
    p = jax.nn.softmax(jnp.concatenate([s_loc, s_ctx], axis=-1), axis=-1).astype(v.dtype)
    out = (jnp.einsum('brhqj,brjhd->brqhd', p[..., :n_loc], vg)
           + jnp.einsum('brhqj,bjhd->brqhd', p[..., n_loc:], v_ctx))
    return out.reshape(B, S, H, dh)


def blocked_diff_attention(q, k, v, lam):
    B, T, H = q.shape[:3]
    nb = T // BLOCK
    scale = D_SUB ** -0.5
    qb = jnp.moveaxis(q.reshape(B, nb, BLOCK, H, 2, D_SUB), 1, 0)

    def one_block(qblk):
        s = jnp.einsum('bqhtd,bjhtd->bhtqj', qblk, k, preferred_element_type=jnp.float32) * scale
        p = jax.nn.softmax(s, axis=-1)
        a = (p[:, :, 0] - lam * p[:, :, 1]).astype(v.dtype)
        return jnp.einsum('bhqj,bjhe->bqhe', a, v)

    out = lax.map(one_block, qb)
    return jnp.moveaxis(out, 0, 1).reshape(B, T, H, v.shape[-1])


def hybrid_mixer(h, hc, rows, w_in, w_out, sink, q_norm_g, k_norm_g, rel_bias, lam, lam_init, subln_g,
                 with_ctx_out):
    B, S, _ = h.shape
    L = hc.shape[1]
    cos_h, sin_h = axial_rope(S, HEAD_DIM)
    cos_s, sin_s = axial_rope(S, D_SUB)
    aq, ak, av, bq, bk, bv, cq, ck, cv, dq, dk, dv = jnp.split(h @ w_in, IN_SPLIT_OFFSETS, axis=-1)
    aqc, akc, avc, bqc, bkc, bvc, cqc, ckc, cvc, dqc, dkc, dvc = jnp.split(hc @ w_in, IN_SPLIT_OFFSETS, axis=-1)

    def heads(t, n):
        return t.reshape(t.shape[0], t.shape[1], n, HEAD_DIM)

    def sub_heads(t):
        return t.reshape(t.shape[0], t.shape[1], GROUP_HEADS, 2, D_SUB)

    ka_c, va_c = heads(akc, A_KV_HEADS), heads(avc, A_KV_HEADS)
    y_a = window_attention(apply_rope(heads(aq, GROUP_HEADS), cos_h, sin_h),
                           apply_rope(heads(ak, A_KV_HEADS), cos_h, sin_h),
                           heads(av, A_KV_HEADS), ka_c, va_c, sink)
    kb_c = rms_norm(heads(bkc, B_KV_HEADS), k_norm_g)
    vb_c = heads(bvc, B_KV_HEADS)
    qb = apply_rope(rms_norm(heads(bq, GROUP_HEADS), q_norm_g), cos_h, sin_h)
    kb = apply_rope(rms_norm(heads(bk, B_KV_HEADS), k_norm_g), cos_h, sin_h)
    y_b = blocked_gqa(qb, jnp.concatenate([kb, kb_c], axis=1),
                      jnp.concatenate([heads(bv, B_KV_HEADS), vb_c], axis=1))
    kc_c, vc_c = heads(ckc, GROUP_HEADS), heads(cvc, GROUP_HEADS)
    y_c = neighbourhood_attention(heads(cq, GROUP_HEADS), heads(ck, GROUP_HEADS), heads(cv, GROUP_HEADS),
                                  kc_c, vc_c, rel_bias, rows)
    kd_c, vd_c = sub_heads(dkc), heads(dvc, GROUP_HEADS)
    qd = apply_rope(sub_heads(dq), cos_s, sin_s)
    kd = apply_rope(sub_heads(dk), cos_s, sin_s)
    y_d = blocked_diff_attention(qd, jnp.concatenate([kd, kd_c], axis=1),
                                 jnp.concatenate([heads(dv, GROUP_HEADS), vd_c], axis=1), lam)
    y_d = rms_norm(y_d, subln_g) * (1.0 - lam_init)
    y = jnp.concatenate([t.reshape(B, S, GROUP_WIDTH) for t in (y_a, y_b, y_c, y_d)], axis=-1) @ w_out
    if not with_ctx_out:
        return y, None
    yc_a = context_sink_attention(heads(aqc, GROUP_HEADS), ka_c, va_c, sink)
    yc_b = blocked_gqa(rms_norm(heads(bqc, GROUP_HEADS), q_norm_g), kb_c, vb_c)
    yc_c = blocked_gqa(heads(cqc, GROUP_HEADS), kc_c, vc_c)
    yc_d = rms_norm(blocked_diff_attention(sub_heads(dqc), kd_c, vd_c, lam), subln_g) * (1.0 - lam_init)
    yc = jnp.concatenate([t.reshape(B, L, GROUP_WIDTH) for t in (yc_a, yc_b, yc_c, yc_d)], axis=-1) @ w_out
    return y, yc


def setup_inputs(seed: int = 0) -> dict:
    key = jax.random.key(seed)
    ks = jax.random.split(key, 24)
    f32 = jnp.float32
    nrm = jax.random.normal

    def dense(k, shape, fan_in, gain=1.0):
        return nrm(k, shape, f32) * (gain * fan_in ** -0.5)

    return {
        'x': nrm(ks[0], (BATCH, SEQ, D_MODEL), f32),
        'c': nrm(ks[1], (BATCH, D_MODEL), f32),
        'ctx': nrm(ks[2], (BATCH, CTX_LEN, D_MODEL), f32),
        'c_ctx': nrm(ks[3], (D_MODEL,), f32),
        'w_ada': dense(ks[4], (DEPTH, D_MODEL, N_MOD * D_MODEL), D_MODEL, 0.5),
        'b_ada': 0.02 * nrm(ks[5], (DEPTH, N_MOD * D_MODEL), f32),
        'w_ffn1_gate': dense(ks[6], (DEPTH, D_MODEL, D_FF), D_MODEL),
        'w_ffn1_up': dense(ks[7], (DEPTH, D_MODEL, D_FF), D_MODEL),
        'w_ffn1_down': dense(ks[8], (DEPTH, D_FF, D_MODEL), D_FF),
        'w_in': dense(ks[9], (DEPTH, D_MODEL, IN_WIDTH), D_MODEL),
        'w_out': dense(ks[10], (DEPTH, MIX_WIDTH, D_MODEL), MIX_WIDTH),
        'sink_logit': 0.5 * nrm(ks[11], (DEPTH, GROUP_HEADS), f32),
        'q_norm_g': 1.0 + 0.1 * nrm(ks[12], (DEPTH, HEAD_DIM), f32),
        'k_norm_g': 1.0 + 0.1 * nrm(ks[13], (DEPTH, HEAD_DIM), f32),
        'rel_pos_bias': 0.1 * nrm(ks[14], (DEPTH, GROUP_HEADS, 2 * NA_KR - 1, 2 * NA_KC - 1), f32),
        'lam_q1': 0.1 * nrm(ks[15], (DEPTH, D_SUB), f32),
        'lam_k1': 0.1 * nrm(ks[16], (DEPTH, D_SUB), f32),
        'lam_q2': 0.1 * nrm(ks[17], (DEPTH, D_SUB), f32),
        'lam_k2': 0.1 * nrm(ks[18], (DEPTH, D_SUB), f32),
        'subln_g': 1.0 + 0.1 * nrm(ks[19], (DEPTH, HEAD_DIM), f32),
        'w_ffn2_gate': dense(ks[20], (DEPTH, D_MODEL, D_FF), D_MODEL),
        'w_ffn2_up': dense(ks[21], (DEPTH, D_MODEL, D_FF), D_MODEL),
        'w_ffn2_down': dense(ks[22], (DEPTH, D_FF, D_MODEL), D_FF),
        'final_norm_g': 1.0 + 0.1 * nrm(ks[23], (D_MODEL,), f32),
    }


def reference(x, c, ctx, c_ctx, w_ada, b_ada, w_ffn1_gate, w_ffn1_up, w_ffn1_down, w_in, w_out,
              sink_logit, q_norm_g, k_norm_g, rel_pos_bias, lam_q1, lam_k1, lam_q2, lam_k2, subln_g,
              w_ffn2_gate, w_ffn2_up, w_ffn2_down, final_norm_g):
    B = x.shape[0]
    rows = x.shape[1] // GRID_W
    xc = ctx
    silu_c = jax.nn.silu(c)
    silu_cc = jax.nn.silu(c_ctx)
    for l in range(DEPTH):
        last = l == DEPTH - 1
        mod = (silu_c @ w_ada[l] + b_ada[l]).reshape(B, N_MOD, 1, D_MODEL)
        mod_c = (silu_cc @ w_ada[l] + b_ada[l]).reshape(N_MOD, 1, 1, D_MODEL)
        x = x + 0.5 * mod[:, 2] * swiglu(modulate(x, mod[:, 0], mod[:, 1]),
                                         w_ffn1_gate[l], w_ffn1_up[l], w_ffn1_down[l])
        xc = xc + 0.5 * mod_c[2] * swiglu(modulate(xc, mod_c[0], mod_c[1]),
                                          w_ffn1_gate[l], w_ffn1_up[l], w_ffn1_down[l])
        lam_init = 0.8 - 0.6 * math.exp(-0.3 * l)
        lam = (jnp.exp(jnp.sum(lam_q1[l].astype(jnp.float32) * lam_k1[l].astype(jnp.float32)))
               - jnp.exp(jnp.sum(lam_q2[l].astype(jnp.float32) * lam_k2[l].astype(jnp.float32)))
               + lam_init)
        h = modulate(x, mod[:, 3], mod[:, 4])
        hc = modulate(xc, mod_c[3], mod_c[4])
        y, yc = hybrid_mixer(h, hc, rows, w_in[l], w_out[l], sink_logit[l], q_norm_g[l], k_norm_g[l],
                             rel_pos_bias[l], lam, lam_init, subln_g[l], not last)
        x = x + mod[:, 5] * y
        if not last:
            xc = xc + mod_c[5] * yc
            xc = xc + 0.5 * mod_c[8] * swiglu(modulate(xc, mod_c[6], mod_c[7]),
                                              w_ffn2_gate[l], w_ffn2_up[l], w_ffn2_down[l])
        x = x + 0.5 * mod[:, 8] * swiglu(modulate(x, mod[:, 6], mod[:, 7]),
                                         w_ffn2_gate[l], w_ffn2_up[l], w_ffn2_down[l])
    return rms_norm(x, final_norm_g)
```

```python
import numpy as np
import concourse.bass as bass
import concourse.mybir as mybir
from concourse.bass_utils import run_bass_kernel_spmd
from contextlib import ExitStack

F32 = mybir.dt.float32
BF16 = mybir.dt.bfloat16
AF = mybir.ActivationFunctionType
ALU = mybir.AluOpType
AX = mybir.AxisListType

P = 128
D = 1024
KC = 8
S = 2048
L = 256
T = S + L
TGS = [(0, 512), (512, 512), (1024, 512), (1536, 512), (2048, 256)]
DFF = 2816
FC = 22
EPS = 1e-6
SAME_ENGINE_SYNC = True


class Res:
    __slots__ = ("name", "w", "r")

    def __init__(self, name=""):
        self.name = name
        self.w = None
        self.r = {}


class DSem:
    def __init__(self, sem, name):
        self.sem = sem
        self.count = 0
        self.name = name


class Prog:
    COMPUTE = ("pe", "act", "dve", "pool")
    ALL = ("pe", "act", "dve", "pool", "sp")

    def __init__(self, nc, es):
        self.nc = nc
        self.es = es
        self.streams = {e: [] for e in self.ALL}
        self.cnt = {e: 0 for e in self.COMPUTE}
        self.sem = {e: es.enter_context(nc.semaphore("c_" + e)) for e in self.COMPUTE}
        self.seen = {e: {} for e in self.ALL}
        self.dsems = []
        self.n_ins = 0

    def dsem(self, name):
        d = DSem(self.es.enter_context(self.nc.semaphore("d_" + name)), name)
        self.dsems.append(d)
        return d

    def _collect(self, eng, reads, writes):
        deps = {}

        def add(tok):
            if tok is None:
                return
            k, v = tok
            if deps.get(k, 0) < v:
                deps[k] = v

        for r in reads:
            add(r.w)
        for w in writes:
            add(w.w)
            for k, v in w.r.items():
                add((k, v))
        waits = []
        for k, v in deps.items():
            if k == eng and (eng == "pe" or not SAME_ENGINE_SYNC):
                continue
            if self.seen[eng].get(k, 0) >= v:
                continue
            self.seen[eng][k] = v
            waits.append((k, v))
        return waits

    def _update(self, tok, reads, writes):
        k, v = tok
        for r in reads:
            if r.r.get(k, 0) < v:
                r.r[k] = v
        for w in writes:
            w.w = tok
            w.r = {}

    def op(self, eng, fn, reads=(), writes=()):
        waits = self._collect(eng, reads, writes)
        self.cnt[eng] += 1
        tok = (eng, self.cnt[eng])
        self.streams[eng].append((waits, fn, (self.sem[eng], 1)))
        self._update(tok, reads, writes)
        return tok

    def dma(self, q, fn, dsem, reads=(), writes=()):
        waits = self._collect(q, reads, writes)
        dsem.count += 16
        tok = (dsem, dsem.count)
        self.streams[q].append((waits, fn, (dsem.sem, 16)))
        self._update(tok, reads, writes)
        return tok

    def wait_all(self, eng):
        waits = []
        for e in self.COMPUTE:
            if e != eng and self.cnt[e] > self.seen[eng].get(e, 0):
                waits.append((e, self.cnt[e]))
                self.seen[eng][e] = self.cnt[e]
        for d in self.dsems:
            if d.count > self.seen[eng].get(d, 0):
                waits.append((d, d.count))
                self.seen[eng][d] = d.count
        self.streams[eng].append((waits, None, None))

    def barrier(self):
        for e in self.ALL:
            self.wait_all(e)

    def emit(self):
        nc = self.nc
        handles = {"pe": "tensor", "act": "scalar", "dve": "vector", "pool": "gpsimd", "sp": "sync"}
        with nc.Block() as block:
            for e in self.ALL:
                recs = self.streams[e]

                def body(eng, recs=recs):
                    for waits, fn, inc in recs:
                        for k, v in waits:
                            s = self.sem[k] if isinstance(k, str) else k.sem
                            eng.wait_ge(s, v)
                        if fn is not None:
                            ins = fn(eng)
                            ins.then_inc(inc[0], inc[1])

                getattr(block, handles[e])(body)


class Builder:
    def __init__(self, stage, debug=False, skipmix=""):
        self.stage = stage
        self.debug = debug
        self.skipmix = skipmix
        self.nc = bass.Bass("TRN2", target_bir_lowering=False)
        self.es = ExitStack()

    def dram_in(self, name, shape, dt=F32):
        return self.nc.dram_tensor(name, list(shape), dt, kind="ExternalInput").ap()

    def sb(self, name, shape, dt):
        return self.es.enter_context(self.nc.sbuf_tensor("sb_" + name, list(shape), dt))

    def build(self):
        nc = self.nc
        with self.es:
            self.p = Prog(nc, self.es)
            self._build()
            self.p.emit()
        return nc

    def _build(self):
        nc, p = self.nc, self.p
        st = self.stage
        self.x_d = self.dram_in("x", [S, D])
        self.ctx_d = self.dram_in("ctx", [L, D])
        self.fng_d = self.dram_in("final_norm_g", [1, D])
        self.identf_d = self.dram_in("identf", [P, P])
        self.cc_d = self.dram_in("cc", [P, 16])
        self.wada_d = self.dram_in("w_ada", [2, D, 9 * D])
        self.bada_d = self.dram_in("b_adaT", [2, P, 72])
        self.wg_d = [self.dram_in("w_ffn1_gate", [2, D, DFF]), self.dram_in("w_ffn2_gate", [2, D, DFF])]
        self.wu_d = [self.dram_in("w_ffn1_up", [2, D, DFF]), self.dram_in("w_ffn2_up", [2, D, DFF])]
        self.wd_d = [self.dram_in("w_ffn1_down", [2, DFF, D]), self.dram_in("w_ffn2_down", [2, DFF, D])]
        self.win_d = self.dram_in("w_in", [2, D, 2560])
        self.wout_d = self.dram_in("w_out", [2, D, D])
        self.cs_d = {64: self.dram_in("cs64", [P, 2, T]), 32: self.dram_in("cs32", [P, 2, T])}
        self.strip_d = self.dram_in("strip", [P, 1152])
        self.sink_d = self.dram_in("sink_logit", [2, 4])
        self.qkg_d = self.dram_in("qkg", [2, P, 4])
        self.lamv_d = self.dram_in("lamv", [2, 4, 32])
        self.subg_d = self.dram_in("subg", [64, 2])
        self.relT_d = self.dram_in("relT", [2, 2, P, 960])
        self.cmask_d = self.dram_in("cmask", [P, 960])
        self.pmask_d = self.dram_in("pmask", [P, 2])
        self.out_d = nc.dram_tensor("out", [S, D], F32, kind="ExternalOutput").ap()
        self.dbg_d = nc.dram_tensor("dbg", [P, 6, 2048], F32, kind="ExternalOutput").ap() if self.debug else None

        self.X = self.sb("X", [P, KC, T], F32)
        self.XR = [[Res(f"X{c}_{g}") for g in range(5)] for c in range(KC)]
        self.H = self.sb("H", [P, KC, T], BF16)
        self.HR = [[Res(f"H{c}_{g}") for g in range(5)] for c in range(KC)]
        self.BIG = self.sb("BIG", [P, 5 * T + 2340], BF16)
        self.STG = [self.sb(f"stg{i}", [P, 2048], F32) for i in range(2)]
        self.STGR = [Res(f"stg{i}") for i in range(2)]
        self.stg_d = [p.dsem(f"stg{i}") for i in range(2)]
        self.stg_rr = 0
        self.stgb_d = [p.dsem(f"stgb{i}") for i in range(2)]
        NWA, NWB = 4, 4
        self.WA = [self.sb(f"wa{i}", [P, 8, 256], BF16) for i in range(NWA)]
        self.WAR = [Res(f"wa{i}") for i in range(NWA)]
        self.wa_d = [p.dsem(f"wa{i}") for i in range(NWA)]
        self.wa_rr = 0
        self.WB = [self.sb(f"wb{i}", [P, 1024], BF16) for i in range(NWB)]
        self.WBR = [Res(f"wb{i}") for i in range(NWB)]
        self.wb_d = [p.dsem(f"wb{i}") for i in range(NWB)]
        self.wb_rr = 0
        NTMP = 6
        self.TMP = [self.sb(f"tmp{i}", [P, 512], F32) for i in range(NTMP)]
        self.TMPR = [Res(f"tmp{i}") for i in range(NTMP)]
        self.tmp_rr = [0, 0, 0]
        self.tmp_any_rr = 0
        self.identf = self.sb("identf", [P, P], F32)
        self.onesf = self.sb("onesf", [P, P], F32)
        self.small = self.sb("small", [P, 64], F32)
        self.SMR = [Res(f"sm{i}") for i in range(64)]
        self.constR = Res("const")
        self.epsc = self.sb("epsc", [P, 1], F32)
        self.cc = self.sb("cc", [P, 16], F32)
        self.sT = self.sb("sT", [P, 16], BF16)
        self.sTR = Res("sT")
        self.bada = self.sb("bada", [P, 2, 72], F32)
        self.MOD = self.sb("MOD", [P, 4 * 72], F32)
        self.MODR = [Res(f"mod{i}") for i in range(4)]
        self.QT = self.BIG[:, 0:T]
        self.KT = self.BIG[:, T:3 * T].rearrange("q (c t) -> q c t", c=2)
        self.V = self.BIG[:, 3 * T:3 * T + 2340].rearrange("q (b c) -> q b c", c=130)
        self.Y = self.BIG[:, 3 * T + 2340:5 * T + 2340].rearrange("q (h t) -> q h t", h=2)
        self.PT = [self.sb(f"pt{i}", [P, 512], BF16) for i in range(5)]
        self.PTR = [Res(f"pt{i}") for i in range(5)]
        self.pt_rr = 0
        self.CS = [self.sb(f"cs{i}", [P, 2, 512], F32) for i in range(2)]
        self.CSR = [Res(f"cs{i}") for i in range(2)]
        self.cs_ds = [p.dsem(f"cs{i}") for i in range(2)]
        self.cs_rr = 0
        self.Tc = self.CS[0][:, :, :].rearrange("q a n -> q (a n)").bitcast(BF16)[:, 0:960]
        self.identb = self.sb("identb", [P, P], BF16)
        self.onesb = self.sb("onesb", [P, 64], BF16)
        self.blockones = self.sb("blockones", [P, P], F32)
        self.strip = self.sb("strip", [P, 1152], BF16)
        yoff = 3 * T + 2340
        self.SK = self.sb("SK", [P, 8], F32)
        self.qkg = self.sb("qkg", [P, 2, 4], F32)
        self.lamt = self.sb("lamt", [1, 2, 128], F32)
        self.lams = self.sb("lams", [P, 16], F32)
        self.LAMR = Res("lam")
        self.subg = self.sb("subg", [64, 2], F32)
        self.pmask = self.sb("pmask", [P, 2], F32)
        self.REC = [self.BIG[:, yoff:yoff + 1024].bitcast(F32)]
        self.RB = [self.BIG[64:65, yoff + 1024 + j * 512:yoff + 1024 + (j + 1) * 512] for j in range(7)]
        self.RBR = [Res(f"rb{j}") for j in range(7)]
        self.rb_rr = 0
        self.rb_of = {}
        self.RECR = [Res(f"rec{i}") for i in range(1)]
        self.rec_rr = 0
        self.PS = [self.es.enter_context(nc.psum_tensor(f"ps{i}", [P, 512], F32)) for i in range(8)]
        self.pool_rr = {"st": 0, "acc": 0}
        self.PSR = [Res(f"ps{i}") for i in range(8)]
        self.ps_rr = 0
        self.bg = []
        self.evac_rr = 0
        self.cd = p.dsem("const")
        self.out_ds = p.dsem("out")

        p.op("pool", lambda e: e.memset(self.epsc[:], EPS), writes=[self.constR])
        p.op("pool", lambda e: e.memset(self.onesf[:], 1.0), writes=[self.constR])
        self.cdma(self.identf[:], self.identf_d)
        self.cdma(self.cc[:], self.cc_d)
        self.cdma(self.bada[:], self.bada_d.rearrange("l p j -> p l j"))
        self.cdma(self.qkg[:], self.qkg_d.rearrange("l p j -> p l j"))
        self.cdma(self.pmask[:], self.pmask_d)
        self.cdma(self.SK[64:65, 0:8], self.sink_d.rearrange("(o l) h -> o (l h)", o=1))
        self.cdma(self.lamt[:], self.lamv_d.rearrange("(o l) a d -> o l (a d)", o=1))
        self.cdma(self.subg[:], self.subg_d)
        p.dma("pool", lambda e: e.dma_start(out=self.strip[:], in_=self.strip_d), p.dsem("strip"), writes=[self.constR])
        p.op("pool", lambda e: e.memset(self.onesb[:], 1.0), writes=[self.constR])
        p.op("pool", lambda e: e.memset(self.blockones[:], 0.0), writes=[self.constR])
        p.op("pool", lambda e: e.memset(self.blockones[0:64, 0:64], 1.0), writes=[self.constR])
        p.op("pool", lambda e: e.memset(self.blockones[64:128, 64:128], 1.0), writes=[self.constR])
        p.op("act", lambda e: e.copy(out=self.identb[:], in_=self.identf[:]), reads=[self.constR], writes=[self.constR])
        p.op("act", lambda e: e.activation(out=self.SK[64:65, 0:8], in_=self.SK[64:65, 0:8], func=AF.Exp),
             reads=[self.constR], writes=[self.constR])
        for l in range(2):
            li = 1.0 - (0.8 - 0.6 * float(np.exp(-0.3 * l)))
            p.op("dve", lambda e, l=l, li=li: e.tensor_scalar_mul(out=self.subg[:, l:l + 1], in0=self.subg[:, l:l + 1],
                                                                   scalar1=li), reads=[self.constR], writes=[self.constR])
        p.op("act", lambda e: e.activation(out=self.sT[:], in_=self.cc[:], func=AF.Silu),
             reads=[self.constR], writes=[self.sTR])

        self.load_x()
        k = 0
        for l in range(2):
            if st >= k + 1:
                if l == 0:
                    for f_ in self.adaln_steps(0, list(range(0, 6)), [0, 1, 2]):
                        f_()
                    self.bg = self.adaln_steps(0, list(range(6, 18)), [3, 4, 5, 6, 7, 8])
                self.norm_mod(l, 0, 1)
                self.ffn(l, 0, 2, ctx=True)
            if st >= k + 2:
                p.barrier()
                self.norm_mod(l, 3, 4)
                self.lam_setup(l)
                self.mixers(l, min(4, st - k - 1))
                p.barrier()
            if st >= k + 6:
                self.norm_mod(l, 6, 7, tgs=range(5) if l == 0 else range(4))
                if l == 0 and st >= 7:
                    self.bg = self.adaln_steps(1, list(range(18)), list(range(9)))
                self.ffn(l, 1, 8, ctx=(l == 0))
            k += 6
        self.final_norm()
        p.barrier()

    def dump(self, slot, ap, reads, np_=P, w=2048):
        if not self.debug:
            return
        p = self.p
        p.barrier()
        p.op("act", lambda e: e.copy(out=self.STG[0][0:np_, 0:w], in_=ap), reads=reads, writes=[self.STGR[0]])
        p.dma("sp", lambda e: e.dma_start(out=self.dbg_d[0:np_, slot, 0:w], in_=self.STG[0][0:np_, 0:w]), self.out_ds,
              reads=[self.STGR[0]])
        p.barrier()

    def cdma(self, dst, src):
        self.p.dma("sp", lambda e: e.dma_start(out=dst, in_=src), self.cd, writes=[self.constR])

    def bank(self):
        b = self.ps_rr
        self.ps_rr = (self.ps_rr + 1) % 7
        return b

    def bg_step(self):
        if self.bg:
            self.bg.pop(0)()

    def bg_flush(self):
        while self.bg:
            self.bg.pop(0)()

    def pbank(self, pool):
        lst = {"st": (0, 1, 2, 3), "acc": (4, 5, 6, 7)}[pool]
        b = lst[self.pool_rr[pool]]
        self.pool_rr[pool] = (self.pool_rr[pool] + 1) % len(lst)
        return b

    def tmp(self, role):
        i = role * 2 + self.tmp_rr[role]
        self.tmp_rr[role] ^= 1
        return i

    def modap(self, l, stream, r, c):
        col = (l * 2 + stream) * 72 + r * 8 + c
        return self.MOD[:, col:col + 1]

    def mm_group(self, out, pairs, reads, writes, **kw):
        n = len(pairs)

        def fn(e):
            ins = None
            for i, (lhsT, rhs) in enumerate(pairs):
                ins = e.matmul(out, lhsT, rhs, start=(i == 0), stop=(i == n - 1), **kw)
            return ins

        return self.p.op("pe", fn, reads=reads, writes=writes)

    def load_x(self):
        p = self.p
        for g, (t0, n) in enumerate(TGS):
            ntile = n // P
            for s in range(ntile // 2):
                if g < 4:
                    src = self.x_d[t0 + s * 256: t0 + s * 256 + 256, :]
                else:
                    src = self.ctx_d[s * 256: s * 256 + 256, :]
                src = src.rearrange("(j p) d -> p j d", p=P)
                dst = self.STG[s][:, :].rearrange("p (j d) -> p j d", j=2)
                p.dma("sp", lambda e, dst=dst, src=src: e.dma_start(out=dst, in_=src), self.stg_d[s],
                      writes=[self.STGR[s]])
            for c in range(KC):
                b = self.bank()

                def tr(e, b=b, c=c, ntile=ntile):
                    ins = None
                    for j in range(ntile):
                        src = self.STG[j // 2][:, (j % 2) * 1024 + c * P:(j % 2) * 1024 + (c + 1) * P]
                        ins = e.transpose(out=self.PS[b][:, j * P:(j + 1) * P], in_=src, identity=self.identf[:])
                    return ins

                p.op("pe", tr, reads=[self.STGR[s] for s in range(ntile // 2)] + [self.constR], writes=[self.PSR[b]])
                self.evac_copy(self.X[:, c, t0:t0 + n], self.PS[b][:, 0:n], [self.PSR[b]], [self.XR[c][g]])

    def evac_copy(self, dst, src, reads, writes):
        p = self.p
        self.evac_rr ^= 1
        if self.evac_rr:
            p.op("act", lambda e: e.copy(out=dst, in_=src), reads=reads, writes=writes)
        else:
            p.op("dve", lambda e: e.tensor_copy(out=dst, in_=src), reads=reads, writes=writes)

    def final_norm(self):
        p = self.p
        self.gbc = self.WA[0][:, :, :].rearrange("q k n -> q (k n)").bitcast(F32)
        p.dma("sp", lambda e: e.dma_start(out=self.gbc, in_=self.fng_d.partition_broadcast(P)), self.cd,
              writes=[self.WAR[0]])
        for tt in range(S // P):
            g = tt // 4
            s = tt % 2
            stg = self.STG[s]
            sr = self.STGR[s]
            for h in range(2):
                b = self.bank()

                def tr(e, b=b, h=h, tt=tt):
                    ins = None
                    for j in range(4):
                        c = h * 4 + j
                        ins = e.transpose(out=self.PS[b][:, j * P:(j + 1) * P],
                                          in_=self.X[:, c, tt * P:(tt + 1) * P], identity=self.identf[:])
                    return ins

                p.op("pe", tr, reads=[self.XR[h * 4 + j][g] for j in range(4)] + [self.constR],
                     writes=[self.PSR[b]])
                dst = stg[:, h * 512:(h + 1) * 512]
                src = self.PS[b][:, :]
                if h == 0:
                    p.op("act", lambda e, dst=dst, src=src: e.copy(out=dst, in_=src), reads=[self.PSR[b]], writes=[sr])
                else:
                    p.op("dve", lambda e, dst=dst, src=src: e.tensor_copy(out=dst, in_=src), reads=[self.PSR[b]],
                         writes=[sr])
            ss = self.small[:, 2 * s:2 * s + 1]
            rs = self.small[:, 2 * s + 1:2 * s + 2]
            smr = self.SMR[s]
            p.op("act", lambda e, stg=stg, ss=ss: e.activation(out=stg[:, 1024:2048], in_=stg[:, 0:1024],
                                                              func=AF.Square, accum_out=ss),
                 reads=[sr], writes=[sr, smr])
            p.op("act", lambda e, ss=ss, rs=rs: e.activation(out=rs, in_=ss, func=AF.Sqrt, scale=1.0 / D,
                                                             bias=self.epsc[:, 0:1]),
                 reads=[smr, self.constR], writes=[smr])
            p.op("dve", lambda e, rs=rs: e.reciprocal(out=rs, in_=rs), reads=[smr], writes=[smr])
            p.op("dve", lambda e, stg=stg, rs=rs: e.scalar_tensor_tensor(out=stg[:, 1024:2048], in0=stg[:, 0:1024],
                                                                        scalar=rs, in1=self.gbc,
                                                                        op0=ALU.mult, op1=ALU.mult),
                 reads=[sr, smr, self.WAR[0]], writes=[sr])
            p.dma("sp", lambda e, stg=stg, tt=tt: e.dma_start(out=self.out_d[tt * P:(tt + 1) * P, :],
                                                             in_=stg[:, 1024:2048]),
                  self.out_ds, reads=[sr])

    def adaln_steps(self, l, pieces, rows):
        p = self.p
        b = 7
        psr = self.PSR[b]
        slots = {}

        def issue(j4):
            s_ = self.stg_rr
            self.stg_rr ^= 1
            slots[j4] = s_
            src = self.wada_d[l, :, j4 * 512:(j4 + 1) * 512].rearrange("(kc q) n -> q kc n", q=P)
            stgb = self.STG[s_][:, :].bitcast(BF16)
            dst = stgb.rearrange("q (kc n) -> q kc n", kc=8)
            p.dma("pool", lambda e: e.dma_start(out=dst, in_=src), self.stgb_d[s_], writes=[self.STGR[s_]])

        def step(idx):
            j4 = pieces[idx]
            if idx == 0:
                issue(j4)
            if idx + 1 < len(pieces):
                issue(pieces[idx + 1])
            s_ = slots[j4]
            stgb = self.STG[s_][:, :].bitcast(BF16)
            for q4 in range(4):
                j = j4 * 4 + q4
                pairs = [(stgb[:, kc * 512 + q4 * 128: kc * 512 + q4 * 128 + 128],
                          self.sT[:, 2 * kc:2 * kc + 2]) for kc in range(8)]
                self.mm_group(self.PS[b][:, 2 * j:2 * j + 2], pairs, reads=[self.STGR[s_], self.sTR], writes=[psr])

        def final():
            c0, c1 = rows[0] * 8, (rows[-1] + 1) * 8
            pv = self.PS[b][:, 0:144].rearrange("q (j t) -> q j t", t=2)
            for stream in range(2):
                base = (l * 2 + stream) * 72
                mr = self.MODR[l * 2 + stream]
                dst = self.MOD[:, base + c0:base + c1]
                p.op("dve", lambda e, dst=dst, stream=stream: e.tensor_tensor(out=dst, in0=pv[:, c0:c1, stream],
                                                                              in1=self.bada[:, l, c0:c1], op=ALU.add),
                     reads=[psr, self.constR], writes=[mr])
                for r in (1, 4, 7):
                    if r in rows:
                        d2 = self.MOD[:, base + r * 8:base + r * 8 + 8]
                        p.op("dve", lambda e, d2=d2: e.tensor_scalar_add(out=d2, in0=d2, scalar1=1.0), reads=[mr],
                             writes=[mr])
                for r in (2, 8):
                    if r in rows:
                        d2 = self.MOD[:, base + r * 8:base + r * 8 + 8]
                        p.op("dve", lambda e, d2=d2: e.tensor_scalar_mul(out=d2, in0=d2, scalar1=0.5), reads=[mr],
                             writes=[mr])

        return [(lambda idx=idx: step(idx)) for idx in range(len(pieces))] + [final]

    def norm_mod(self, l, r_shift, r_scale, tgs=range(5)):
        p = self.p
        for g in tgs:
            t0, n = TGS[g]
            stream = 0 if g < 4 else 1
            mr = self.MODR[l * 2 + stream]
            b = self.bank()
            sqs = []
            for c in range(KC):
                ti = self.tmp(0)
                xs = self.X[:, c, t0:t0 + n]
                sq = self.TMP[ti][:, 0:n]
                p.op("pool", lambda e, sq=sq, xs=xs: e.tensor_tensor(out=sq, in0=xs, in1=xs, op=ALU.mult),
                     reads=[self.XR[c][g]], writes=[self.TMPR[ti]])
                p.op("pe", lambda e, sq=sq, c=c, b=b, n=n: e.matmul(self.PS[b][:, 0:n], self.onesf[:], sq,
                                                                    start=(c == 0), stop=(c == KC - 1)),
                     reads=[self.TMPR[ti], self.constR], writes=[self.PSR[b]])
            ri = self.tmp(1)
            rs = self.TMP[ri][:, 0:n]
            p.op("act", lambda e, rs=rs, b=b, n=n: e.activation(out=rs, in_=self.PS[b][:, 0:n], func=AF.Sqrt,
                                                                scale=1.0 / D, bias=self.epsc[:, 0:1]),
                 reads=[self.PSR[b], self.constR], writes=[self.TMPR[ri]])
            p.op("dve", lambda e, rs=rs: e.reciprocal(out=rs, in_=rs), reads=[self.TMPR[ri]], writes=[self.TMPR[ri]])
            for c in range(KC):
                ti = self.tmp(2)
                tt = self.TMP[ti][:, 0:n]
                xs = self.X[:, c, t0:t0 + n]
                p.op("dve", lambda e, tt=tt, xs=xs, rs=rs: e.tensor_tensor(out=tt, in0=xs, in1=rs, op=ALU.mult),
                     reads=[self.XR[c][g], self.TMPR[ri]], writes=[self.TMPR[ti]])
                hd = self.H[:, c, t0:t0 + n]
                sc = self.modap(l, stream, r_scale, c)
                sh = self.modap(l, stream, r_shift, c)
                p.op("act", lambda e, hd=hd, tt=tt, sc=sc, sh=sh: e.activation(out=hd, in_=tt, func=AF.Identity,
                                                                              scale=sc, bias=sh),
                     reads=[self.TMPR[ti], mr], writes=[self.HR[c][g]])

    def load_wa(self, src):
        i = self.wa_rr
        self.wa_rr = (self.wa_rr + 1) % len(self.WA)
        srcv = src.rearrange("(kc q) n -> q kc n", q=P)
        dst = self.WA[i][:, :, :]
        self.p.dma("pool", lambda e: e.dma_start(out=dst, in_=srcv), self.wa_d[i], writes=[self.WAR[i]])
        return i

    def load_wb(self, src):
        i = self.wb_rr
        self.wb_rr = (self.wb_rr + 1) % len(self.WB)
        dst = self.WB[i][:, :]
        self.p.dma("pool", lambda e: e.dma_start(out=dst, in_=src), self.wb_d[i], writes=[self.WBR[i]])
        return i

    def ffn(self, l, which, r_gate, ctx=True):
        p = self.p
        wg, wu, wd = self.wg_d[which][l], self.wu_d[which][l], self.wd_d[which][l]
        tgs = list(range(5)) if ctx else list(range(4))
        U = self.BIG[:, 0:4 * T].rearrange("q (f t) -> q f t", f=4)
        UR = [[Res(f"U{f}_{g}") for g in range(5)] for f in range(4)]
        npieces = FC // 2
        slabs = [list(range(i, min(i + 2, npieces))) for i in range(0, npieces, 2)]

        def issue_piece(j):
            return (self.load_wa(wg[:, j * 256:(j + 1) * 256]), self.load_wa(wu[:, j * 256:(j + 1) * 256]))

        def issue_down(slab):
            return [self.load_wb(wd[f * P:(f + 1) * P, :]) for j in slab for f in (2 * j, 2 * j + 1)]

        pend = {0: issue_piece(0)}
        pend_d = {0: issue_down(slabs[0])}
        for si, slab in enumerate(slabs):
            for j in slab:
                if j + 1 < npieces:
                    pend[j + 1] = issue_piece(j + 1)
                ig, iu = pend.pop(j)
                for half in range(2):
                    fl = (j - slab[0]) * 2 + half
                    self.bg_step()
                    for g in tgs:
                        t0, n = TGS[g]
                        hreads = [self.HR[kc][g] for kc in range(KC)]
                        bg = self.bank()
                        self.mm_group(self.PS[bg][:, 0:n],
                                      [(self.WA[ig][:, kc, half * P:(half + 1) * P], self.H[:, kc, t0:t0 + n])
                                       for kc in range(KC)], reads=hreads + [self.WAR[ig]], writes=[self.PSR[bg]])
                        bu = self.bank()
                        self.mm_group(self.PS[bu][:, 0:n],
                                      [(self.WA[iu][:, kc, half * P:(half + 1) * P], self.H[:, kc, t0:t0 + n])
                                       for kc in range(KC)], reads=hreads + [self.WAR[iu]], writes=[self.PSR[bu]])
                        ti = self.tmp(0)
                        sg = self.TMP[ti][:, 0:n]
                        p.op("act", lambda e, sg=sg, bg=bg, n=n: e.activation(out=sg, in_=self.PS[bg][:, 0:n],
                                                                              func=AF.Silu),
                             reads=[self.PSR[bg]], writes=[self.TMPR[ti]])
                        ud = U[:, fl, t0:t0 + n]
                        p.op("dve", lambda e, ud=ud, sg=sg, bu=bu, n=n: e.tensor_tensor(out=ud, in0=sg,
                                                                                        in1=self.PS[bu][:, 0:n],
                                                                                        op=ALU.mult),
                             reads=[self.TMPR[ti], self.PSR[bu]], writes=[UR[fl][g]])
            wbs = pend_d.pop(si)
            nfl = len(wbs)
            for g in tgs:
                t0, n = TGS[g]
                stream = 0 if g < 4 else 1
                mr = self.MODR[l * 2 + stream]
                for m in range(KC):
                    b = self.bank()
                    self.mm_group(self.PS[b][:, 0:n],
                                  [(self.WB[wbs[fl]][:, m * P:(m + 1) * P], U[:, fl, t0:t0 + n]) for fl in range(nfl)],
                                  reads=[UR[fl][g] for fl in range(nfl)] + [self.WBR[w] for w in wbs],
                                  writes=[self.PSR[b]])
                    xs = self.X[:, m, t0:t0 + n]
                    ga = self.modap(l, stream, r_gate, m)
                    p.op("dve", lambda e, xs=xs, b=b, n=n, ga=ga: e.scalar_tensor_tensor(
                        out=xs, in0=self.PS[b][:, 0:n], scalar=ga, in1=xs, op0=ALU.mult, op1=ALU.add),
                         reads=[self.PSR[b], mr, self.XR[m][g]], writes=[self.XR[m][g]])
            if si + 1 < len(slabs):
                pend_d[si + 1] = issue_down(slabs[si + 1])
        self.bg_flush()

    def lam_setup(self, l):
        p = self.p
        r = self.LAMR
        lt = self.lamt
        sm = self.lams
        for t in range(2):
            a = lt[0:1, l, (2 * t) * 32:(2 * t) * 32 + 32]
            bb = lt[0:1, l, (2 * t + 1) * 32:(2 * t + 1) * 32 + 32]
            p.op("dve", lambda e, a=a, bb=bb: e.tensor_tensor(out=a, in0=a, in1=bb, op=ALU.mult),
                 reads=[self.constR, r], writes=[r])
            p.op("dve", lambda e, a=a, t=t: e.reduce_sum(out=sm[0:1, 8 + t:9 + t], in_=a, axis=AX.X),
                 reads=[r], writes=[r])
        p.op("act", lambda e: e.activation(out=sm[0:1, 8:10], in_=sm[0:1, 8:10], func=AF.Exp), reads=[r], writes=[r])
        lam_init = 0.8 - 0.6 * float(np.exp(-0.3 * l))
        p.op("dve", lambda e: e.tensor_tensor(out=sm[0:1, 10:11], in0=sm[0:1, 9:10], in1=sm[0:1, 8:9],
                                              op=ALU.subtract), reads=[r], writes=[r])
        p.op("dve", lambda e: e.tensor_scalar_add(out=sm[0:1, 10:11], in0=sm[0:1, 10:11], scalar1=-lam_init),
             reads=[r], writes=[r])
        b = self.pbank("st")
        p.op("pe", lambda e: e.matmul(self.PS[b][:, 0:1], self.onesf[0:1, :], sm[0:1, 10:11], start=True, stop=True),
             reads=[r, self.constR], writes=[self.PSR[b]])
        p.op("dve", lambda e: e.tensor_copy(out=sm[:, l:l + 1], in_=self.PS[b][:, 0:1]), reads=[self.PSR[b]],
             writes=[r])

    def load_cs(self, kind, g):
        t0, n = TGS[g]
        i = self.cs_rr
        self.cs_rr ^= 1
        src = self.cs_d[kind][:, :, t0:t0 + n]
        dst = self.CS[i][:, :, 0:n]
        self.p.dma("sp", lambda e: e.dma_start(out=dst, in_=src), self.cs_ds[i], writes=[self.CSR[i]])
        return i

    def perm_weights(self, wi, qd):
        wp = self.wa_rr
        self.wa_rr = (self.wa_rr + 1) % len(self.WA)
        sv = self.WA[wi][:, :, :].rearrange("q k (b t d) -> q k b t d", t=2, d=qd)
        dv = self.WA[wp][:, :, :].rearrange("q k (b t d) -> q k b t d", t=2, d=qd)
        self.p.op("pool", lambda e: e.tensor_copy(out=dv[:, :, :, 0, :], in_=sv[:, :, :, 1, :]),
                  reads=[self.WAR[wi]], writes=[self.WAR[wp]])
        self.p.op("pool", lambda e: e.tensor_copy(out=dv[:, :, :, 1, :], in_=sv[:, :, :, 0, :]),
                  reads=[self.WAR[wi]], writes=[self.WAR[wp]])
        return wp

    def proj_fm(self, l, wi, wpi, colsel, dsts, kind, tgs, norm_col=None, pad=False):
        p = self.p
        for g in tgs:
            t0, n = TGS[g]
            hreads = [self.HR[kc][g] for kc in range(KC)]
            bp = self.pbank("st")
            self.mm_parts(bp, n, colsel, wi, t0, hreads)
            if wpi is None:
                dst = dsts[0][0](g)
                self.evac_copy(dst, self.PS[bp][:, 0:n], [self.PSR[bp]], dsts[0][1](g))
                continue
            ci = self.load_cs(kind, g)
            cos = self.CS[ci][:, 0, 0:n]
            ssin = self.CS[ci][:, 1, 0:n]
            br = self.pbank("st")
            self.mm_parts(br, n, colsel, wpi, t0, hreads)
            i1 = self.tmp(0)
            i2 = self.tmp(1)
            t1 = self.TMP[i1][:, 0:n]
            t2 = self.TMP[i2][:, 0:n]
            pp = self.PS[bp][:, 0:n]
            pr = self.PS[br][:, 0:n]
            if norm_col is None:
                p.op("dve", lambda e, t1=t1, pp=pp, cos=cos: e.tensor_tensor(out=t1, in0=pp, in1=cos, op=ALU.mult),
                     reads=[self.PSR[bp], self.CSR[ci]], writes=[self.TMPR[i1]])
                p.op("dve", lambda e, t2=t2, pr=pr, ssin=ssin: e.tensor_tensor(out=t2, in0=pr, in1=ssin, op=ALU.mult),
                     reads=[self.PSR[br], self.CSR[ci]], writes=[self.TMPR[i2]])
                if not pad:
                    dst = dsts[0][0](g)
                    p.op("pool", lambda e, dst=dst, t1=t1, t2=t2: e.tensor_tensor(out=dst, in0=t1, in1=t2, op=ALU.add),
                         reads=[self.TMPR[i1], self.TMPR[i2]], writes=dsts[0][1](g))
                else:
                    p.op("pool", lambda e, t1=t1, t2=t2: e.tensor_tensor(out=t1, in0=t1, in1=t2, op=ALU.add),
                         reads=[self.TMPR[i1], self.TMPR[i2]], writes=[self.TMPR[i1]])
                    for k2 in range(2):
                        dst = dsts[k2][0](g)
                        m = self.pmask[:, k2:k2 + 1]
                        eng = "dve" if k2 == 0 else "pool"
                        p.op(eng, lambda e, dst=dst, t1=t1, m=m: e.tensor_scalar_mul(out=dst, in0=t1, scalar1=m),
                             reads=[self.TMPR[i1], self.constR], writes=dsts[k2][1](g))
            else:
                i3 = self.tmp(2)
                i4 = self.tmp(2)
                sq = self.TMP[i3][:, 0:n]
                rs = self.TMP[i4][:, 0:n]
                p.op("act", lambda e, sq=sq, pp=pp: e.activation(out=sq, in_=pp, func=AF.Square),
                     reads=[self.PSR[bp]], writes=[self.TMPR[i3]])
                bs = self.pbank("st")
                p.op("pe", lambda e, bs=bs, sq=sq, n=n: e.matmul(self.PS[bs][:, 0:n], self.blockones[:], sq,
                                                                 start=True, stop=True),
                     reads=[self.TMPR[i3], self.constR], writes=[self.PSR[bs]])
                p.op("act", lambda e, rs=rs, bs=bs, n=n: e.activation(out=rs, in_=self.PS[bs][:, 0:n], func=AF.Sqrt,
                                                                      scale=1.0 / 64, bias=self.epsc[:, 0:1]),
                     reads=[self.PSR[bs], self.constR], writes=[self.TMPR[i4]])
                p.op("dve", lambda e, rs=rs: e.reciprocal(out=rs, in_=rs), reads=[self.TMPR[i4]],
                     writes=[self.TMPR[i4]])
                g1 = self.qkg[:, l, norm_col:norm_col + 1]
                g2 = self.qkg[:, l, norm_col + 1:norm_col + 2]
                p.op("dve", lambda e, t1=t1, pp=pp, cos=cos, g1=g1: e.scalar_tensor_tensor(
                    out=t1, in0=pp, scalar=g1, in1=cos, op0=ALU.mult, op1=ALU.mult),
                     reads=[self.PSR[bp], self.CSR[ci], self.constR], writes=[self.TMPR[i1]])
                p.op("dve", lambda e, t2=t2, pr=pr, ssin=ssin, g2=g2: e.scalar_tensor_tensor(
                    out=t2, in0=pr, scalar=g2, in1=ssin, op0=ALU.mult, op1=ALU.mult),
                     reads=[self.PSR[br], self.CSR[ci], self.constR], writes=[self.TMPR[i2]])
                p.op("pool", lambda e, t1=t1, t2=t2: e.tensor_tensor(out=t1, in0=t1, in1=t2, op=ALU.add),
                     reads=[self.TMPR[i1], self.TMPR[i2]], writes=[self.TMPR[i1]])
                dst = dsts[0][0](g)
                p.op("dve", lambda e, dst=dst, t1=t1, rs=rs: e.tensor_tensor(out=dst, in0=t1, in1=rs, op=ALU.mult),
                     reads=[self.TMPR[i1], self.TMPR[i4]], writes=dsts[0][1](g))

    def mm_parts(self, b, n, colsel, slot, t0, hreads):
        parts = colsel(slot, 0)

        def fn(e):
            ins = None
            for pi_ in range(len(parts)):
                for kc in range(KC):
                    psl, lhsT = colsel(slot, kc)[pi_]
                    ins = e.matmul(self.PS[b][psl, 0:n], lhsT, self.H[:, kc, t0:t0 + n], start=(kc == 0),
                                   stop=(kc == KC - 1))
            return ins

        self.p.op("pe", fn, reads=hreads + [self.WAR[slot]], writes=[self.PSR[b]])

    def proj_v(self, wi, col0):
        p = self.p
        Vv = self.V.rearrange("q b (h c) -> q b h c", c=65)
        for b0 in range(0, 18, 4):
            nb = min(4, 18 - b0)
            bk = self.pbank("st")
            for j in range(nb):
                blk = b0 + j
                g = min(blk // 4, 4)
                self.mm_group(self.PS[bk][:, j * P:(j + 1) * P],
                              [(self.H[:, kc, blk * P:(blk + 1) * P], self.WA[wi][:, kc, col0:col0 + P])
                               for kc in range(KC)],
                              reads=[self.HR[kc][g] for kc in range(KC)] + [self.WAR[wi]], writes=[self.PSR[bk]])
            dst = Vv[:, b0:b0 + nb, :, 0:64]
            src = self.PS[bk][:, 0:nb * P].rearrange("q (b h c) -> q b h c", h=2, c=64)
            self.evac_copy(dst, src, [self.PSR[bk]], [self.VR[b0 + j] for j in range(nb)])

    def next_pt(self):
        i = self.pt_rr
        self.pt_rr = (self.pt_rr + 1) % len(self.PT)
        return i

    def attn_multi(self, streams, n, scale, hooks=None):
        p = self.p
        ns = len(streams)
        ni = len(streams[0]["items"])
        seq = []
        for i in range(ni):
            for sidx in range(ns):
                seq.append((sidx, i))
        LA = len(self.PT) - 1
        pend = []

        def issue_qk(sidx, i):
            it = streams[sidx]["items"][i]
            st = self.pbank("st")
            pairs = [(it["kT"], it["q"])]
            reads = list(it["reads"])
            if it.get("extra") is not None:
                pairs.append(it["extra"])
                reads.append(self.constR)
            self.mm_group(self.PS[st][:, 0:n], pairs, reads=reads, writes=[self.PSR[st]])
            pi = self.next_pt()
            pt = self.PT[pi][:, 0:n]
            p.op("act", lambda e, pt=pt, st=st: e.activation(out=pt, in_=self.PS[st][:, 0:n], func=AF.Exp, scale=scale),
                 reads=[self.PSR[st]], writes=[self.PTR[pi]])
            return pi

        LA = min(LA, 4, len(self.PT) - 2)
        for k in range(min(LA, len(seq))):
            pend.append(issue_qk(*seq[k]))
        hooks = list(hooks) if hooks else []
        G = 2
        for k0 in range(0, len(seq), G):
            if hooks and k0 >= 4 and (k0 - 4) % 8 == 0:
                hooks.pop(0)()
            for k in range(k0, min(k0 + G, len(seq))):
                if k + LA < len(seq):
                    pend.append(issue_qk(*seq[k + LA]))
            for k in range(k0, min(k0 + G, len(seq))):
                pi = pend.pop(0)
                sidx, i = seq[k]
                stt = streams[sidx]
                blk = stt["items"][i]["vblk"]
                lhsT = stt["vsel"](blk)
                rhs = self.PT[pi][:, 0:n]
                acc = stt["acc"]
                p.op("pe", lambda e, lhsT=lhsT, rhs=rhs, i=i, acc=acc: e.matmul(self.PS[acc][:, 0:n], lhsT, rhs,
                                                                                start=(i == 0), stop=(i == ni - 1)),
                     reads=[self.PTR[pi], self.VR[blk]], writes=[self.PSR[acc]])
        while hooks:
            hooks.pop(0)()

    def tmp_any(self):
        i = self.tmp_any_rr
        self.tmp_any_rr = (self.tmp_any_rr + 1) % len(self.TMP)
        return i

    def fin1(self, acc, n, add_ap=None, mul_ap=None):
        p = self.p
        ti = self.tmp_any()
        t = self.TMP[ti]
        tr = self.TMPR[ti]
        p.op("act", lambda e: e.copy(out=t[0:65, 0:n], in_=self.PS[acc][0:65, 0:n]), reads=[self.PSR[acc]], writes=[tr])
        d = t[64:65, 0:n]
        if add_ap is not None:
            p.op("dve", lambda e: e.tensor_scalar_add(out=d, in0=d, scalar1=add_ap), reads=[tr, self.constR], writes=[tr])
        p.op("dve", lambda e: e.reciprocal(out=d, in_=d), reads=[tr], writes=[tr])
        if mul_ap is not None:
            p.op("dve", lambda e: e.tensor_scalar_mul(out=d, in0=d, scalar1=mul_ap), reads=[tr, self.LAMR], writes=[tr])
        j = self.rb_rr
        self.rb_rr = (self.rb_rr + 1) % len(self.RB)
        self.rb_of[ti] = j
        rb = self.RB[j][:, 0:n]
        p.op("dve", lambda e: e.tensor_copy(out=rb, in_=d), reads=[tr], writes=[self.RBR[j]])
        return ti

    def fin_bc(self, ti, n):
        p = self.p
        bc = self.pbank("st")
        t = self.TMP[ti]
        j = self.rb_of[ti]
        rb = self.RB[j][:, 0:n]
        p.op("pe", lambda e: e.matmul(self.PS[bc][0:64, 0:n], self.onesb[64:65, 0:64], rb, start=True,
                                      stop=True), reads=[self.RBR[j], self.constR], writes=[self.PSR[bc]])
        return bc

    def fin2_simple(self, ti, n, ydst, yres):
        p = self.p
        bc = self.fin_bc(ti, n)
        t = self.TMP[ti]
        p.op("dve", lambda e: e.tensor_tensor(out=ydst, in0=t[0:64, 0:n], in1=self.PS[bc][0:64, 0:n], op=ALU.mult),
             reads=[self.TMPR[ti], self.PSR[bc]], writes=yres)

    def fin2_diff_a(self, i0, i1, n):
        p = self.p
        bc0 = self.fin_bc(i0, n)
        bc1 = self.fin_bc(i1, n)
        t0_ = self.TMP[i0][0:64, 0:n]
        t1_ = self.TMP[i1][0:64, 0:n]
        p.op("dve", lambda e: e.tensor_tensor(out=t0_, in0=t0_, in1=self.PS[bc0][0:64, 0:n], op=ALU.mult),
             reads=[self.TMPR[i0], self.PSR[bc0]], writes=[self.TMPR[i0]])
        p.op("dve", lambda e: e.tensor_tensor(out=t1_, in0=t1_, in1=self.PS[bc1][0:64, 0:n], op=ALU.mult),
             reads=[self.TMPR[i1], self.PSR[bc1]], writes=[self.TMPR[i1]])
        p.op("pool", lambda e: e.tensor_tensor(out=t0_, in0=t0_, in1=t1_, op=ALU.add),
             reads=[self.TMPR[i0], self.TMPR[i1]], writes=[self.TMPR[i0]])
        p.op("pool", lambda e: e.tensor_tensor(out=t1_, in0=t0_, in1=t0_, op=ALU.mult),
             reads=[self.TMPR[i0], self.TMPR[i1]], writes=[self.TMPR[i1]])

    def fin2_diff_b(self, l, i0, i1, n, ydst, yres):
        p = self.p
        t0_ = self.TMP[i0][0:64, 0:n]
        t1_ = self.TMP[i1][0:64, 0:n]
        bs = self.pbank("st")
        p.op("pe", lambda e: e.matmul(self.PS[bs][0:64, 0:n], self.onesf[0:64, 0:64], t1_, start=True, stop=True),
             reads=[self.TMPR[i1], self.constR], writes=[self.PSR[bs]])
        p.op("act", lambda e: e.activation(out=t1_, in_=self.PS[bs][0:64, 0:n], func=AF.Sqrt, scale=1.0 / 64,
                                           bias=self.epsc[0:64, 0:1]),
             reads=[self.PSR[bs], self.constR], writes=[self.TMPR[i1]])
        p.op("dve", lambda e: e.reciprocal(out=t1_, in_=t1_), reads=[self.TMPR[i1]], writes=[self.TMPR[i1]])
        p.op("dve", lambda e: e.tensor_tensor(out=t0_, in0=t0_, in1=t1_, op=ALU.mult),
             reads=[self.TMPR[i0], self.TMPR[i1]], writes=[self.TMPR[i0]])
        p.op("act", lambda e: e.activation(out=ydst, in_=t0_, func=AF.Identity, scale=self.subg[:, l:l + 1]),
             reads=[self.TMPR[i0], self.constR], writes=yres)

    def recip_bcast(self, acc, n, add_ap=None, mul_ap=None):
        p = self.p
        ri = 0
        rec = self.REC[ri][64:65, 0:n]
        rr = self.RECR[ri]
        den = self.PS[acc][64:65, 0:n]
        if add_ap is not None:
            p.op("dve", lambda e: e.tensor_scalar_add(out=rec, in0=den, scalar1=add_ap),
                 reads=[self.PSR[acc], self.constR], writes=[rr])
            p.op("dve", lambda e: e.reciprocal(out=rec, in_=rec), reads=[rr], writes=[rr])
        else:
            p.op("dve", lambda e: e.reciprocal(out=rec, in_=den), reads=[self.PSR[acc]], writes=[rr])
        if mul_ap is not None:
            p.op("dve", lambda e: e.tensor_scalar_mul(out=rec, in0=rec, scalar1=mul_ap), reads=[rr, self.LAMR],
                 writes=[rr])
        bc = self.pbank("st")
        p.op("pe", lambda e: e.matmul(self.PS[bc][0:64, 0:n], self.onesf[64:65, 0:64], rec, start=True, stop=True),
             reads=[rr, self.constR], writes=[self.PSR[bc]])
        return bc

    def finish_simple(self, acc, n, ydst, yres, add_ap=None):
        p = self.p
        bc = self.recip_bcast(acc, n, add_ap=add_ap)
        ti = self.tmp(0)
        t = self.TMP[ti][0:64, 0:n]
        p.op("act", lambda e: e.copy(out=t, in_=self.PS[acc][0:64, 0:n]), reads=[self.PSR[acc]], writes=[self.TMPR[ti]])
        p.op("dve", lambda e: e.tensor_tensor(out=ydst, in0=t, in1=self.PS[bc][0:64, 0:n], op=ALU.mult),
             reads=[self.TMPR[ti], self.PSR[bc]], writes=yres)

    def finish_c(self, accs, n, ydst, yres):
        p = self.p
        ti = self.tmp(0)
        t = self.TMP[ti][0:65, 0:n]
        p.op("act", lambda e: e.copy(out=t, in_=self.PS[accs[1]][0:65, 0:n]), reads=[self.PSR[accs[1]]],
             writes=[self.TMPR[ti]])
        p.op("dve", lambda e: e.tensor_tensor(out=t, in0=t, in1=self.PS[accs[0]][0:65, 0:n], op=ALU.add),
             reads=[self.TMPR[ti], self.PSR[accs[0]]], writes=[self.TMPR[ti]])
        rec = self.REC[0][64:65, 0:n]
        rr = self.RECR[0]
        p.op("dve", lambda e: e.reciprocal(out=rec, in_=self.TMP[ti][64:65, 0:n]), reads=[self.TMPR[ti]], writes=[rr])
        bc = self.pbank("st")
        p.op("pe", lambda e: e.matmul(self.PS[bc][0:64, 0:n], self.onesf[64:65, 0:64], rec, start=True, stop=True),
             reads=[rr, self.constR], writes=[self.PSR[bc]])
        p.op("dve", lambda e: e.tensor_tensor(out=ydst, in0=self.TMP[ti][0:64, 0:n], in1=self.PS[bc][0:64, 0:n],
                                              op=ALU.mult),
             reads=[self.TMPR[ti], self.PSR[bc]], writes=yres)

    def out_proj(self, l, heads_rows, tgs):
        p = self.p
        wbs = []
        for r0 in heads_rows:
            i = self.wb_rr
            self.wb_rr = (self.wb_rr + 1) % len(self.WB)
            dst = self.WB[i][0:64, :]
            src = self.wout_d[l, r0:r0 + 64, :]
            p.dma("pool", lambda e, dst=dst, src=src: e.dma_start(out=dst, in_=src), self.wb_d[i],
                  writes=[self.WBR[i]])
            wbs.append(i)
        for g in tgs:
            t0, n = TGS[g]
            stream = 0 if g < 4 else 1
            mr = self.MODR[l * 2 + stream]
            for m in range(KC):
                b = self.pbank("st")
                self.mm_group(self.PS[b][:, 0:n],
                              [(self.WB[wbs[h]][0:64, m * P:(m + 1) * P], self.Y[0:64, h, t0:t0 + n]) for h in range(2)],
                              reads=[self.YR[h][g] for h in range(2)] + [self.WBR[w] for w in wbs],
                              writes=[self.PSR[b]])
                xs = self.X[:, m, t0:t0 + n]
                ga = self.modap(l, stream, 5, m)
                if m % 2 == 0:
                    p.op("dve", lambda e, xs=xs, b=b, n=n, ga=ga: e.scalar_tensor_tensor(
                        out=xs, in0=self.PS[b][:, 0:n], scalar=ga, in1=xs, op0=ALU.mult, op1=ALU.add),
                         reads=[self.PSR[b], mr, self.XR[m][g]], writes=[self.XR[m][g]])
                else:
                    ti = self.tmp(2)
                    tt = self.TMP[ti][:, 0:n]
                    p.op("act", lambda e, tt=tt, b=b, n=n, ga=ga: e.activation(out=tt, in_=self.PS[b][:, 0:n],
                                                                               func=AF.Identity, scale=ga),
                         reads=[self.PSR[b], mr], writes=[self.TMPR[ti]])
                    p.op("pool", lambda e, xs=xs, tt=tt: e.tensor_tensor(out=xs, in0=xs, in1=tt, op=ALU.add),
                         reads=[self.TMPR[ti], self.XR[m][g]], writes=[self.XR[m][g]])

    def mixers(self, l, nmix):
        p = self.p
        win = self.win_d[l]
        qtgs = list(range(5)) if l == 0 else list(range(4))
        alltg = list(range(5))
        self.QR = [[Res(f"Q{g}_{e}") for e in range(2)] for g in range(5)]
        self.KR = [[Res(f"K{c}_{g}") for g in range(5)] for c in range(2)]
        self.VR = [Res(f"V{b}") for b in range(18)]
        self.YR = [[Res(f"Y{h}_{g}") for g in range(5)] for h in range(2)]
        Vv = self.V.rearrange("q b (h c) -> q b h c", c=65)

        def set_ones():
            p.op("pool", lambda e: e.memset(Vv[:, :, :, 64:65], 1.0), writes=self.VR)

        def qdst():
            return [(lambda g: self.QT[:, TGS[g][0]:TGS[g][0] + TGS[g][1]], lambda g: [self.QR[g][0], self.QR[g][1]])]

        def kdst(c):
            return (lambda g: self.KT[:, c, TGS[g][0]:TGS[g][0] + TGS[g][1]], lambda g: [self.KR[c][g]])

        nat = lambda c0: (lambda slot, kc: [(slice(0, P), self.WA[slot][:, kc, c0:c0 + P])])
        pair = lambda a: (lambda slot, kc: [(slice(0, 64), self.WA[slot][:, kc, a * 64:a * 64 + 64]),
                                            (slice(64, P), self.WA[slot][:, kc, (a + 2) * 64:(a + 2) * 64 + 64])])

        for mix in range(nmix):
            name = "ABCD"[mix]
            if name in self.skipmix:
                continue
            kind = 32 if name == "D" else 64
            qd = 8 if name == "D" else 16
            scale = (32 ** -0.5) if name == "D" else 0.125
            for a in range(2):
                if name in "AB":
                    if a == 0:
                        set_ones()
                        wkv = self.load_wa(win[:, mix * 512 + 256: mix * 512 + 512])
                        wkvp = self.perm_weights(wkv, qd)
                        self.proj_fm(l, wkv, wkvp, nat(0), [kdst(0)], kind, alltg, norm_col=(2 if name == "B" else None))
                        self.proj_v(wkv, 128)
                    wq = self.load_wa(win[:, mix * 512: mix * 512 + 256])
                    wqp = self.perm_weights(wq, qd)
                    self.proj_fm(l, wq, wqp, pair(a), qdst(), kind, qtgs, norm_col=(0 if name == "B" else None))
                    heads = [a, a + 2]
                    kvh = [0, 1]
                elif name == "C":
                    set_ones()
                    wk = self.load_wa(win[:, 1280:1536])
                    self.proj_fm(l, wk, None, nat(a * P), [kdst(0)], kind, alltg)
                    wv = self.load_wa(win[:, 1536:1792])
                    self.proj_v(wv, a * P)
                    wq = self.load_wa(win[:, 1024:1280])
                    self.proj_fm(l, wq, None, nat(a * P), qdst(), kind, qtgs)
                    heads = [2 * a, 2 * a + 1]
                    kvh = [0, 1]
                    self.build_tc(l, a)
                else:
                    set_ones()
                    wk = self.load_wa(win[:, 2048:2304])
                    wkp = self.perm_weights(wk, qd)
                    self.proj_fm(l, wk, wkp, nat(a * P), [kdst(0), kdst(1)], kind, alltg, pad=True)
                    wv = self.load_wa(win[:, 2304:2560])
                    self.proj_v(wv, a * P)
                    wq = self.load_wa(win[:, 1792:2048])
                    wqp = self.perm_weights(wq, qd)
                    self.proj_fm(l, wq, wqp, nat(a * P), qdst(), kind, qtgs)
                    heads = [2 * a, 2 * a + 1]
                    kvh = [0, 1]
                pending = []

                def run_pending():
                    while pending:
                        pending.pop(0)()

                for g in qtgs:
                    t0, n = TGS[g]
                    if g == 4:
                        kbs = [16, 17]
                    elif name == "A":
                        kbs = [kb for kb in range(4 * g - 1, 4 * g + 5) if 0 <= kb < 16] + [16, 17]
                    else:
                        kbs = list(range(18))

                    def mk(e, kb, c=0, mask=False, g=g, t0=t0, n=n):
                        pr = slice(64 * e, 64 * e + 64)
                        it = {"kT": self.KT[pr, c, kb * P:(kb + 1) * P], "q": self.QT[pr, t0:t0 + n],
                              "reads": [self.KR[c][min(kb // 4, 4)], self.QR[g][e]], "vblk": kb}
                        if mask:
                            o = kb - 4 * g
                            it["extra"] = (self.identb[:, :], self.strip[:, (4 - o) * P:(4 - o) * P + n])
                        return it

                    vsels = [(lambda blk, e=e: self.BIG[:, 3 * T + blk * 130 + kvh[e] * 65:
                                                        3 * T + blk * 130 + kvh[e] * 65 + 128]) for e in range(2)]
                    vsels65 = [(lambda blk, e=e: self.V[:, blk, kvh[e] * 65:kvh[e] * 65 + 65]) for e in range(2)]
                    ydsts = [self.Y[0:64, e, t0:t0 + n] for e in range(2)]
                    yress = [[self.YR[e][g]] for e in range(2)]
                    if name in "AB" or (name == "C" and g == 4):
                        accs = [self.pbank("acc"), self.pbank("acc")]
                        streams = [{"items": [mk(e, kb, 0, mask=(name == "A" and g < 4 and kb < 16)) for kb in kbs],
                                    "acc": accs[e], "vsel": vsels[e]} for e in range(2)]
                        hk = list(pending)
                        del pending[:]
                        self.attn_multi(streams, n, scale, hooks=hk)
                        for e in range(2):
                            add_ap = None
                            if name == "A":
                                add_ap = self.SK[64:65, l * 4 + heads[e]:l * 4 + heads[e] + 1]
                            ti = self.fin1(accs[e], n, add_ap=add_ap)
                            pending.append(lambda ti=ti, n=n, yd=ydsts[e], yr=yress[e]: self.fin2_simple(ti, n, yd, yr))
                    elif name == "C":
                        run_pending()
                        for e in range(2):
                            pr = slice(64 * e, 64 * e + 64)
                            accs = [self.pbank("acc"), self.pbank("acc")]
                            self.attn_c(accs, e, g, pr, vsels65[e])
                            self.finish_c(accs, n, ydsts[e], yress[e])
                    else:
                        accs = [self.pbank("acc") for _ in range(4)]
                        streams = [{"items": [mk(e, kb, c) for kb in kbs], "acc": accs[c * 2 + e], "vsel": vsels[e]}
                                   for c in range(2) for e in range(2)]
                        hk = list(pending)
                        del pending[:]
                        self.attn_multi(streams, n, scale, hooks=hk)
                        for e in range(2):
                            i0 = self.fin1(accs[e], n)
                            i1 = self.fin1(accs[2 + e], n, mul_ap=self.lams[64:65, l:l + 1])
                            pending.append(lambda i0=i0, i1=i1, n=n: self.fin2_diff_a(i0, i1, n))
                            pending.append(lambda i0=i0, i1=i1, n=n, yd=ydsts[e], yr=yress[e]:
                                           self.fin2_diff_b(l, i0, i1, n, yd, yr))
                run_pending()
                if mix == 0 and a == 0 and l == 0:
                    self.dump(0, self.QT[:, 0:2048], [])
                    self.dump(1, self.KT[:, 0, 0:2048], [])
                    self.dump(2, self.BIG[:, 3 * T:3 * T + 2048], [])
                    self.dump(3, self.Y[0:64, 0, 0:2048], [], np_=64)
                    self.dump(4, self.Y[0:64, 1, 0:2048], [], np_=64)
                    self.dump(5, self.H[:, 0, 0:2048], [])
                self.out_proj(l, [mix * 256 + h * 64 for h in heads], qtgs)

    def finish_diff(self, l, acc0, acc1, n, ydst, yres):
        p = self.p
        bc0 = self.recip_bcast(acc0, n)
        i0 = self.tmp(0)
        t0_ = self.TMP[i0][0:64, 0:n]
        p.op("act", lambda e: e.copy(out=t0_, in_=self.PS[acc0][0:64, 0:n]), reads=[self.PSR[acc0]],
             writes=[self.TMPR[i0]])
        p.op("dve", lambda e: e.tensor_tensor(out=t0_, in0=t0_, in1=self.PS[bc0][0:64, 0:n], op=ALU.mult),
             reads=[self.TMPR[i0], self.PSR[bc0]], writes=[self.TMPR[i0]])
        bc1 = self.recip_bcast(acc1, n, mul_ap=self.lams[64:65, l:l + 1])
        i1 = self.tmp(1)
        t1_ = self.TMP[i1][0:64, 0:n]
        p.op("act", lambda e: e.copy(out=t1_, in_=self.PS[acc1][0:64, 0:n]), reads=[self.PSR[acc1]],
             writes=[self.TMPR[i1]])
        p.op("dve", lambda e: e.tensor_tensor(out=t1_, in0=t1_, in1=self.PS[bc1][0:64, 0:n], op=ALU.mult),
             reads=[self.TMPR[i1], self.PSR[bc1]], writes=[self.TMPR[i1]])
        p.op("pool", lambda e: e.tensor_tensor(out=t0_, in0=t0_, in1=t1_, op=ALU.add),
             reads=[self.TMPR[i0], self.TMPR[i1]], writes=[self.TMPR[i0]])
        p.op("pool", lambda e: e.tensor_tensor(out=t1_, in0=t0_, in1=t0_, op=ALU.mult),
             reads=[self.TMPR[i0], self.TMPR[i1]], writes=[self.TMPR[i1]])
        bs = self.pbank("st")
        p.op("pe", lambda e: e.matmul(self.PS[bs][0:64, 0:n], self.onesf[0:64, 0:64], t1_, start=True, stop=True),
             reads=[self.TMPR[i1], self.constR], writes=[self.PSR[bs]])
        p.op("act", lambda e: e.activation(out=t1_, in_=self.PS[bs][0:64, 0:n], func=AF.Sqrt, scale=1.0 / 64,
                                           bias=self.epsc[0:64, 0:1]),
             reads=[self.PSR[bs], self.constR], writes=[self.TMPR[i1]])
        p.op("dve", lambda e: e.reciprocal(out=t1_, in_=t1_), reads=[self.TMPR[i1]], writes=[self.TMPR[i1]])
        p.op("dve", lambda e: e.tensor_tensor(out=t0_, in0=t0_, in1=t1_, op=ALU.mult),
             reads=[self.TMPR[i0], self.TMPR[i1]], writes=[self.TMPR[i0]])
        p.op("act", lambda e: e.activation(out=ydst, in_=t0_, func=AF.Identity, scale=self.subg[:, l:l + 1]),
             reads=[self.TMPR[i0], self.constR], writes=yres)

    def build_tc(self, l, a):
        p = self.p
        p.dma("sp", lambda e: e.dma_start(out=self.STG[0][:, 0:960], in_=self.relT_d[l, a]), self.stg_d[0],
              writes=[self.STGR[0]])
        p.dma("sp", lambda e: e.dma_start(out=self.STG[1][:, 0:960], in_=self.cmask_d), self.stg_d[1],
              writes=[self.STGR[1]])
        p.op("dve", lambda e: e.scalar_tensor_tensor(out=self.Tc, in0=self.STG[0][:, 0:960], scalar=8.0,
                                                     in1=self.STG[1][:, 0:960], op0=ALU.mult, op1=ALU.add),
             reads=[self.STGR[0], self.STGR[1]], writes=[self.CSR[0]])

    def attn_c(self, accs, e, g, pr, vsel):
        p = self.p
        t0, n = TGS[g]
        scale = 0.125
        acc = accs[0]
        sts = []
        for kb in (16, 17):
            st = self.pbank("st")
            self.mm_group(self.PS[st][:, 0:n], [(self.KT[pr, 0, kb * P:(kb + 1) * P], self.QT[pr, t0:t0 + n])],
                          reads=[self.KR[0][4], self.QR[g][e]], writes=[self.PSR[st]])
            pi = self.next_pt()
            pt = self.PT[pi][:, 0:n]
            p.op("act", lambda e_, pt=pt, st=st: e_.activation(out=pt, in_=self.PS[st][:, 0:n], func=AF.Exp, scale=scale),
                 reads=[self.PSR[st]], writes=[self.PTR[pi]])
            sts.append((pi, kb))
        for i, (pi, kb) in enumerate(sts):
            lhsT = vsel(kb)
            rhs = self.PT[pi][:, 0:n]
            p.op("pe", lambda e_, lhsT=lhsT, rhs=rhs, i=i: e_.matmul(self.PS[acc][0:65, 0:n], lhsT, rhs, start=(i == 0),
                                                                     stop=False, skip_group_check=True),
                 reads=[self.PTR[pi], self.VR[kb]], writes=[self.PSR[acc]])
        Tcv = self.Tc[pr, :].rearrange("q (b c) -> q b c", c=64)
        pend = []

        def issue_row(rr):
            r = 8 * g + rr
            rs_ = min(max(r - 4, 0), 24)
            st = self.pbank("st")
            stv = self.PS[st]

            def fn(e_, r=r, rs_=rs_, stv=stv):
                ins = None
                first = {0: True, 1: True}
                for j in range(8):
                    rk = rs_ + j
                    par = rk % 2
                    ins = e_.matmul(stv[par * 64:(par + 1) * 64, j * 64:(j + 1) * 64],
                                    self.KT[pr, 0, rk * 64:(rk + 1) * 64], self.QT[pr, r * 64:(r + 1) * 64],
                                    start=first[par], stop=False, skip_group_check=True)
                    first[par] = False
                for par in range(2):
                    j0 = (par - rs_) % 2
                    b0 = rs_ + j0 - r + 7
                    ov = stv[par * 64:(par + 1) * 64, :].rearrange("q (j c) -> q j c", c=64)[:, j0:8:2, :]
                    ins = e_.matmul(ov, self.identb[pr, pr], Tcv[:, b0:b0 + 7:2, :], start=False, stop=True,
                                    skip_group_check=True)
                return ins

            kg = sorted(set(min((rs_ + j) // 8, 3) for j in range(8)))
            p.op("pe", fn, reads=[self.KR[0][k] for k in kg] + [self.QR[g][e], self.CSR[0], self.constR],
                 writes=[self.PSR[st]])
            pi = self.next_pt()
            pt = self.PT[pi][:, :]
            p.op("act", lambda e_, pt=pt, stv=stv: e_.activation(out=pt, in_=stv[:, :], func=AF.Exp, scale=scale),
                 reads=[self.PSR[st]], writes=[self.PTR[pi]])
            return (pi, rr, rs_)

        LA = 2
        for rr in range(min(LA, 8)):
            pend.append(issue_row(rr))
        for rr in range(8):
            if rr + LA < 8:
                pend.append(issue_row(rr + LA))
            pi, rr_, rs_ = pend.pop(0)

            def fn2(e_, pi=pi, rr_=rr_, rs_=rs_):
                ins = None
                for j in range(8):
                    rk = rs_ + j
                    par = rk % 2
                    ins = e_.matmul(self.PS[accs[par]][0:65, rr_ * 64:(rr_ + 1) * 64],
                                    self.V[par * 64:(par + 1) * 64, rk // 2, e * 65:e * 65 + 65],
                                    self.PT[pi][par * 64:(par + 1) * 64, j * 64:(j + 1) * 64],
                                    start=(par == 1 and rr_ == 0 and j < 2), stop=(rr_ == 7 and j >= 6),
                                    skip_group_check=True)
                return ins

            vb = sorted(set((rs_ + j) // 2 for j in range(8)))
            p.op("pe", fn2, reads=[self.PTR[pi]] + [self.VR[b] for b in vb],
                 writes=[self.PSR[accs[0]], self.PSR[accs[1]]])


_NC_CACHE = {}


def _get_nc(stage, debug=False, skipmix=""):
    if (stage, debug, skipmix) not in _NC_CACHE:
        _NC_CACHE[(stage, debug, skipmix)] = Builder(stage, debug, skipmix).build()
    return _NC_CACHE[(stage, debug, skipmix)]


_HC = {}


def _rope_tables(dim):
    t = np.arange(S, dtype=np.int32)
    row = (t // 64).astype(np.float32)
    col = (t % 64).astype(np.float32)
    half = dim // 2
    freqs = (np.float32(10000.0) ** (-np.arange(0, half, 2, dtype=np.float32) / np.float32(half))).astype(np.float32)
    ang_r = row[:, None] * freqs[None, :]
    ang_c = col[:, None] * freqs[None, :]
    ang = np.concatenate([ang_r, ang_r, ang_c, ang_c], axis=-1).astype(np.float32)
    cos = np.cos(ang).astype(np.float32)
    sin = np.sin(ang).astype(np.float32)
    qd = half // 2
    sign = np.where((np.arange(dim) % half) < qd, -1.0, 1.0).astype(np.float32)
    tab = np.zeros((P, 2, T), np.float32)
    reps = P // dim
    tab[:, 0, :S] = np.tile(cos.T, (reps, 1))
    tab[:, 1, :S] = np.tile((sin * sign[None, :]).T, (reps, 1))
    tab[:, 0, S:] = 1.0
    return tab


def _host_consts():
    if _HC:
        return _HC
    _HC["cs64"] = _rope_tables(64)
    _HC["cs32"] = _rope_tables(32)
    NEG = -30000.0
    kk = np.arange(P)[:, None]; qq = np.arange(P)[None, :]
    Mb = np.full((P, P), NEG, np.float32)
    Ub = np.where(kk <= qq, 0.0, NEG).astype(np.float32)
    Lb = np.where(kk >= qq, 0.0, NEG).astype(np.float32)
    Zb = np.zeros((P, P), np.float32)
    _HC["strip"] = np.ascontiguousarray(np.concatenate([Mb, Mb, Mb, Ub, Zb, Lb, Mb, Mb, Mb], axis=1))
    col = np.arange(64)
    c_start = np.clip(col - 8, 0, 48)
    ok = (col[:, None] >= c_start[None, :]) & (col[:, None] < c_start[None, :] + 16)
    cm = np.where(ok, 0.0, NEG).astype(np.float32)
    _HC["cmask"] = np.ascontiguousarray(np.tile(cm, (2, 15)))
    pm = np.zeros((P, 2), np.float32)
    pm[:, 0] = ((np.arange(P) % 64) < 32)
    pm[:, 1] = ((np.arange(P) % 64) >= 32)
    _HC["pmask"] = pm
    return _HC


def kernel(stage=99, debug=False, skipmix="", **inputs):
    nc = _get_nc(stage, debug, skipmix)
    f = lambda a: np.ascontiguousarray(np.asarray(a), dtype=np.float32)
    x = f(inputs["x"]); ctx = f(inputs["ctx"]); c = f(inputs["c"]); c_ctx = f(inputs["c_ctx"])
    common = {
        "final_norm_g": f(inputs["final_norm_g"]).reshape(1, D),
        "identf": np.eye(P, dtype=np.float32),
        "w_ada": f(inputs["w_ada"]),
        "b_adaT": np.ascontiguousarray(f(inputs["b_ada"]).reshape(2, 72, P).transpose(0, 2, 1)),
    }
    for n in ("w_ffn1_gate", "w_ffn1_up", "w_ffn1_down", "w_ffn2_gate", "w_ffn2_up", "w_ffn2_down", "w_in", "w_out",
              "sink_logit"):
        common[n] = f(inputs[n])
    common.update(_host_consts())
    perm64 = np.array([(d + 16) if (d % 32) < 16 else (d - 16) for d in range(64)])
    qg = f(inputs["q_norm_g"]); kg = f(inputs["k_norm_g"])
    qkg = np.stack([np.tile(qg, (1, 2)), np.tile(qg[:, perm64], (1, 2)),
                    np.tile(kg, (1, 2)), np.tile(kg[:, perm64], (1, 2))], axis=-1)
    common["qkg"] = np.ascontiguousarray(qkg)
    common["lamv"] = np.ascontiguousarray(np.stack([f(inputs["lam_q1"]), f(inputs["lam_k1"]), f(inputs["lam_q2"]),
                                                    f(inputs["lam_k2"])], axis=1))
    common["subg"] = np.ascontiguousarray(f(inputs["subln_g"]).T)
    rel = f(inputs["rel_pos_bias"])
    ck = np.arange(64)[:, None]; cq = np.arange(64)[None, :]
    dc = np.clip(ck - cq + 15, 0, 30)
    relT = rel[:, :, :, dc]
    relT = relT.transpose(0, 1, 3, 2, 4).reshape(2, 2, P, 960)
    common["relT"] = np.ascontiguousarray(relT)
    in_maps = []
    for b in range(8):
        cc = np.stack([c[b].reshape(KC, P).T, c_ctx.reshape(KC, P).T], axis=-1).reshape(P, 16)
        m = dict(common)
        m.update({"x": x[b], "ctx": ctx[b], "cc": np.ascontiguousarray(cc)})
        in_maps.append(m)
    res = run_bass_kernel_spmd(nc, in_maps, core_ids=list(range(8)))
    if debug:
        return np.stack([r["out"] for r in res.results], axis=0), res.results[0]["dbg"]
    return np.stack([r["out"] for r in res.results], axis=0)
```

```python
import numpy as np
import concourse.bass as bass
import concourse.mybir as mybir
from concourse.bass_utils import run_bass_kernel_spmd
from contextlib import ExitStack

F32 = mybir.dt.float32
BF16 = mybir.dt.bfloat16
AF = mybir.ActivationFunctionType
ALU = mybir.AluOpType
AX = mybir.AxisListType

P = 128
D = 1024
KC = 8
S = 2048
L = 256
T = S + L
TGS = [(0, 512), (512, 512), (1024, 512), (1536, 512), (2048, 256)]
DFF = 2816
FC = 22
EPS = 1e-6
SAME_ENGINE_SYNC = True


class Res:
    __slots__ = ("name", "w", "r")

    def __init__(self, name=""):
        self.name = name
        self.w = None
        self.r = {}


class DSem:
    def __init__(self, sem, name):
        self.sem = sem
        self.count = 0
        self.name = name


class Prog:
    COMPUTE = ("pe", "act", "dve", "pool")
    ALL = ("pe", "act", "dve", "pool", "sp")

    def __init__(self, nc, es):
        self.nc = nc
        self.es = es
        self.streams = {e: [] for e in self.ALL}
        self.cnt = {e: 0 for e in self.COMPUTE}
        self.sem = {e: es.enter_context(nc.semaphore("c_" + e)) for e in self.COMPUTE}
        self.seen = {e: {} for e in self.ALL}
        self.dsems = []
        self.n_ins = 0

    def dsem(self, name):
        d = DSem(self.es.enter_context(self.nc.semaphore("d_" + name)), name)
        self.dsems.append(d)
        return d

    def _collect(self, eng, reads, writes):
        deps = {}

        def add(tok):
            if tok is None:
                return
            k, v = tok
            if deps.get(k, 0) < v:
                deps[k] = v

        for r in reads:
            add(r.w)
        for w in writes:
            add(w.w)
            for k, v in w.r.items():
                add((k, v))
        waits = []
        for k, v in deps.items():
            if k == eng and (eng == "pe" or not SAME_ENGINE_SYNC):
                continue
            if self.seen[eng].get(k, 0) >= v:
                continue
            self.seen[eng][k] = v
            waits.append((k, v))
        return waits

    def _update(self, tok, reads, writes):
        k, v = tok
        for r in reads:
            if r.r.get(k, 0) < v:
                r.r[k] = v
        for w in writes:
            w.w = tok
            w.r = {}

    def op(self, eng, fn, reads=(), writes=()):
        waits = self._collect(eng, reads, writes)
        self.cnt[eng] += 1
        tok = (eng, self.cnt[eng])
        self.streams[eng].append((waits, fn, (self.sem[eng], 1)))
        self._update(tok, reads, writes)
        return tok

    def dma(self, q, fn, dsem, reads=(), writes=()):
        waits = self._collect(q, reads, writes)
        dsem.count += 16
        tok = (dsem, dsem.count)
        self.streams[q].append((waits, fn, (dsem.sem, 16)))
        self._update(tok, reads, writes)
        return tok

    def wait_all(self, eng):
        waits = []
        for e in self.COMPUTE:
            if e != eng and self.cnt[e] > self.seen[eng].get(e, 0):
                waits.append((e, self.cnt[e]))
                self.seen[eng][e] = self.cnt[e]
        for d in self.dsems:
            if d.count > self.seen[eng].get(d, 0):
                waits.append((d, d.count))
                self.seen[eng][d] = d.count
        self.streams[eng].append((waits, None, None))

    def barrier(self):
        for e in self.ALL:
            self.wait_all(e)

    def emit(self):
        nc = self.nc
        handles = {"pe": "tensor", "act": "scalar", "dve": "vector", "pool": "gpsimd", "sp": "sync"}
        with nc.Block() as block:
            for e in self.ALL:
                recs = self.streams[e]

                def body(eng, recs=recs):
                    for waits, fn, inc in recs:
                        for k, v in waits:
                            s = self.sem[k] if isinstance(k, str) else k.sem
                            eng.wait_ge(s, v)
                        if fn is not None:
                            ins = fn(eng)
                            ins.then_inc(inc[0], inc[1])

                getattr(block, handles[e])(body)


class Builder:
    def __init__(self, stage, debug=False, skipmix=""):
        self.stage = stage
        self.debug = debug
        self.skipmix = skipmix
        self.nc = bass.Bass("TRN2", target_bir_lowering=False)
        self.es = ExitStack()

    def dram_in(self, name, shape, dt=F32):
        return self.nc.dram_tensor(name, list(shape), dt, kind="ExternalInput").ap()

    def sb(self, name, shape, dt):
        return self.es.enter_context(self.nc.sbuf_tensor("sb_" + name, list(shape), dt))

    def build(self):
        nc = self.nc
        with self.es:
            self.p = Prog(nc, self.es)
            self._build()
            self.p.emit()
        return nc

    def _build(self):
        nc, p = self.nc, self.p
        st = self.stage
        self.x_d = self.dram_in("x", [S, D])
        self.ctx_d = self.dram_in("ctx", [L, D])
        self.fng_d = self.dram_in("final_norm_g", [1, D])
        self.identf_d = self.dram_in("identf", [P, P])
        self.cc_d = self.dram_in("cc", [P, 16])
        self.wada_d = self.dram_in("w_ada", [2, D, 9 * D])
        self.bada_d = self.dram_in("b_adaT", [2, P, 72])
        self.wg_d = [self.dram_in("w_ffn1_gate", [2, D, DFF]), self.dram_in("w_ffn2_gate", [2, D, DFF])]
        self.wu_d = [self.dram_in("w_ffn1_up", [2, D, DFF]), self.dram_in("w_ffn2_up", [2, D, DFF])]
        self.wd_d = [self.dram_in("w_ffn1_down", [2, DFF, D]), self.dram_in("w_ffn2_down", [2, DFF, D])]
        self.win_d = self.dram_in("w_in", [2, D, 2560])
        self.wout_d = self.dram_in("w_out", [2, D, D])
        self.cs_d = {64: self.dram_in("cs64", [P, 2, T]), 32: self.dram_in("cs32", [P, 2, T])}
        self.strip_d = self.dram_in("strip", [P, 1152])
        self.sink_d = self.dram_in("sink_logit", [2, 4])
        self.qkg_d = self.dram_in("qkg", [2, P, 4])
        self.lamv_d = self.dram_in("lamv", [2, 4, 32])
        self.subg_d = self.dram_in("subg", [64, 2])
        self.relT_d = self.dram_in("relT", [2, 2, P, 960])
        self.cmask_d = self.dram_in("cmask", [P, 960])
        self.pmask_d = self.dram_in("pmask", [P, 2])
        self.out_d = nc.dram_tensor("out", [S, D], F32, kind="ExternalOutput").ap()
        self.dbg_d = nc.dram_tensor("dbg", [P, 6, 2048], F32, kind="ExternalOutput").ap() if self.debug else None

        self.X = self.sb("X", [P, KC, T], F32)
        self.XR = [[Res(f"X{c}_{g}") for g in range(5)] for c in range(KC)]
        self.H = self.sb("H", [P, KC, T], BF16)
        self.HR = [[Res(f"H{c}_{g}") for g in range(5)] for c in range(KC)]
        self.BIG = self.sb("BIG", [P, 5 * T + 2340], BF16)
        self.STG = [self.sb(f"stg{i}", [P, 2048], F32) for i in range(2)]
        self.STGR = [Res(f"stg{i}") for i in range(2)]
        self.stg_d = [p.dsem(f"stg{i}") for i in range(2)]
        self.stg_rr = 0
        self.stgb_d = [p.dsem(f"stgb{i}") for i in range(2)]
        NWA, NWB = 4, 4
        self.WA = [self.sb(f"wa{i}", [P, 8, 256], BF16) for i in range(NWA)]
        self.WAR = [Res(f"wa{i}") for i in range(NWA)]
        self.wa_d = [p.dsem(f"wa{i}") for i in range(NWA)]
        self.wa_rr = 0
        self.WB = [self.sb(f"wb{i}", [P, 1024], BF16) for i in range(NWB)]
        self.WBR = [Res(f"wb{i}") for i in range(NWB)]
        self.wb_d = [p.dsem(f"wb{i}") for i in range(NWB)]
        self.wb_rr = 0
        NTMP = 6
        self.TMP = [self.sb(f"tmp{i}", [P, 512], F32) for i in range(NTMP)]
        self.TMPR = [Res(f"tmp{i}") for i in range(NTMP)]
        self.tmp_rr = [0, 0, 0]
        self.tmp_any_rr = 0
        self.identf = self.sb("identf", [P, P], F32)
        self.onesf = self.sb("onesf", [P, P], F32)
        self.small = self.sb("small", [P, 64], F32)
        self.SMR = [Res(f"sm{i}") for i in range(64)]
        self.constR = Res("const")
        self.epsc = self.sb("epsc", [P, 1], F32)
        self.cc = self.sb("cc", [P, 16], F32)
        self.sT = self.sb("sT", [P, 16], BF16)
        self.sTR = Res("sT")
        self.bada = self.sb("bada", [P, 2, 72], F32)
        self.MOD = self.sb("MOD", [P, 4 * 72], F32)
        self.MODR = [Res(f"mod{i}") for i in range(4)]
        self.QT = self.BIG[:, 0:T]
        self.KT = self.BIG[:, T:3 * T].rearrange("q (c t) -> q c t", c=2)
        self.V = self.BIG[:, 3 * T:3 * T + 2340].rearrange("q (b c) -> q b c", c=130)
        self.Y = self.BIG[:, 3 * T + 2340:5 * T + 2340].rearrange("q (h t) -> q h t", h=2)
        self.PT = [self.sb(f"pt{i}", [P, 512], BF16) for i in range(6)]
        self.PTR = [Res(f"pt{i}") for i in range(6)]
        self.pt_rr = 0
        self.CS = [self.sb(f"cs{i}", [P, 2, 512], F32) for i in range(2)]
        self.CSR = [Res(f"cs{i}") for i in range(2)]
        self.cs_ds = [p.dsem(f"cs{i}") for i in range(2)]
        self.cs_rr = 0
        self.Tc = self.CS[0][:, :, :].rearrange("q a n -> q (a n)").bitcast(BF16)[:, 0:960]
        self.identb = self.sb("identb", [P, P], BF16)
        self.onesb = self.sb("onesb", [P, 64], BF16)
        self.blockones = self.sb("blockones", [P, P], F32)
        self.strip = self.sb("strip", [P, 1152], BF16)
        yoff = 3 * T + 2340
        self.SK = self.sb("SK", [P, 8], F32)
        self.qkg = self.sb("qkg", [P, 2, 4], F32)
        self.lams = self.sb("lams", [P, 16], F32)
        self.LAMR = Res("lam")
        self.subg = self.sb("subg", [64, 2], F32)
        self.pmask = self.sb("pmask", [P, 2], F32)
        self.REC = [self.BIG[:, yoff:yoff + 1024].bitcast(F32)]
        self.RB = [self.BIG[64:65, yoff + 1024 + j * 512:yoff + 1024 + (j + 1) * 512] for j in range(7)]
        self.RBR = [Res(f"rb{j}") for j in range(7)]
        self.rb_rr = 0
        self.rb_of = {}
        self.RECR = [Res(f"rec{i}") for i in range(1)]
        self.rec_rr = 0
        self.PS = [self.es.enter_context(nc.psum_tensor(f"ps{i}", [P, 512], F32)) for i in range(8)]
        self.pool_rr = {"st": 0, "acc": 0}
        self.pools = {"st": (0, 1, 2, 3), "acc": (4, 5, 6, 7)}
        self.PSR = [Res(f"ps{i}") for i in range(8)]
        self.ps_rr = 0
        self.bg = []
        self.evac_rr = 0
        self.cd = p.dsem("const")
        self.out_ds = p.dsem("out")

        p.op("pool", lambda e: e.memset(self.epsc[:], EPS), writes=[self.constR])
        p.op("pool", lambda e: e.memset(self.onesf[:], 1.0), writes=[self.constR])
        self.cdma(self.identf[:], self.identf_d)
        self.cdma(self.cc[:], self.cc_d)
        self.cdma(self.bada[:], self.bada_d.rearrange("l p j -> p l j"))
        self.cdma(self.qkg[:], self.qkg_d.rearrange("l p j -> p l j"))
        self.cdma(self.pmask[:], self.pmask_d)
        self.cdma(self.SK[64:65, 0:8], self.sink_d.rearrange("(o l) h -> o (l h)", o=1))
        self.cdma(self.subg[:], self.subg_d)
        p.dma("pool", lambda e: e.dma_start(out=self.strip[:], in_=self.strip_d), p.dsem("strip"), writes=[self.constR])
        p.op("pool", lambda e: e.memset(self.onesb[:], 1.0), writes=[self.constR])
        p.op("pool", lambda e: e.memset(self.blockones[:], 0.0), writes=[self.constR])
        p.op("pool", lambda e: e.memset(self.blockones[0:64, 0:64], 1.0), writes=[self.constR])
        p.op("pool", lambda e: e.memset(self.blockones[64:128, 64:128], 1.0), writes=[self.constR])
        p.op("act", lambda e: e.copy(out=self.identb[:], in_=self.identf[:]), reads=[self.constR], writes=[self.constR])
        p.op("act", lambda e: e.activation(out=self.SK[64:65, 0:8], in_=self.SK[64:65, 0:8], func=AF.Exp),
             reads=[self.constR], writes=[self.constR])
        for l in range(2):
            li = 1.0 - (0.8 - 0.6 * float(np.exp(-0.3 * l)))
            p.op("dve", lambda e, l=l, li=li: e.tensor_scalar_mul(out=self.subg[:, l:l + 1], in0=self.subg[:, l:l + 1],
                                                                   scalar1=li), reads=[self.constR], writes=[self.constR])
        p.op("act", lambda e: e.activation(out=self.sT[:], in_=self.cc[:], func=AF.Silu),
             reads=[self.constR], writes=[self.sTR])

        self.load_x()
        k = 0
        for l in range(2):
            if st >= k + 1:
                if l == 0:
                    for f_ in self.adaln_steps(0, list(range(0, 6)), [0, 1, 2]):
                        f_()
                    self.bg = self.adaln_steps(0, list(range(6, 18)), [3, 4, 5, 6, 7, 8])
                self.norm_mod(l, 0, 1)
                self.ffn(l, 0, 2, ctx=True)
            if st >= k + 2:
                p.barrier()
                self.norm_mod(l, 3, 4)
                self.lam_setup(l)
                self.mixers(l, min(4, st - k - 1))
                p.barrier()
            if st >= k + 6:
                self.norm_mod(l, 6, 7, tgs=range(5) if l == 0 else range(4))
                if l == 0 and st >= 7:
                    self.bg = self.adaln_steps(1, list(range(18)), list(range(9)))
                self.ffn(l, 1, 8, ctx=(l == 0))
            k += 6
        self.final_norm()
        p.barrier()

    def dump(self, slot, ap, reads, np_=P, w=2048):
        if not self.debug:
            return
        p = self.p
        p.barrier()
        p.op("act", lambda e: e.copy(out=self.STG[0][0:np_, 0:w], in_=ap), reads=reads, writes=[self.STGR[0]])
        p.dma("sp", lambda e: e.dma_start(out=self.dbg_d[0:np_, slot, 0:w], in_=self.STG[0][0:np_, 0:w]), self.out_ds,
              reads=[self.STGR[0]])
        p.barrier()

    def cdma(self, dst, src):
        self.p.dma("sp", lambda e: e.dma_start(out=dst, in_=src), self.cd, writes=[self.constR])

    def bank(self):
        b = self.ps_rr
        self.ps_rr = (self.ps_rr + 1) % 7
        return b

    def bg_step(self):
        if self.bg:
            self.bg.pop(0)()

    def bg_flush(self):
        while self.bg:
            self.bg.pop(0)()

    def pbank(self, pool):
        lst = self.pools[pool]
        self.pool_rr[pool] = (self.pool_rr[pool] + 1) % len(lst)
        return lst[self.pool_rr[pool]]

    def tmp(self, role):
        i = role * 2 + self.tmp_rr[role]
        self.tmp_rr[role] ^= 1
        return i

    def modap(self, l, stream, r, c):
        col = (l * 2 + stream) * 72 + r * 8 + c
        return self.MOD[:, col:col + 1]

    def mm_group(self, out, pairs, reads, writes, **kw):
        n = len(pairs)

        def fn(e):
            ins = None
            for i, (lhsT, rhs) in enumerate(pairs):
                ins = e.matmul(out, lhsT, rhs, start=(i == 0), stop=(i == n - 1), **kw)
            return ins

        return self.p.op("pe", fn, reads=reads, writes=writes)

    def load_x(self):
        p = self.p
        for g, (t0, n) in enumerate(TGS):
            ntile = n // P
            for s in range(ntile // 2):
                if g < 4:
                    src = self.x_d[t0 + s * 256: t0 + s * 256 + 256, :]
                else:
                    src = self.ctx_d[s * 256: s * 256 + 256, :]
                src = src.rearrange("(j p) d -> p j d", p=P)
                dst = self.STG[s][:, :].rearrange("p (j d) -> p j d", j=2)
                p.dma("sp", lambda e, dst=dst, src=src: e.dma_start(out=dst, in_=src), self.stg_d[s],
                      writes=[self.STGR[s]])
            for c in range(KC):
                b = self.bank()

                def tr(e, b=b, c=c, ntile=ntile):
                    ins = None
                    for j in range(ntile):
                        src = self.STG[j // 2][:, (j % 2) * 1024 + c * P:(j % 2) * 1024 + (c + 1) * P]
                        ins = e.transpose(out=self.PS[b][:, j * P:(j + 1) * P], in_=src, identity=self.identf[:])
                    return ins

                p.op("pe", tr, reads=[self.STGR[s] for s in range(ntile // 2)] + [self.constR], writes=[self.PSR[b]])
                self.evac_copy(self.X[:, c, t0:t0 + n], self.PS[b][:, 0:n], [self.PSR[b]], [self.XR[c][g]])

    def evac_copy(self, dst, src, reads, writes):
        p = self.p
        self.evac_rr ^= 1
        if self.evac_rr:
            p.op("act", lambda e: e.copy(out=dst, in_=src), reads=reads, writes=writes)
        else:
            p.op("dve", lambda e: e.tensor_copy(out=dst, in_=src), reads=reads, writes=writes)

    def final_norm(self):
        p = self.p
        self.gbc = self.WA[0][:, :, :].rearrange("q k n -> q (k n)").bitcast(F32)
        p.dma("sp", lambda e: e.dma_start(out=self.gbc, in_=self.fng_d.partition_broadcast(P)), self.cd,
              writes=[self.WAR[0]])
        for tt in range(S // P):
            g = tt // 4
            s = tt % 2
            stg = self.STG[s]
            sr = self.STGR[s]
            for h in range(2):
                b = self.bank()

                def tr(e, b=b, h=h, tt=tt):
                    ins = None
                    for j in range(4):
                        c = h * 4 + j
                        ins = e.transpose(out=self.PS[b][:, j * P:(j + 1) * P],
                                          in_=self.X[:, c, tt * P:(tt + 1) * P], identity=self.identf[:])
                    return ins

                p.op("pe", tr, reads=[self.XR[h * 4 + j][g] for j in range(4)] + [self.constR],
                     writes=[self.PSR[b]])
                dst = stg[:, h * 512:(h + 1) * 512]
                src = self.PS[b][:, :]
                if h == 0:
                    p.op("act", lambda e, dst=dst, src=src: e.copy(out=dst, in_=src), reads=[self.PSR[b]], writes=[sr])
                else:
                    p.op("dve", lambda e, dst=dst, src=src: e.tensor_copy(out=dst, in_=src), reads=[self.PSR[b]],
                         writes=[sr])
            ss = self.small[:, 2 * s:2 * s + 1]
            rs = self.small[:, 2 * s + 1:2 * s + 2]
            smr = self.SMR[s]
            p.op("act", lambda e, stg=stg, ss=ss: e.activation(out=stg[:, 1024:2048], in_=stg[:, 0:1024],
                                                              func=AF.Square, accum_out=ss),
                 reads=[sr], writes=[sr, smr])
            p.op("act", lambda e, ss=ss, rs=rs: e.activation(out=rs, in_=ss, func=AF.Sqrt, scale=1.0 / D,
                                                             bias=self.epsc[:, 0:1]),
                 reads=[smr, self.constR], writes=[smr])
            p.op("dve", lambda e, rs=rs: e.reciprocal(out=rs, in_=rs), reads=[smr], writes=[smr])
            p.op("dve", lambda e, stg=stg, rs=rs: e.scalar_tensor_tensor(out=stg[:, 1024:2048], in0=stg[:, 0:1024],
                                                                        scalar=rs, in1=self.gbc,
                                                                        op0=ALU.mult, op1=ALU.mult),
                 reads=[sr, smr, self.WAR[0]], writes=[sr])
            p.dma("sp", lambda e, stg=stg, tt=tt: e.dma_start(out=self.out_d[tt * P:(tt + 1) * P, :],
                                                             in_=stg[:, 1024:2048]),
                  self.out_ds, reads=[sr])

    def adaln_steps(self, l, pieces, rows):
        p = self.p
        b = 7
        psr = self.PSR[b]
        slots = {}

        def issue(j4):
            s_ = self.stg_rr
            self.stg_rr ^= 1
            slots[j4] = s_
            src = self.wada_d[l, :, j4 * 512:(j4 + 1) * 512].rearrange("(kc q) n -> q kc n", q=P)
            stgb = self.STG[s_][:, :].bitcast(BF16)
            dst = stgb.rearrange("q (kc n) -> q kc n", kc=8)
            p.dma("pool", lambda e: e.dma_start(out=dst, in_=src), self.stgb_d[s_], writes=[self.STGR[s_]])

        def step(idx):
            j4 = pieces[idx]
            if idx == 0:
                issue(j4)
            if idx + 1 < len(pieces):
                issue(pieces[idx + 1])
            s_ = slots[j4]
            stgb = self.STG[s_][:, :].bitcast(BF16)
            for q4 in range(4):
                j = j4 * 4 + q4
                pairs = [(stgb[:, kc * 512 + q4 * 128: kc * 512 + q4 * 128 + 128],
                          self.sT[:, 2 * kc:2 * kc + 2]) for kc in range(8)]
                self.mm_group(self.PS[b][:, 2 * j:2 * j + 2], pairs, reads=[self.STGR[s_], self.sTR], writes=[psr])

        def final():
            c0, c1 = rows[0] * 8, (rows[-1] + 1) * 8
            pv = self.PS[b][:, 0:144].rearrange("q (j t) -> q j t", t=2)
            for stream in range(2):
                base = (l * 2 + stream) * 72
                mr = self.MODR[l * 2 + stream]
                dst = self.MOD[:, base + c0:base + c1]
                p.op("dve", lambda e, dst=dst, stream=stream: e.tensor_tensor(out=dst, in0=pv[:, c0:c1, stream],
                                                                              in1=self.bada[:, l, c0:c1], op=ALU.add),
                     reads=[psr, self.constR], writes=[mr])
                for r in (1, 4, 7):
                    if r in rows:
                        d2 = self.MOD[:, base + r * 8:base + r * 8 + 8]
                        p.op("dve", lambda e, d2=d2: e.tensor_scalar_add(out=d2, in0=d2, scalar1=1.0), reads=[mr],
                             writes=[mr])
                for r in (2, 8):
                    if r in rows:
                        d2 = self.MOD[:, base + r * 8:base + r * 8 + 8]
                        p.op("dve", lambda e, d2=d2: e.tensor_scalar_mul(out=d2, in0=d2, scalar1=0.5), reads=[mr],
                             writes=[mr])

        return [(lambda idx=idx: step(idx)) for idx in range(len(pieces))] + [final]

    def norm_mod(self, l, r_shift, r_scale, tgs=range(5)):
        p = self.p
        for g in tgs:
            t0, n = TGS[g]
            stream = 0 if g < 4 else 1
            mr = self.MODR[l * 2 + stream]
            b = self.bank()
            sqs = []
            for c in range(KC):
                ti = self.tmp(0)
                xs = self.X[:, c, t0:t0 + n]
                sq = self.TMP[ti][:, 0:n]
                p.op("pool", lambda e, sq=sq, xs=xs: e.tensor_tensor(out=sq, in0=xs, in1=xs, op=ALU.mult),
                     reads=[self.XR[c][g]], writes=[self.TMPR[ti]])
                p.op("pe", lambda e, sq=sq, c=c, b=b, n=n: e.matmul(self.PS[b][:, 0:n], self.onesf[:], sq,
                                                                    start=(c == 0), stop=(c == KC - 1)),
                     reads=[self.TMPR[ti], self.constR], writes=[self.PSR[b]])
            ri = self.tmp(1)
            rs = self.TMP[ri][:, 0:n]
            p.op("act", lambda e, rs=rs, b=b, n=n: e.activation(out=rs, in_=self.PS[b][:, 0:n], func=AF.Sqrt,
                                                                scale=1.0 / D, bias=self.epsc[:, 0:1]),
                 reads=[self.PSR[b], self.constR], writes=[self.TMPR[ri]])
            p.op("dve", lambda e, rs=rs: e.reciprocal(out=rs, in_=rs), reads=[self.TMPR[ri]], writes=[self.TMPR[ri]])
            for c in range(KC):
                ti = self.tmp(2)
                tt = self.TMP[ti][:, 0:n]
                xs = self.X[:, c, t0:t0 + n]
                p.op("dve", lambda e, tt=tt, xs=xs, rs=rs: e.tensor_tensor(out=tt, in0=xs, in1=rs, op=ALU.mult),
                     reads=[self.XR[c][g], self.TMPR[ri]], writes=[self.TMPR[ti]])
                hd = self.H[:, c, t0:t0 + n]
                sc = self.modap(l, stream, r_scale, c)
                sh = self.modap(l, stream, r_shift, c)
                p.op("act", lambda e, hd=hd, tt=tt, sc=sc, sh=sh: e.activation(out=hd, in_=tt, func=AF.Identity,
                                                                              scale=sc, bias=sh),
                     reads=[self.TMPR[ti], mr], writes=[self.HR[c][g]])

    def load_wa(self, src):
        i = self.wa_rr
        self.wa_rr = (self.wa_rr + 1) % len(self.WA)
        srcv = src.rearrange("(kc q) n -> q kc n", q=P)
        dst = self.WA[i][:, :, :]
        self.p.dma("pool", lambda e: e.dma_start(out=dst, in_=srcv), self.wa_d[i], writes=[self.WAR[i]])
        return i

    def load_wb(self, src):
        i = self.wb_rr
        self.wb_rr = (self.wb_rr + 1) % len(self.WB)
        dst = self.WB[i][:, :]
        self.p.dma("pool", lambda e: e.dma_start(out=dst, in_=src), self.wb_d[i], writes=[self.WBR[i]])
        return i

    def ffn(self, l, which, r_gate, ctx=True):
        p = self.p
        wg, wu, wd = self.wg_d[which][l], self.wu_d[which][l], self.wd_d[which][l]
        tgs = list(range(5)) if ctx else list(range(4))
        U = self.BIG[:, 0:4 * T].rearrange("q (f t) -> q f t", f=4)
        UR = [[Res(f"U{f}_{g}") for g in range(5)] for f in range(4)]
        npieces = FC // 2
        slabs = [list(range(i, min(i + 2, npieces))) for i in range(0, npieces, 2)]

        def issue_piece(j):
            return (self.load_wa(wg[:, j * 256:(j + 1) * 256]), self.load_wa(wu[:, j * 256:(j + 1) * 256]))

        def issue_down(slab):
            return [self.load_wb(wd[f * P:(f + 1) * P, :]) for j in slab for f in (2 * j, 2 * j + 1)]

        pend = {0: issue_piece(0)}
        pend_d = {0: issue_down(slabs[0])}
        for si, slab in enumerate(slabs):
            for j in slab:
                if j + 1 < npieces:
                    pend[j + 1] = issue_piece(j + 1)
                ig, iu = pend.pop(j)
                for half in range(2):
                    fl = (j - slab[0]) * 2 + half
                    self.bg_step()
                    for g in tgs:
                        t0, n = TGS[g]
                        hreads = [self.HR[kc][g] for kc in range(KC)]
                        bg = self.bank()
                        self.mm_group(self.PS[bg][:, 0:n],
                                      [(self.WA[ig][:, kc, half * P:(half + 1) * P], self.H[:, kc, t0:t0 + n])
                                       for kc in range(KC)], reads=hreads + [self.WAR[ig]], writes=[self.PSR[bg]])
                        bu = self.bank()
                        self.mm_group(self.PS[bu][:, 0:n],
                                      [(self.WA[iu][:, kc, half * P:(half + 1) * P], self.H[:, kc, t0:t0 + n])
                                       for kc in range(KC)], reads=hreads + [self.WAR[iu]], writes=[self.PSR[bu]])
                        ti = self.tmp(0)
                        sg = self.TMP[ti][:, 0:n]
                        p.op("act", lambda e, sg=sg, bg=bg, n=n: e.activation(out=sg, in_=self.PS[bg][:, 0:n],
                                                                              func=AF.Silu),
                             reads=[self.PSR[bg]], writes=[self.TMPR[ti]])
                        ud = U[:, fl, t0:t0 + n]
                        p.op("dve", lambda e, ud=ud, sg=sg, bu=bu, n=n: e.tensor_tensor(out=ud, in0=sg,
                                                                                        in1=self.PS[bu][:, 0:n],
                                                                                        op=ALU.mult),
                             reads=[self.TMPR[ti], self.PSR[bu]], writes=[UR[fl][g]])
            wbs = pend_d.pop(si)
            nfl = len(wbs)
            for g in tgs:
                t0, n = TGS[g]
                stream = 0 if g < 4 else 1
                mr = self.MODR[l * 2 + stream]
                for m in range(KC):
                    b = self.bank()
                    self.mm_group(self.PS[b][:, 0:n],
                                  [(self.WB[wbs[fl]][:, m * P:(m + 1) * P], U[:, fl, t0:t0 + n]) for fl in range(nfl)],
                                  reads=[UR[fl][g] for fl in range(nfl)] + [self.WBR[w] for w in wbs],
                                  writes=[self.PSR[b]])
                    xs = self.X[:, m, t0:t0 + n]
                    ga = self.modap(l, stream, r_gate, m)
                    p.op("dve", lambda e, xs=xs, b=b, n=n, ga=ga: e.scalar_tensor_tensor(
                        out=xs, in0=self.PS[b][:, 0:n], scalar=ga, in1=xs, op0=ALU.mult, op1=ALU.add),
                         reads=[self.PSR[b], mr, self.XR[m][g]], writes=[self.XR[m][g]])
            if si + 1 < len(slabs):
                pend_d[si + 1] = issue_down(slabs[si + 1])
        self.bg_flush()

    def lam_setup(self, l):
        p = self.p
        r = self.LAMR
        ti = self.tmp(0)
        lt = self.TMP[ti]
        sm = self.lams
        p.dma("sp", lambda e: e.dma_start(out=lt[0:1, 0:128], in_=self.lamv_d[l:l + 1].rearrange("o a d -> o (a d)")),
              self.cd, writes=[self.TMPR[ti], r])
        for t in range(2):
            a = lt[0:1, (2 * t) * 32:(2 * t) * 32 + 32]
            bb = lt[0:1, (2 * t + 1) * 32:(2 * t + 1) * 32 + 32]
            p.op("dve", lambda e, a=a, bb=bb: e.tensor_tensor(out=a, in0=a, in1=bb, op=ALU.mult),
                 reads=[self.constR, r], writes=[r, self.TMPR[ti]])
            p.op("dve", lambda e, a=a, t=t: e.reduce_sum(out=sm[0:1, 8 + t:9 + t], in_=a, axis=AX.X),
                 reads=[r, self.TMPR[ti]], writes=[r])
        p.op("act", lambda e: e.activation(out=sm[0:1, 8:10], in_=sm[0:1, 8:10], func=AF.Exp), reads=[r], writes=[r])
        lam_init = 0.8 - 0.6 * float(np.exp(-0.3 * l))
        p.op("dve", lambda e: e.tensor_tensor(out=sm[0:1, 10:11], in0=sm[0:1, 9:10], in1=sm[0:1, 8:9],
                                              op=ALU.subtract), reads=[r], writes=[r])
        p.op("dve", lambda e: e.tensor_scalar_add(out=sm[0:1, 10:11], in0=sm[0:1, 10:11], scalar1=-lam_init),
             reads=[r], writes=[r])
        b = self.pbank("st")
        p.op("pe", lambda e: e.matmul(self.PS[b][:, 0:1], self.onesf[0:1, :], sm[0:1, 10:11], start=True, stop=True),
             reads=[r, self.constR], writes=[self.PSR[b]])
        p.op("dve", lambda e: e.tensor_copy(out=sm[:, l:l + 1], in_=self.PS[b][:, 0:1]), reads=[self.PSR[b]],
             writes=[r])

    def load_cs(self, kind, g):
        t0, n = TGS[g]
        i = self.cs_rr
        self.cs_rr ^= 1
        src = self.cs_d[kind][:, :, t0:t0 + n]
        dst = self.CS[i][:, :, 0:n]
        self.p.dma("sp", lambda e: e.dma_start(out=dst, in_=src), self.cs_ds[i], writes=[self.CSR[i]])
        return i

    def perm_weights(self, wi, qd):
        wp = self.wa_rr
        self.wa_rr = (self.wa_rr + 1) % len(self.WA)
        sv = self.WA[wi][:, :, :].rearrange("q k (b t d) -> q k b t d", t=2, d=qd)
        dv = self.WA[wp][:, :, :].rearrange("q k (b t d) -> q k b t d", t=2, d=qd)
        self.p.op("pool", lambda e: e.tensor_copy(out=dv[:, :, :, 0, :], in_=sv[:, :, :, 1, :]),
                  reads=[self.WAR[wi]], writes=[self.WAR[wp]])
        self.p.op("pool", lambda e: e.tensor_copy(out=dv[:, :, :, 1, :], in_=sv[:, :, :, 0, :]),
                  reads=[self.WAR[wi]], writes=[self.WAR[wp]])
        return wp

    def proj_fm(self, l, wi, wpi, colsel, dsts, kind, tgs, norm_col=None, pad=False):
        p = self.p
        for g in tgs:
            t0, n = TGS[g]
            hreads = [self.HR[kc][g] for kc in range(KC)]
            bp = self.pbank("st")
            self.mm_parts(bp, n, colsel, wi, t0, hreads)
            if wpi is None:
                dst = dsts[0][0](g)
                self.evac_copy(dst, self.PS[bp][:, 0:n], [self.PSR[bp]], dsts[0][1](g))
                continue
            ci = self.load_cs(kind, g)
            cos = self.CS[ci][:, 0, 0:n]
            ssin = self.CS[ci][:, 1, 0:n]
            br = self.pbank("st")
            self.mm_parts(br, n, colsel, wpi, t0, hreads)
            i1 = self.tmp(0)
            i2 = self.tmp(1)
            t1 = self.TMP[i1][:, 0:n]
            t2 = self.TMP[i2][:, 0:n]
            pp = self.PS[bp][:, 0:n]
            pr = self.PS[br][:, 0:n]
            if norm_col is None:
                p.op("dve", lambda e, t1=t1, pp=pp, cos=cos: e.tensor_tensor(out=t1, in0=pp, in1=cos, op=ALU.mult),
                     reads=[self.PSR[bp], self.CSR[ci]], writes=[self.TMPR[i1]])
                p.op("dve", lambda e, t2=t2, pr=pr, ssin=ssin: e.tensor_tensor(out=t2, in0=pr, in1=ssin, op=ALU.mult),
                     reads=[self.PSR[br], self.CSR[ci]], writes=[self.TMPR[i2]])
                if not pad:
                    dst = dsts[0][0](g)
                    p.op("pool", lambda e, dst=dst, t1=t1, t2=t2: e.tensor_tensor(out=dst, in0=t1, in1=t2, op=ALU.add),
                         reads=[self.TMPR[i1], self.TMPR[i2]], writes=dsts[0][1](g))
                else:
                    p.op("pool", lambda e, t1=t1, t2=t2: e.tensor_tensor(out=t1, in0=t1, in1=t2, op=ALU.add),
                         reads=[self.TMPR[i1], self.TMPR[i2]], writes=[self.TMPR[i1]])
                    for k2 in range(2):
                        dst = dsts[k2][0](g)
                        m = self.pmask[:, k2:k2 + 1]
                        eng = "dve" if k2 == 0 else "pool"
                        p.op(eng, lambda e, dst=dst, t1=t1, m=m: e.tensor_scalar_mul(out=dst, in0=t1, scalar1=m),
                             reads=[self.TMPR[i1], self.constR], writes=dsts[k2][1](g))
            else:
                i3 = self.tmp(2)
                i4 = self.tmp(2)
                sq = self.TMP[i3][:, 0:n]
                rs = self.TMP[i4][:, 0:n]
                p.op("act", lambda e, sq=sq, pp=pp: e.activation(out=sq, in_=pp, func=AF.Square),
                     reads=[self.PSR[bp]], writes=[self.TMPR[i3]])
                bs = self.pbank("st")
                p.op("pe", lambda e, bs=bs, sq=sq, n=n: e.matmul(self.PS[bs][:, 0:n], self.blockones[:], sq,
                                                                 start=True, stop=True),
                     reads=[self.TMPR[i3], self.constR], writes=[self.PSR[bs]])
                p.op("act", lambda e, rs=rs, bs=bs, n=n: e.activation(out=rs, in_=self.PS[bs][:, 0:n], func=AF.Sqrt,
                                                                      scale=1.0 / 64, bias=self.epsc[:, 0:1]),
                     reads=[self.PSR[bs], self.constR], writes=[self.TMPR[i4]])
                p.op("dve", lambda e, rs=rs: e.reciprocal(out=rs, in_=rs), reads=[self.TMPR[i4]],
                     writes=[self.TMPR[i4]])
                g1 = self.qkg[:, l, norm_col:norm_col + 1]
                g2 = self.qkg[:, l, norm_col + 1:norm_col + 2]
                p.op("dve", lambda e, t1=t1, pp=pp, cos=cos, g1=g1: e.scalar_tensor_tensor(
                    out=t1, in0=pp, scalar=g1, in1=cos, op0=ALU.mult, op1=ALU.mult),
                     reads=[self.PSR[bp], self.CSR[ci], self.constR], writes=[self.TMPR[i1]])
                p.op("dve", lambda e, t2=t2, pr=pr, ssin=ssin, g2=g2: e.scalar_tensor_tensor(
                    out=t2, in0=pr, scalar=g2, in1=ssin, op0=ALU.mult, op1=ALU.mult),
                     reads=[self.PSR[br], self.CSR[ci], self.constR], writes=[self.TMPR[i2]])
                p.op("pool", lambda e, t1=t1, t2=t2: e.tensor_tensor(out=t1, in0=t1, in1=t2, op=ALU.add),
                     reads=[self.TMPR[i1], self.TMPR[i2]], writes=[self.TMPR[i1]])
                dst = dsts[0][0](g)
                p.op("dve", lambda e, dst=dst, t1=t1, rs=rs: e.tensor_tensor(out=dst, in0=t1, in1=rs, op=ALU.mult),
                     reads=[self.TMPR[i1], self.TMPR[i4]], writes=dsts[0][1](g))

    def mm_parts(self, b, n, colsel, slot, t0, hreads):
        parts = colsel(slot, 0)

        def fn(e):
            ins = None
            for pi_ in range(len(parts)):
                for kc in range(KC):
                    psl, lhsT = colsel(slot, kc)[pi_]
                    ins = e.matmul(self.PS[b][psl, 0:n], lhsT, self.H[:, kc, t0:t0 + n], start=(kc == 0),
                                   stop=(kc == KC - 1))
            return ins

        self.p.op("pe", fn, reads=hreads + [self.WAR[slot]], writes=[self.PSR[b]])

    def proj_v(self, wi, col0):
        p = self.p
        Vv = self.V.rearrange("q b (h c) -> q b h c", c=65)
        for b0 in range(0, 18, 4):
            nb = min(4, 18 - b0)
            bk = self.pbank("st")
            for j in range(nb):
                blk = b0 + j
                g = min(blk // 4, 4)
                self.mm_group(self.PS[bk][:, j * P:(j + 1) * P],
                              [(self.H[:, kc, blk * P:(blk + 1) * P], self.WA[wi][:, kc, col0:col0 + P])
                               for kc in range(KC)],
                              reads=[self.HR[kc][g] for kc in range(KC)] + [self.WAR[wi]], writes=[self.PSR[bk]])
            dst = Vv[:, b0:b0 + nb, :, 0:64]
            src = self.PS[bk][:, 0:nb * P].rearrange("q (b h c) -> q b h c", h=2, c=64)
            self.evac_copy(dst, src, [self.PSR[bk]], [self.VR[b0 + j] for j in range(nb)])

    def next_pt(self):
        i = self.pt_rr
        self.pt_rr = (self.pt_rr + 1) % len(self.PT)
        return i

    def attn_multi(self, streams, n, scale, hooks=None):
        p = self.p
        ns = len(streams)
        ni = len(streams[0]["items"])
        seq = []
        for i in range(ni):
            for sidx in range(ns):
                seq.append((sidx, i))
        LA = len(self.PT) - 1
        pend = []

        def issue_qk(sidx, i):
            it = streams[sidx]["items"][i]
            st = self.pbank("st")
            pairs = [(it["kT"], it["q"])]
            reads = list(it["reads"])
            if it.get("extra") is not None:
                pairs.append(it["extra"])
                reads.append(self.constR)
            self.mm_group(self.PS[st][:, 0:n], pairs, reads=reads, writes=[self.PSR[st]])
            pi = self.next_pt()
            pt = self.PT[pi][:, 0:n]
            p.op("act", lambda e, pt=pt, st=st: e.activation(out=pt, in_=self.PS[st][:, 0:n], func=AF.Exp, scale=scale),
                 reads=[self.PSR[st]], writes=[self.PTR[pi]])
            return pi

        LA = min(LA, len(self.pools["st"]), len(self.PT) - 2)
        for k in range(min(LA, len(seq))):
            pend.append(issue_qk(*seq[k]))
        hooks = list(hooks) if hooks else []
        G = 2
        for k0 in range(0, len(seq), G):
            if hooks and k0 >= 4 and (k0 - 4) % 8 == 0:
                hooks.pop(0)()
            for k in range(k0, min(k0 + G, len(seq))):
                if k + LA < len(seq):
                    pend.append(issue_qk(*seq[k + LA]))
            for k in range(k0, min(k0 + G, len(seq))):
                pi = pend.pop(0)
                sidx, i = seq[k]
                stt = streams[sidx]
                blk = stt["items"][i]["vblk"]
                lhsT = stt["vsel"](blk)
                rhs = self.PT[pi][:, 0:n]
                acc = stt["acc"]
                p.op("pe", lambda e, lhsT=lhsT, rhs=rhs, i=i, acc=acc: e.matmul(self.PS[acc][0:65, 0:n], lhsT, rhs,
                                                                                start=(i == 0), stop=(i == ni - 1)),
                     reads=[self.PTR[pi], self.VR[blk]], writes=[self.PSR[acc]])
        while hooks:
            hooks.pop(0)()

    def tmp_any(self):
        i = self.tmp_any_rr
        self.tmp_any_rr = (self.tmp_any_rr + 1) % len(self.TMP)
        return i

    def fin1(self, acc, n, add_ap=None, mul_ap=None):
        p = self.p
        ti = self.tmp_any()
        t = self.TMP[ti]
        tr = self.TMPR[ti]
        p.op("act", lambda e: e.copy(out=t[0:65, 0:n], in_=self.PS[acc][0:65, 0:n]), reads=[self.PSR[acc]], writes=[tr])
        d = t[64:65, 0:n]
        if add_ap is not None:
            p.op("dve", lambda e: e.tensor_scalar_add(out=d, in0=d, scalar1=add_ap), reads=[tr, self.constR], writes=[tr])
        p.op("dve", lambda e: e.reciprocal(out=d, in_=d), reads=[tr], writes=[tr])
        if mul_ap is not None:
            p.op("dve", lambda e: e.tensor_scalar_mul(out=d, in0=d, scalar1=mul_ap), reads=[tr, self.LAMR], writes=[tr])
        j = self.rb_rr
        self.rb_rr = (self.rb_rr + 1) % len(self.RB)
        self.rb_of[ti] = j
        rb = self.RB[j][:, 0:n]
        p.op("dve", lambda e: e.tensor_copy(out=rb, in_=d), reads=[tr], writes=[self.RBR[j]])
        return ti

    def fin_bc(self, ti, n):
        p = self.p
        bc = self.pbank("st")
        t = self.TMP[ti]
        j = self.rb_of[ti]
        rb = self.RB[j][:, 0:n]
        p.op("pe", lambda e: e.matmul(self.PS[bc][0:64, 0:n], self.onesb[64:65, 0:64], rb, start=True,
                                      stop=True), reads=[self.RBR[j], self.constR], writes=[self.PSR[bc]])
        return bc

    def fin2_simple(self, ti, n, ydst, yres):
        p = self.p
        bc = self.fin_bc(ti, n)
        t = self.TMP[ti]
        p.op("dve", lambda e: e.tensor_tensor(out=ydst, in0=t[0:64, 0:n], in1=self.PS[bc][0:64, 0:n], op=ALU.mult),
             reads=[self.TMPR[ti], self.PSR[bc]], writes=yres)

    def fin2_diff_a(self, i0, i1, n):
        p = self.p
        bc0 = self.fin_bc(i0, n)
        bc1 = self.fin_bc(i1, n)
        t0_ = self.TMP[i0][0:64, 0:n]
        t1_ = self.TMP[i1][0:64, 0:n]
        p.op("dve", lambda e: e.tensor_tensor(out=t0_, in0=t0_, in1=self.PS[bc0][0:64, 0:n], op=ALU.mult),
             reads=[self.TMPR[i0], self.PSR[bc0]], writes=[self.TMPR[i0]])
        p.op("dve", lambda e: e.tensor_tensor(out=t1_, in0=t1_, in1=self.PS[bc1][0:64, 0:n], op=ALU.mult),
             reads=[self.TMPR[i1], self.PSR[bc1]], writes=[self.TMPR[i1]])
        p.op("pool", lambda e: e.tensor_tensor(out=t0_, in0=t0_, in1=t1_, op=ALU.add),
             reads=[self.TMPR[i0], self.TMPR[i1]], writes=[self.TMPR[i0]])
        p.op("pool", lambda e: e.tensor_tensor(out=t1_, in0=t0_, in1=t0_, op=ALU.mult),
             reads=[self.TMPR[i0], self.TMPR[i1]], writes=[self.TMPR[i1]])

    def fin2_diff_b(self, l, i0, i1, n, ydst, yres):
        p = self.p
        t0_ = self.TMP[i0][0:64, 0:n]
        t1_ = self.TMP[i1][0:64, 0:n]
        bs = self.pbank("st")
        p.op("pe", lambda e: e.matmul(self.PS[bs][0:64, 0:n], self.onesf[0:64, 0:64], t1_, start=True, stop=True),
             reads=[self.TMPR[i1], self.constR], writes=[self.PSR[bs]])
        p.op("act", lambda e: e.activation(out=t1_, in_=self.PS[bs][0:64, 0:n], func=AF.Sqrt, scale=1.0 / 64,
                                           bias=self.epsc[0:64, 0:1]),
             reads=[self.PSR[bs], self.constR], writes=[self.TMPR[i1]])
        p.op("dve", lambda e: e.reciprocal(out=t1_, in_=t1_), reads=[self.TMPR[i1]], writes=[self.TMPR[i1]])
        p.op("dve", lambda e: e.tensor_tensor(out=t0_, in0=t0_, in1=t1_, op=ALU.mult),
             reads=[self.TMPR[i0], self.TMPR[i1]], writes=[self.TMPR[i0]])
        p.op("act", lambda e: e.activation(out=ydst, in_=t0_, func=AF.Identity, scale=self.subg[:, l:l + 1]),
             reads=[self.TMPR[i0], self.constR], writes=yres)

    def recip_bcast(self, acc, n, add_ap=None, mul_ap=None):
        p = self.p
        ri = 0
        rec = self.REC[ri][64:65, 0:n]
        rr = self.RECR[ri]
        den = self.PS[acc][64:65, 0:n]
        if add_ap is not None:
            p.op("dve", lambda e: e.tensor_scalar_add(out=rec, in0=den, scalar1=add_ap),
                 reads=[self.PSR[acc], self.constR], writes=[rr])
            p.op("dve", lambda e: e.reciprocal(out=rec, in_=rec), reads=[rr], writes=[rr])
        else:
            p.op("dve", lambda e: e.reciprocal(out=rec, in_=den), reads=[self.PSR[acc]], writes=[rr])
        if mul_ap is not None:
            p.op("dve", lambda e: e.tensor_scalar_mul(out=rec, in0=rec, scalar1=mul_ap), reads=[rr, self.LAMR],
                 writes=[rr])
        bc = self.pbank("st")
        p.op("pe", lambda e: e.matmul(self.PS[bc][0:64, 0:n], self.onesf[64:65, 0:64], rec, start=True, stop=True),
             reads=[rr, self.constR], writes=[self.PSR[bc]])
        return bc

    def finish_simple(self, acc, n, ydst, yres, add_ap=None):
        p = self.p
        bc = self.recip_bcast(acc, n, add_ap=add_ap)
        ti = self.tmp(0)
        t = self.TMP[ti][0:64, 0:n]
        p.op("act", lambda e: e.copy(out=t, in_=self.PS[acc][0:64, 0:n]), reads=[self.PSR[acc]], writes=[self.TMPR[ti]])
        p.op("dve", lambda e: e.tensor_tensor(out=ydst, in0=t, in1=self.PS[bc][0:64, 0:n], op=ALU.mult),
             reads=[self.TMPR[ti], self.PSR[bc]], writes=yres)

    def finish_c(self, accs, n, ydst, yres):
        p = self.p
        ti = self.tmp(0)
        t = self.TMP[ti][0:65, 0:n]
        p.op("act", lambda e: e.copy(out=t, in_=self.PS[accs[1]][0:65, 0:n]), reads=[self.PSR[accs[1]]],
             writes=[self.TMPR[ti]])
        p.op("dve", lambda e: e.tensor_tensor(out=t, in0=t, in1=self.PS[accs[0]][0:65, 0:n], op=ALU.add),
             reads=[self.TMPR[ti], self.PSR[accs[0]]], writes=[self.TMPR[ti]])
        rec = self.REC[0][64:65, 0:n]
        rr = self.RECR[0]
        p.op("dve", lambda e: e.reciprocal(out=rec, in_=self.TMP[ti][64:65, 0:n]), reads=[self.TMPR[ti]], writes=[rr])
        bc = self.pbank("st")
        p.op("pe", lambda e: e.matmul(self.PS[bc][0:64, 0:n], self.onesf[64:65, 0:64], rec, start=True, stop=True),
             reads=[rr, self.constR], writes=[self.PSR[bc]])
        p.op("dve", lambda e: e.tensor_tensor(out=ydst, in0=self.TMP[ti][0:64, 0:n], in1=self.PS[bc][0:64, 0:n],
                                              op=ALU.mult),
             reads=[self.TMPR[ti], self.PSR[bc]], writes=yres)

    def out_proj(self, l, heads_rows, tgs):
        p = self.p
        wbs = []
        for r0 in heads_rows:
            i = self.wb_rr
            self.wb_rr = (self.wb_rr + 1) % len(self.WB)
            dst = self.WB[i][0:64, :]
            src = self.wout_d[l, r0:r0 + 64, :]
            p.dma("pool", lambda e, dst=dst, src=src: e.dma_start(out=dst, in_=src), self.wb_d[i],
                  writes=[self.WBR[i]])
            wbs.append(i)
        for g in tgs:
            t0, n = TGS[g]
            stream = 0 if g < 4 else 1
            mr = self.MODR[l * 2 + stream]
            for m in range(KC):
                b = self.pbank("st")
                self.mm_group(self.PS[b][:, 0:n],
                              [(self.WB[wbs[h]][0:64, m * P:(m + 1) * P], self.Y[0:64, h, t0:t0 + n]) for h in range(2)],
                              reads=[self.YR[h][g] for h in range(2)] + [self.WBR[w] for w in wbs],
                              writes=[self.PSR[b]])
                xs = self.X[:, m, t0:t0 + n]
                ga = self.modap(l, stream, 5, m)
                if m % 2 == 0:
                    p.op("dve", lambda e, xs=xs, b=b, n=n, ga=ga: e.scalar_tensor_tensor(
                        out=xs, in0=self.PS[b][:, 0:n], scalar=ga, in1=xs, op0=ALU.mult, op1=ALU.add),
                         reads=[self.PSR[b], mr, self.XR[m][g]], writes=[self.XR[m][g]])
                else:
                    ti = self.tmp(2)
                    tt = self.TMP[ti][:, 0:n]
                    p.op("act", lambda e, tt=tt, b=b, n=n, ga=ga: e.activation(out=tt, in_=self.PS[b][:, 0:n],
                                                                               func=AF.Identity, scale=ga),
                         reads=[self.PSR[b], mr], writes=[self.TMPR[ti]])
                    p.op("pool", lambda e, xs=xs, tt=tt: e.tensor_tensor(out=xs, in0=xs, in1=tt, op=ALU.add),
                         reads=[self.TMPR[ti], self.XR[m][g]], writes=[self.XR[m][g]])

    def mixers(self, l, nmix):
        p = self.p
        win = self.win_d[l]
        qtgs = list(range(5)) if l == 0 else list(range(4))
        alltg = list(range(5))
        self.QR = [[Res(f"Q{g}_{e}") for e in range(2)] for g in range(5)]
        self.KR = [[Res(f"K{c}_{g}") for g in range(5)] for c in range(2)]
        self.VR = [Res(f"V{b}") for b in range(18)]
        self.YR = [[Res(f"Y{h}_{g}") for g in range(5)] for h in range(2)]
        Vv = self.V.rearrange("q b (h c) -> q b h c", c=65)

        def set_ones():
            p.op("pool", lambda e: e.memset(Vv[:, :, :, 64:65], 1.0), writes=self.VR)

        def qdst():
            return [(lambda g: self.QT[:, TGS[g][0]:TGS[g][0] + TGS[g][1]], lambda g: [self.QR[g][0], self.QR[g][1]])]

        def kdst(c):
            return (lambda g: self.KT[:, c, TGS[g][0]:TGS[g][0] + TGS[g][1]], lambda g: [self.KR[c][g]])

        nat = lambda c0: (lambda slot, kc: [(slice(0, P), self.WA[slot][:, kc, c0:c0 + P])])
        pair = lambda a: (lambda slot, kc: [(slice(0, 64), self.WA[slot][:, kc, a * 64:a * 64 + 64]),
                                            (slice(64, P), self.WA[slot][:, kc, (a + 2) * 64:(a + 2) * 64 + 64])])

        for mix in range(nmix):
            name = "ABCD"[mix]
            if name in self.skipmix:
                continue
            kind = 32 if name == "D" else 64
            qd = 8 if name == "D" else 16
            scale = (32 ** -0.5) if name == "D" else 0.125
            if name in "AB":
                self.pools = {"st": (0, 1, 2, 3, 4, 5), "acc": (6, 7)}
            else:
                self.pools = {"st": (0, 1, 2, 3), "acc": (4, 5, 6, 7)}
            self.pool_rr = {"st": 0, "acc": 0}
            for a in range(2):
                if name in "AB":
                    if a == 0:
                        set_ones()
                        wkv = self.load_wa(win[:, mix * 512 + 256: mix * 512 + 512])
                        wkvp = self.perm_weights(wkv, qd)
                        self.proj_fm(l, wkv, wkvp, nat(0), [kdst(0)], kind, alltg, norm_col=(2 if name == "B" else None))
                        self.proj_v(wkv, 128)
                    wq = self.load_wa(win[:, mix * 512: mix * 512 + 256])
                    wqp = self.perm_weights(wq, qd)
                    self.proj_fm(l, wq, wqp, pair(a), qdst(), kind, qtgs, norm_col=(0 if name == "B" else None))
                    heads = [a, a + 2]
                    kvh = [0, 1]
                elif name == "C":
                    set_ones()
                    wk = self.load_wa(win[:, 1280:1536])
                    self.proj_fm(l, wk, None, nat(a * P), [kdst(0)], kind, alltg)
                    wv = self.load_wa(win[:, 1536:1792])
                    self.proj_v(wv, a * P)
                    wq = self.load_wa(win[:, 1024:1280])
                    self.proj_fm(l, wq, None, nat(a * P), qdst(), kind, qtgs)
                    heads = [2 * a, 2 * a + 1]
                    kvh = [0, 1]
                    self.build_tc(l, a)
                else:
                    set_ones()
                    wk = self.load_wa(win[:, 2048:2304])
                    wkp = self.perm_weights(wk, qd)
                    self.proj_fm(l, wk, wkp, nat(a * P), [kdst(0), kdst(1)], kind, alltg, pad=True)
                    wv = self.load_wa(win[:, 2304:2560])
                    self.proj_v(wv, a * P)
                    wq = self.load_wa(win[:, 1792:2048])
                    wqp = self.perm_weights(wq, qd)
                    self.proj_fm(l, wq, wqp, nat(a * P), qdst(), kind, qtgs)
                    heads = [2 * a, 2 * a + 1]
                    kvh = [0, 1]
                pending = []

                def run_pending():
                    while pending:
                        pending.pop(0)()

                for g in qtgs:
                    t0, n = TGS[g]
                    if g == 4:
                        kbs = [16, 17]
                    elif name == "A":
                        kbs = [kb for kb in range(4 * g - 1, 4 * g + 5) if 0 <= kb < 16] + [16, 17]
                    else:
                        kbs = list(range(18))

                    def mk(e, kb, c=0, mask=False, g=g, t0=t0, n=n):
                        pr = slice(64 * e, 64 * e + 64)
                        it = {"kT": self.KT[pr, c, kb * P:(kb + 1) * P], "q": self.QT[pr, t0:t0 + n],
                              "reads": [self.KR[c][min(kb // 4, 4)], self.QR[g][e]], "vblk": kb}
                        if mask:
                            o = kb - 4 * g
                            it["extra"] = (self.identb[:, :], self.strip[:, (4 - o) * P:(4 - o) * P + n])
                        return it

                    vsels = [(lambda blk, e=e: self.V[:, blk, kvh[e] * 65:kvh[e] * 65 + 65]) for e in range(2)]
                    ydsts = [self.Y[0:64, e, t0:t0 + n] for e in range(2)]
                    yress = [[self.YR[e][g]] for e in range(2)]
                    if name in "AB" or (name == "C" and g == 4):
                        accs = [self.pbank("acc"), self.pbank("acc")]
                        streams = [{"items": [mk(e, kb, 0, mask=(name == "A" and g < 4 and kb < 16)) for kb in kbs],
                                    "acc": accs[e], "vsel": vsels[e]} for e in range(2)]
                        hk = list(pending)
                        del pending[:]
                        self.attn_multi(streams, n, scale, hooks=hk)
                        for e in range(2):
                            add_ap = None
                            if name == "A":
                                add_ap = self.SK[64:65, l * 4 + heads[e]:l * 4 + heads[e] + 1]
                            ti = self.fin1(accs[e], n, add_ap=add_ap)
                            pending.append(lambda ti=ti, n=n, yd=ydsts[e], yr=yress[e]: self.fin2_simple(ti, n, yd, yr))
                    elif name == "C":
                        run_pending()
                        for e in range(2):
                            pr = slice(64 * e, 64 * e + 64)
                            accs = [self.pbank("acc"), self.pbank("acc")]
                            self.attn_c(accs, e, g, pr, vsels[e])
                            self.finish_c(accs, n, ydsts[e], yress[e])
                    else:
                        accs = [self.pbank("acc") for _ in range(4)]
                        streams = [{"items": [mk(e, kb, c) for kb in kbs], "acc": accs[c * 2 + e], "vsel": vsels[e]}
                                   for c in range(2) for e in range(2)]
                        hk = list(pending)
                        del pending[:]
                        self.attn_multi(streams, n, scale, hooks=hk)
                        for e in range(2):
                            i0 = self.fin1(accs[e], n)
                            i1 = self.fin1(accs[2 + e], n, mul_ap=self.lams[64:65, l:l + 1])
                            pending.append(lambda i0=i0, i1=i1, n=n: self.fin2_diff_a(i0, i1, n))
                            pending.append(lambda i0=i0, i1=i1, n=n, yd=ydsts[e], yr=yress[e]:
                                           self.fin2_diff_b(l, i0, i1, n, yd, yr))
                run_pending()
                if mix == 0 and a == 0 and l == 0:
                    self.dump(0, self.QT[:, 0:2048], [])
                    self.dump(1, self.KT[:, 0, 0:2048], [])
                    self.dump(2, self.BIG[:, 3 * T:3 * T + 2048], [])
                    self.dump(3, self.Y[0:64, 0, 0:2048], [], np_=64)
                    self.dump(4, self.Y[0:64, 1, 0:2048], [], np_=64)
                    self.dump(5, self.H[:, 0, 0:2048], [])
                self.out_proj(l, [mix * 256 + h * 64 for h in heads], qtgs)

    def finish_diff(self, l, acc0, acc1, n, ydst, yres):
        p = self.p
        bc0 = self.recip_bcast(acc0, n)
        i0 = self.tmp(0)
        t0_ = self.TMP[i0][0:64, 0:n]
        p.op("act", lambda e: e.copy(out=t0_, in_=self.PS[acc0][0:64, 0:n]), reads=[self.PSR[acc0]],
             writes=[self.TMPR[i0]])
        p.op("dve", lambda e: e.tensor_tensor(out=t0_, in0=t0_, in1=self.PS[bc0][0:64, 0:n], op=ALU.mult),
             reads=[self.TMPR[i0], self.PSR[bc0]], writes=[self.TMPR[i0]])
        bc1 = self.recip_bcast(acc1, n, mul_ap=self.lams[64:65, l:l + 1])
        i1 = self.tmp(1)
        t1_ = self.TMP[i1][0:64, 0:n]
        p.op("act", lambda e: e.copy(out=t1_, in_=self.PS[acc1][0:64, 0:n]), reads=[self.PSR[acc1]],
             writes=[self.TMPR[i1]])
        p.op("dve", lambda e: e.tensor_tensor(out=t1_, in0=t1_, in1=self.PS[bc1][0:64, 0:n], op=ALU.mult),
             reads=[self.TMPR[i1], self.PSR[bc1]], writes=[self.TMPR[i1]])
        p.op("pool", lambda e: e.tensor_tensor(out=t0_, in0=t0_, in1=t1_, op=ALU.add),
             reads=[self.TMPR[i0], self.TMPR[i1]], writes=[self.TMPR[i0]])
        p.op("pool", lambda e: e.tensor_tensor(out=t1_, in0=t0_, in1=t0_, op=ALU.mult),
             reads=[self.TMPR[i0], self.TMPR[i1]], writes=[self.TMPR[i1]])
        bs = self.pbank("st")
        p.op("pe", lambda e: e.matmul(self.PS[bs][0:64, 0:n], self.onesf[0:64, 0:64], t1_, start=True, stop=True),
             reads=[self.TMPR[i1], self.constR], writes=[self.PSR[bs]])
        p.op("act", lambda e: e.activation(out=t1_, in_=self.PS[bs][0:64, 0:n], func=AF.Sqrt, scale=1.0 / 64,
                                           bias=self.epsc[0:64, 0:1]),
             reads=[self.PSR[bs], self.constR], writes=[self.TMPR[i1]])
        p.op("dve", lambda e: e.reciprocal(out=t1_, in_=t1_), reads=[self.TMPR[i1]], writes=[self.TMPR[i1]])
        p.op("dve", lambda e: e.tensor_tensor(out=t0_, in0=t0_, in1=t1_, op=ALU.mult),
             reads=[self.TMPR[i0], self.TMPR[i1]], writes=[self.TMPR[i0]])
        p.op("act", lambda e: e.activation(out=ydst, in_=t0_, func=AF.Identity, scale=self.subg[:, l:l + 1]),
             reads=[self.TMPR[i0], self.constR], writes=yres)

    def build_tc(self, l, a):
        p = self.p
        p.dma("sp", lambda e: e.dma_start(out=self.STG[0][:, 0:960], in_=self.relT_d[l, a]), self.stg_d[0],
              writes=[self.STGR[0]])
        p.dma("sp", lambda e: e.dma_start(out=self.STG[1][:, 0:960], in_=self.cmask_d), self.stg_d[1],
              writes=[self.STGR[1]])
        p.op("dve", lambda e: e.scalar_tensor_tensor(out=self.Tc, in0=self.STG[0][:, 0:960], scalar=8.0,
                                                     in1=self.STG[1][:, 0:960], op0=ALU.mult, op1=ALU.add),
             reads=[self.STGR[0], self.STGR[1]], writes=[self.CSR[0]])

    def attn_c(self, accs, e, g, pr, vsel):
        p = self.p
        t0, n = TGS[g]
        scale = 0.125
        acc = accs[0]
        sts = []
        for kb in (16, 17):
            st = self.pbank("st")
            self.mm_group(self.PS[st][:, 0:n], [(self.KT[pr, 0, kb * P:(kb + 1) * P], self.QT[pr, t0:t0 + n])],
                          reads=[self.KR[0][4], self.QR[g][e]], writes=[self.PSR[st]])
            pi = self.next_pt()
            pt = self.PT[pi][:, 0:n]
            p.op("act", lambda e_, pt=pt, st=st: e_.activation(out=pt, in_=self.PS[st][:, 0:n], func=AF.Exp, scale=scale),
                 reads=[self.PSR[st]], writes=[self.PTR[pi]])
            sts.append((pi, kb))
        for i, (pi, kb) in enumerate(sts):
            lhsT = vsel(kb)
            rhs = self.PT[pi][:, 0:n]
            p.op("pe", lambda e_, lhsT=lhsT, rhs=rhs, i=i: e_.matmul(self.PS[acc][0:65, 0:n], lhsT, rhs, start=(i == 0),
                                                                     stop=False, skip_group_check=True),
                 reads=[self.PTR[pi], self.VR[kb]], writes=[self.PSR[acc]])
        Tcv = self.Tc[pr, :].rearrange("q (b c) -> q b c", c=64)
        pend = []

        def issue_row(rr):
            r = 8 * g + rr
            rs_ = min(max(r - 4, 0), 24)
            st = self.pbank("st")
            stv = self.PS[st]

            def fn(e_, r=r, rs_=rs_, stv=stv):
                ins = None
                first = {0: True, 1: True}
                for j in range(8):
                    rk = rs_ + j
                    par = rk % 2
                    ins = e_.matmul(stv[par * 64:(par + 1) * 64, j * 64:(j + 1) * 64],
                                    self.KT[pr, 0, rk * 64:(rk + 1) * 64], self.QT[pr, r * 64:(r + 1) * 64],
                                    start=first[par], stop=False, skip_group_check=True)
                    first[par] = False
                for par in range(2):
                    j0 = (par - rs_) % 2
                    b0 = rs_ + j0 - r + 7
                    ov = stv[par * 64:(par + 1) * 64, :].rearrange("q (j c) -> q j c", c=64)[:, j0:8:2, :]
                    ins = e_.matmul(ov, self.identb[pr, pr], Tcv[:, b0:b0 + 7:2, :], start=False, stop=True,
                                    skip_group_check=True)
                return ins

            kg = sorted(set(min((rs_ + j) // 8, 3) for j in range(8)))
            p.op("pe", fn, reads=[self.KR[0][k] for k in kg] + [self.QR[g][e], self.CSR[0], self.constR],
                 writes=[self.PSR[st]])
            pi = self.next_pt()
            pt = self.PT[pi][:, :]
            p.op("act", lambda e_, pt=pt, stv=stv: e_.activation(out=pt, in_=stv[:, :], func=AF.Exp, scale=scale),
                 reads=[self.PSR[st]], writes=[self.PTR[pi]])
            return (pi, rr, rs_)

        LA = 2
        for rr in range(min(LA, 8)):
            pend.append(issue_row(rr))
        for rr in range(8):
            if rr + LA < 8:
                pend.append(issue_row(rr + LA))
            pi, rr_, rs_ = pend.pop(0)

            def fn2(e_, pi=pi, rr_=rr_, rs_=rs_):
                ins = None
                for j in range(8):
                    rk = rs_ + j
                    par = rk % 2
                    ins = e_.matmul(self.PS[accs[par]][0:65, rr_ * 64:(rr_ + 1) * 64],
                                    self.V[par * 64:(par + 1) * 64, rk // 2, e * 65:e * 65 + 65],
                                    self.PT[pi][par * 64:(par + 1) * 64, j * 64:(j + 1) * 64],
                                    start=(par == 1 and rr_ == 0 and j < 2), stop=(rr_ == 7 and j >= 6),
                                    skip_group_check=True)
                return ins

            vb = sorted(set((rs_ + j) // 2 for j in range(8)))
            p.op("pe", fn2, reads=[self.PTR[pi]] + [self.VR[b] for b in vb],
                 writes=[self.PSR[accs[0]], self.PSR[accs[1]]])


_NC_CACHE = {}


def _get_nc(stage, debug=False, skipmix=""):
    if (stage, debug, skipmix) not in _NC_CACHE:
        _NC_CACHE[(stage, debug, skipmix)] = Builder(stage, debug, skipmix).build()
    return _NC_CACHE[(stage, debug, skipmix)]


_HC = {}


def _rope_tables(dim):
    t = np.arange(S, dtype=np.int32)
    row = (t // 64).astype(np.float32)
    col = (t % 64).astype(np.float32)
    half = dim // 2
    freqs = (np.float32(10000.0) ** (-np.arange(0, half, 2, dtype=np.float32) / np.float32(half))).astype(np.float32)
    ang_r = row[:, None] * freqs[None, :]
    ang_c = col[:, None] * freqs[None, :]
    ang = np.concatenate([ang_r, ang_r, ang_c, ang_c], axis=-1).astype(np.float32)
    cos = np.cos(ang).astype(np.float32)
    sin = np.sin(ang).astype(np.float32)
    qd = half // 2
    sign = np.where((np.arange(dim) % half) < qd, -1.0, 1.0).astype(np.float32)
    tab = np.zeros((P, 2, T), np.float32)
    reps = P // dim
    tab[:, 0, :S] = np.tile(cos.T, (reps, 1))
    tab[:, 1, :S] = np.tile((sin * sign[None, :]).T, (reps, 1))
    tab[:, 0, S:] = 1.0
    return tab


def _host_consts():
    if _HC:
        return _HC
    _HC["cs64"] = _rope_tables(64)
    _HC["cs32"] = _rope_tables(32)
    NEG = -30000.0
    kk = np.arange(P)[:, None]; qq = np.arange(P)[None, :]
    Mb = np.full((P, P), NEG, np.float32)
    Ub = np.where(kk <= qq, 0.0, NEG).astype(np.float32)
    Lb = np.where(kk >= qq, 0.0, NEG).astype(np.float32)
    Zb = np.zeros((P, P), np.float32)
    _HC["strip"] = np.ascontiguousarray(np.concatenate([Mb, Mb, Mb, Ub, Zb, Lb, Mb, Mb, Mb], axis=1))
    col = np.arange(64)
    c_start = np.clip(col - 8, 0, 48)
    ok = (col[:, None] >= c_start[None, :]) & (col[:, None] < c_start[None, :] + 16)
    cm = np.where(ok, 0.0, NEG).astype(np.float32)
    _HC["cmask"] = np.ascontiguousarray(np.tile(cm, (2, 15)))
    pm = np.zeros((P, 2), np.float32)
    pm[:, 0] = ((np.arange(P) % 64) < 32)
    pm[:, 1] = ((np.arange(P) % 64) >= 32)
    _HC["pmask"] = pm
    return _HC


def kernel(stage=99, debug=False, skipmix="", **inputs):
    nc = _get_nc(stage, debug, skipmix)
    f = lambda a: np.ascontiguousarray(np.asarray(a), dtype=np.float32)
    x = f(inputs["x"]); ctx = f(inputs["ctx"]); c = f(inputs["c"]); c_ctx = f(inputs["c_ctx"])
    common = {
        "final_norm_g": f(inputs["final_norm_g"]).reshape(1, D),
        "identf": np.eye(P, dtype=np.float32),
        "w_ada": f(inputs["w_ada"]),
        "b_adaT": np.ascontiguousarray(f(inputs["b_ada"]).reshape(2, 72, P).transpose(0, 2, 1)),
    }
    for n in ("w_ffn1_gate", "w_ffn1_up", "w_ffn1_down", "w_ffn2_gate", "w_ffn2_up", "w_ffn2_down", "w_in", "w_out",
              "sink_logit"):
        common[n] = f(inputs[n])
    common.update(_host_consts())
    perm64 = np.array([(d + 16) if (d % 32) < 16 else (d - 16) for d in range(64)])
    qg = f(inputs["q_norm_g"]); kg = f(inputs["k_norm_g"])
    qkg = np.stack([np.tile(qg, (1, 2)), np.tile(qg[:, perm64], (1, 2)),
                    np.tile(kg, (1, 2)), np.tile(kg[:, perm64], (1, 2))], axis=-1)
    common["qkg"] = np.ascontiguousarray(qkg)
    common["lamv"] = np.ascontiguousarray(np.stack([f(inputs["lam_q1"]), f(inputs["lam_k1"]), f(inputs["lam_q2"]),
                                                    f(inputs["lam_k2"])], axis=1))
    common["subg"] = np.ascontiguousarray(f(inputs["subln_g"]).T)
    rel = f(inputs["rel_pos_bias"])
    ck = np.arange(64)[:, None]; cq = np.arange(64)[None, :]
    dc = np.clip(ck - cq + 15, 0, 30)
    relT = rel[:, :, :, dc]
    relT = relT.transpose(0, 1, 3, 2, 4).reshape(2, 2, P, 960)
    common["relT"] = np.ascontiguousarray(relT)
    in_maps = []
    for b in range(8):
        cc = np.stack([c[b].reshape(KC, P).T, c_ctx.reshape(KC, P).T], axis=-1).reshape(P, 16)
        m = dict(common)
        m.update({"x": x[b], "ctx": ctx[b], "cc": np.ascontiguousarray(cc)})
        in_maps.append(m)
    res = run_bass_kernel_spmd(nc, in_maps, core_ids=list(range(8)))
    if debug:
        return np.stack([r["out"] for r in res.results], axis=0), res.results[0]["dbg"]
    return np.stack([r["out"] for r in res.results], axis=0)
```

```python
import numpy as np
import concourse.bass as bass
import concourse.mybir as mybir
from concourse.bass_utils import run_bass_kernel_spmd
from contextlib import ExitStack

F32 = mybir.dt.float32
BF16 = mybir.dt.bfloat16
AF = mybir.ActivationFunctionType
ALU = mybir.AluOpType
AX = mybir.AxisListType

P = 128
D = 1024
KC = 8
S = 2048
L = 256
T = S + L
TGS = [(0, 512), (512, 512), (1024, 512), (1536, 512), (2048, 256)]
DFF = 2816
FC = 22
EPS = 1e-6
SAME_ENGINE_SYNC = True


class Res:
    __slots__ = ("name", "w", "r")

    def __init__(self, name=""):
        self.name = name
        self.w = None
        self.r = {}


class DSem:
    def __init__(self, sem, name):
        self.sem = sem
        self.count = 0
        self.name = name


class Prog:
    COMPUTE = ("pe", "act", "dve", "pool")
    ALL = ("pe", "act", "dve", "pool", "sp")

    def __init__(self, nc, es):
        self.nc = nc
        self.es = es
        self.streams = {e: [] for e in self.ALL}
        self.cnt = {e: 0 for e in self.COMPUTE}
        self.sem = {e: es.enter_context(nc.semaphore("c_" + e)) for e in self.COMPUTE}
        self.seen = {e: {} for e in self.ALL}
        self.dsems = []
        self.n_ins = 0

    def dsem(self, name):
        d = DSem(self.es.enter_context(self.nc.semaphore("d_" + name)), name)
        self.dsems.append(d)
        return d

    def _collect(self, eng, reads, writes):
        deps = {}

        def add(tok):
            if tok is None:
                return
            k, v = tok
            if deps.get(k, 0) < v:
                deps[k] = v

        for r in reads:
            add(r.w)
        for w in writes:
            add(w.w)
            for k, v in w.r.items():
                add((k, v))
        waits = []
        for k, v in deps.items():
            if k == eng and (eng == "pe" or not SAME_ENGINE_SYNC):
                continue
            if self.seen[eng].get(k, 0) >= v:
                continue
            self.seen[eng][k] = v
            waits.append((k, v))
        return waits

    def _update(self, tok, reads, writes):
        k, v = tok
        for r in reads:
            if r.r.get(k, 0) < v:
                r.r[k] = v
        for w in writes:
            w.w = tok
            w.r = {}

    def op(self, eng, fn, reads=(), writes=()):
        waits = self._collect(eng, reads, writes)
        self.cnt[eng] += 1
        tok = (eng, self.cnt[eng])
        self.streams[eng].append((waits, fn, (self.sem[eng], 1)))
        self._update(tok, reads, writes)
        return tok

    def dma(self, q, fn, dsem, reads=(), writes=()):
        waits = self._collect(q, reads, writes)
        dsem.count += 16
        tok = (dsem, dsem.count)
        self.streams[q].append((waits, fn, (dsem.sem, 16)))
        self._update(tok, reads, writes)
        return tok

    def wait_all(self, eng):
        waits = []
        for e in self.COMPUTE:
            if e != eng and self.cnt[e] > self.seen[eng].get(e, 0):
                waits.append((e, self.cnt[e]))
                self.seen[eng][e] = self.cnt[e]
        for d in self.dsems:
            if d.count > self.seen[eng].get(d, 0):
                waits.append((d, d.count))
                self.seen[eng][d] = d.count
        self.streams[eng].append((waits, None, None))

    def barrier(self):
        for e in self.ALL:
            self.wait_all(e)

    def emit(self):
        nc = self.nc
        handles = {"pe": "tensor", "act": "scalar", "dve": "vector", "pool": "gpsimd", "sp": "sync"}
        with nc.Block() as block:
            for e in self.ALL:
                recs = self.streams[e]

                def body(eng, recs=recs):
                    for waits, fn, inc in recs:
                        for k, v in waits:
                            s = self.sem[k] if isinstance(k, str) else k.sem
                            eng.wait_ge(s, v)
                        if fn is not None:
                            ins = fn(eng)
                            ins.then_inc(inc[0], inc[1])

                getattr(block, handles[e])(body)


class Builder:
    def __init__(self, stage, debug=False, skipmix=""):
        self.stage = stage
        self.debug = debug
        self.skipmix = skipmix
        self.nc = bass.Bass("TRN2", target_bir_lowering=False)
        self.es = ExitStack()

    def dram_in(self, name, shape, dt=F32):
        return self.nc.dram_tensor(name, list(shape), dt, kind="ExternalInput").ap()

    def sb(self, name, shape, dt):
        return self.es.enter_context(self.nc.sbuf_tensor("sb_" + name, list(shape), dt))

    def build(self):
        nc = self.nc
        with self.es:
            self.p = Prog(nc, self.es)
            self._build()
            self.p.emit()
        return nc

    def _build(self):
        nc, p = self.nc, self.p
        st = self.stage
        self.x_d = self.dram_in("x", [S, D])
        self.ctx_d = self.dram_in("ctx", [L, D])
        self.fng_d = self.dram_in("final_norm_g", [1, D])
        self.identf_d = self.dram_in("identf", [P, P])
        self.cc_d = self.dram_in("cc", [P, 16])
        self.wada_d = self.dram_in("w_ada", [2, D, 9 * D])
        self.bada_d = self.dram_in("b_adaT", [2, P, 72])
        self.wg_d = [self.dram_in("w_ffn1_gate", [2, D, DFF]), self.dram_in("w_ffn2_gate", [2, D, DFF])]
        self.wu_d = [self.dram_in("w_ffn1_up", [2, D, DFF]), self.dram_in("w_ffn2_up", [2, D, DFF])]
        self.wd_d = [self.dram_in("w_ffn1_down", [2, DFF, D]), self.dram_in("w_ffn2_down", [2, DFF, D])]
        self.win_d = self.dram_in("w_in", [2, D, 2560])
        self.wout_d = self.dram_in("w_out", [2, D, D])
        self.cs_d = {64: self.dram_in("cs64", [P, 2, T]), 32: self.dram_in("cs32", [P, 2, T])}
        self.strip_d = self.dram_in("strip", [P, 1152])
        self.sink_d = self.dram_in("sink_logit", [2, 4])
        self.qkg_d = self.dram_in("qkg", [2, P, 4])
        self.lamv_d = self.dram_in("lamv", [2, 4, 32])
        self.subg_d = self.dram_in("subg", [64, 2])
        self.relT_d = self.dram_in("relT", [2, 2, P, 960])
        self.cmask_d = self.dram_in("cmask", [P, 960])
        self.pmask_d = self.dram_in("pmask", [P, 2])
        self.out_d = nc.dram_tensor("out", [S, D], F32, kind="ExternalOutput").ap()
        self.dbg_d = nc.dram_tensor("dbg", [P, 6, 2048], F32, kind="ExternalOutput").ap() if self.debug else None

        self.X = self.sb("X", [P, KC, T], F32)
        self.XR = [[Res(f"X{c}_{g}") for g in range(5)] for c in range(KC)]
        self.H = self.sb("H", [P, KC, T], BF16)
        self.HR = [[Res(f"H{c}_{g}") for g in range(5)] for c in range(KC)]
        self.BIG = self.sb("BIG", [P, 5 * T + 2340], BF16)
        self.STG = [self.sb(f"stg{i}", [P, 2048], F32) for i in range(2)]
        self.STGR = [Res(f"stg{i}") for i in range(2)]
        self.stg_d = [p.dsem(f"stg{i}") for i in range(2)]
        self.stg_rr = 0
        self.stgb_d = [p.dsem(f"stgb{i}") for i in range(2)]
        NWA, NWB = 4, 4
        self.WA = [self.sb(f"wa{i}", [P, 8, 256], BF16) for i in range(NWA)]
        self.WAR = [Res(f"wa{i}") for i in range(NWA)]
        self.wa_d = [p.dsem(f"wa{i}") for i in range(NWA)]
        self.wa_rr = 0
        self.WB = [self.sb(f"wb{i}", [P, 1024], BF16) for i in range(NWB)]
        self.WBR = [Res(f"wb{i}") for i in range(NWB)]
        self.wb_d = [p.dsem(f"wb{i}") for i in range(NWB)]
        self.wb_rr = 0
        NTMP = 6
        self.TMP = [self.sb(f"tmp{i}", [P, 512], F32) for i in range(NTMP)]
        self.TMPR = [Res(f"tmp{i}") for i in range(NTMP)]
        self.tmp_rr = [0, 0, 0]
        self.tmp_any_rr = 0
        self.identf = self.sb("identf", [P, P], F32)
        self.onesf = self.sb("onesf", [P, P], F32)
        self.small = self.sb("small", [P, 64], F32)
        self.SMR = [Res(f"sm{i}") for i in range(64)]
        self.constR = Res("const")
        self.epsc = self.sb("epsc", [P, 1], F32)
        self.cc = self.sb("cc", [P, 16], F32)
        self.sT = self.sb("sT", [P, 16], BF16)
        self.sTR = Res("sT")
        self.bada = self.sb("bada", [P, 2, 72], F32)
        self.MOD = self.sb("MOD", [P, 4 * 72], F32)
        self.MODR = [Res(f"mod{i}") for i in range(4)]
        self.QT = self.BIG[:, 0:T]
        self.KT = self.BIG[:, T:3 * T].rearrange("q (c t) -> q c t", c=2)
        self.V = self.BIG[:, 3 * T:3 * T + 2340].rearrange("q (b c) -> q b c", c=130)
        self.Y = self.BIG[:, 3 * T + 2340:5 * T + 2340].rearrange("q (h t) -> q h t", h=2)
        self.PT = [self.sb(f"pt{i}", [P, 512], BF16) for i in range(6)]
        self.PTR = [Res(f"pt{i}") for i in range(6)]
        self.pt_rr = 0
        self.CS = [self.sb(f"cs{i}", [P, 2, 512], F32) for i in range(2)]
        self.CSR = [Res(f"cs{i}") for i in range(2)]
        self.cs_ds = [p.dsem(f"cs{i}") for i in range(2)]
        self.cs_rr = 0
        self.Tc = self.CS[0][:, :, :].rearrange("q a n -> q (a n)").bitcast(BF16)[:, 0:960]
        self.identb = self.sb("identb", [P, P], BF16)
        self.onesb = self.sb("onesb", [P, 64], BF16)
        self.blockones = self.sb("blockones", [P, P], F32)
        self.strip = self.sb("strip", [P, 1152], BF16)
        yoff = 3 * T + 2340
        self.SK = self.sb("SK", [P, 8], F32)
        self.qkg = self.sb("qkg", [P, 2, 4], F32)
        self.lams = self.sb("lams", [P, 16], F32)
        self.LAMR = Res("lam")
        self.subg = self.sb("subg", [64, 2], F32)
        self.pmask = self.sb("pmask", [P, 2], F32)
        self.REC = [self.BIG[:, yoff:yoff + 1024].bitcast(F32)]
        self.RB = [self.BIG[64:65, yoff + 1024 + j * 512:yoff + 1024 + (j + 1) * 512] for j in range(7)]
        self.RBR = [Res(f"rb{j}") for j in range(7)]
        self.rb_rr = 0
        self.rb_of = {}
        self.RECR = [Res(f"rec{i}") for i in range(1)]
        self.rec_rr = 0
        self.PS = [self.es.enter_context(nc.psum_tensor(f"ps{i}", [P, 512], F32)) for i in range(8)]
        self.pool_rr = {"st": 0, "acc": 0}
        self.pools = {"st": (0, 1, 2, 3), "acc": (4, 5, 6, 7)}
        self.PSR = [Res(f"ps{i}") for i in range(8)]
        self.ps_rr = 0
        self.bg = []
        self.evac_rr = 0
        self.cd = p.dsem("const")
        self.out_ds = p.dsem("out")

        p.op("pool", lambda e: e.memset(self.epsc[:], EPS), writes=[self.constR])
        p.op("pool", lambda e: e.memset(self.onesf[:], 1.0), writes=[self.constR])
        self.cdma(self.identf[:], self.identf_d)
        self.cdma(self.cc[:], self.cc_d)
        self.cdma(self.bada[:], self.bada_d.rearrange("l p j -> p l j"))
        self.cdma(self.qkg[:], self.qkg_d.rearrange("l p j -> p l j"))
        self.cdma(self.pmask[:], self.pmask_d)
        self.cdma(self.SK[64:65, 0:8], self.sink_d.rearrange("(o l) h -> o (l h)", o=1))
        self.cdma(self.subg[:], self.subg_d)
        p.dma("pool", lambda e: e.dma_start(out=self.strip[:], in_=self.strip_d), p.dsem("strip"), writes=[self.constR])
        p.op("pool", lambda e: e.memset(self.onesb[:], 1.0), writes=[self.constR])
        p.op("pool", lambda e: e.memset(self.blockones[:], 0.0), writes=[self.constR])
        p.op("pool", lambda e: e.memset(self.blockones[0:64, 0:64], 1.0), writes=[self.constR])
        p.op("pool", lambda e: e.memset(self.blockones[64:128, 64:128], 1.0), writes=[self.constR])
        p.op("act", lambda e: e.copy(out=self.identb[:], in_=self.identf[:]), reads=[self.constR], writes=[self.constR])
        p.op("act", lambda e: e.activation(out=self.SK[64:65, 0:8], in_=self.SK[64:65, 0:8], func=AF.Exp),
             reads=[self.constR], writes=[self.constR])
        for l in range(2):
            li = 1.0 - (0.8 - 0.6 * float(np.exp(-0.3 * l)))
            p.op("dve", lambda e, l=l, li=li: e.tensor_scalar_mul(out=self.subg[:, l:l + 1], in0=self.subg[:, l:l + 1],
                                                                   scalar1=li), reads=[self.constR], writes=[self.constR])
        p.op("act", lambda e: e.activation(out=self.sT[:], in_=self.cc[:], func=AF.Silu),
             reads=[self.constR], writes=[self.sTR])

        self.load_x()
        k = 0
        for l in range(2):
            if st >= k + 1:
                if l == 0:
                    for f_ in self.adaln_steps(0, list(range(0, 6)), [0, 1, 2]):
                        f_()
                    self.bg = self.adaln_steps(0, list(range(6, 18)), [3, 4, 5, 6, 7, 8])
                self.norm_mod(l, 0, 1)
                self.ffn(l, 0, 2, ctx=True)
            if st >= k + 2:
                p.barrier()
                self.norm_mod(l, 3, 4)
                self.lam_setup(l)
                self.mixers(l, min(4, st - k - 1))
                p.barrier()
            if st >= k + 6:
                self.norm_mod(l, 6, 7, tgs=range(5) if l == 0 else range(4))
                if l == 0 and st >= 7:
                    self.bg = self.adaln_steps(1, list(range(18)), list(range(9)))
                self.ffn(l, 1, 8, ctx=(l == 0))
            k += 6
        self.final_norm()
        p.barrier()

    def dump(self, slot, ap, reads, np_=P, w=2048):
        if not self.debug:
            return
        p = self.p
        p.barrier()
        p.op("act", lambda e: e.copy(out=self.STG[0][0:np_, 0:w], in_=ap), reads=reads, writes=[self.STGR[0]])
        p.dma("sp", lambda e: e.dma_start(out=self.dbg_d[0:np_, slot, 0:w], in_=self.STG[0][0:np_, 0:w]), self.out_ds,
              reads=[self.STGR[0]])
        p.barrier()

    def cdma(self, dst, src):
        self.p.dma("sp", lambda e: e.dma_start(out=dst, in_=src), self.cd, writes=[self.constR])

    def bank(self):
        b = self.ps_rr
        self.ps_rr = (self.ps_rr + 1) % 7
        return b

    def bg_step(self):
        if self.bg:
            self.bg.pop(0)()

    def bg_flush(self):
        while self.bg:
            self.bg.pop(0)()

    def pbank(self, pool):
        lst = self.pools[pool]
        self.pool_rr[pool] = (self.pool_rr[pool] + 1) % len(lst)
        return lst[self.pool_rr[pool]]

    def tmp(self, role):
        i = role * 2 + self.tmp_rr[role]
        self.tmp_rr[role] ^= 1
        return i

    def modap(self, l, stream, r, c):
        col = (l * 2 + stream) * 72 + r * 8 + c
        return self.MOD[:, col:col + 1]

    def mm_group(self, out, pairs, reads, writes, **kw):
        n = len(pairs)

        def fn(e):
            ins = None
            for i, (lhsT, rhs) in enumerate(pairs):
                ins = e.matmul(out, lhsT, rhs, start=(i == 0), stop=(i == n - 1), **kw)
            return ins

        return self.p.op("pe", fn, reads=reads, writes=writes)

    def load_x(self):
        p = self.p
        for g, (t0, n) in enumerate(TGS):
            ntile = n // P
            for s in range(ntile // 2):
                if g < 4:
                    src = self.x_d[t0 + s * 256: t0 + s * 256 + 256, :]
                else:
                    src = self.ctx_d[s * 256: s * 256 + 256, :]
                src = src.rearrange("(j p) d -> p j d", p=P)
                dst = self.STG[s][:, :].rearrange("p (j d) -> p j d", j=2)
                p.dma("sp", lambda e, dst=dst, src=src: e.dma_start(out=dst, in_=src), self.stg_d[s],
                      writes=[self.STGR[s]])
            for c in range(KC):
                b = self.bank()

                def tr(e, b=b, c=c, ntile=ntile):
                    ins = None
                    for j in range(ntile):
                        src = self.STG[j // 2][:, (j % 2) * 1024 + c * P:(j % 2) * 1024 + (c + 1) * P]
                        ins = e.transpose(out=self.PS[b][:, j * P:(j + 1) * P], in_=src, identity=self.identf[:])
                    return ins

                p.op("pe", tr, reads=[self.STGR[s] for s in range(ntile // 2)] + [self.constR], writes=[self.PSR[b]])
                self.evac_copy(self.X[:, c, t0:t0 + n], self.PS[b][:, 0:n], [self.PSR[b]], [self.XR[c][g]])

    def evac_copy(self, dst, src, reads, writes):
        p = self.p
        self.evac_rr ^= 1
        if self.evac_rr:
            p.op("act", lambda e: e.copy(out=dst, in_=src), reads=reads, writes=writes)
        else:
            p.op("dve", lambda e: e.tensor_copy(out=dst, in_=src), reads=reads, writes=writes)

    def final_norm(self):
        p = self.p
        self.gbc = self.WA[0][:, :, :].rearrange("q k n -> q (k n)").bitcast(F32)
        p.dma("sp", lambda e: e.dma_start(out=self.gbc, in_=self.fng_d.partition_broadcast(P)), self.cd,
              writes=[self.WAR[0]])
        for tt in range(S // P):
            g = tt // 4
            s = tt % 2
            stg = self.STG[s]
            sr = self.STGR[s]
            for h in range(2):
                b = self.bank()

                def tr(e, b=b, h=h, tt=tt):
                    ins = None
                    for j in range(4):
                        c = h * 4 + j
                        ins = e.transpose(out=self.PS[b][:, j * P:(j + 1) * P],
                                          in_=self.X[:, c, tt * P:(tt + 1) * P], identity=self.identf[:])
                    return ins

                p.op("pe", tr, reads=[self.XR[h * 4 + j][g] for j in range(4)] + [self.constR],
                     writes=[self.PSR[b]])
                dst = stg[:, h * 512:(h + 1) * 512]
                src = self.PS[b][:, :]
                if h == 0:
                    p.op("act", lambda e, dst=dst, src=src: e.copy(out=dst, in_=src), reads=[self.PSR[b]], writes=[sr])
                else:
                    p.op("dve", lambda e, dst=dst, src=src: e.tensor_copy(out=dst, in_=src), reads=[self.PSR[b]],
                         writes=[sr])
            ss = self.small[:, 2 * s:2 * s + 1]
            rs = self.small[:, 2 * s + 1:2 * s + 2]
            smr = self.SMR[s]
            p.op("act", lambda e, stg=stg, ss=ss: e.activation(out=stg[:, 1024:2048], in_=stg[:, 0:1024],
                                                              func=AF.Square, accum_out=ss),
                 reads=[sr], writes=[sr, smr])
            p.op("act", lambda e, ss=ss, rs=rs: e.activation(out=rs, in_=ss, func=AF.Sqrt, scale=1.0 / D,
                                                             bias=self.epsc[:, 0:1]),
                 reads=[smr, self.constR], writes=[smr])
            p.op("dve", lambda e, rs=rs: e.reciprocal(out=rs, in_=rs), reads=[smr], writes=[smr])
            p.op("dve", lambda e, stg=stg, rs=rs: e.scalar_tensor_tensor(out=stg[:, 1024:2048], in0=stg[:, 0:1024],
                                                                        scalar=rs, in1=self.gbc,
                                                                        op0=ALU.mult, op1=ALU.mult),
                 reads=[sr, smr, self.WAR[0]], writes=[sr])
            p.dma("sp", lambda e, stg=stg, tt=tt: e.dma_start(out=self.out_d[tt * P:(tt + 1) * P, :],
                                                             in_=stg[:, 1024:2048]),
                  self.out_ds, reads=[sr])

    def adaln_steps(self, l, pieces, rows):
        p = self.p
        b = 7
        psr = self.PSR[b]
        slots = {}

        def issue(j4):
            s_ = self.stg_rr
            self.stg_rr ^= 1
            slots[j4] = s_
            src = self.wada_d[l, :, j4 * 512:(j4 + 1) * 512].rearrange("(kc q) n -> q kc n", q=P)
            stgb = self.STG[s_][:, :].bitcast(BF16)
            dst = stgb.rearrange("q (kc n) -> q kc n", kc=8)
            p.dma("pool", lambda e: e.dma_start(out=dst, in_=src), self.stgb_d[s_], writes=[self.STGR[s_]])

        def step(idx):
            j4 = pieces[idx]
            if idx == 0:
                issue(j4)
            if idx + 1 < len(pieces):
                issue(pieces[idx + 1])
            s_ = slots[j4]
            stgb = self.STG[s_][:, :].bitcast(BF16)
            for q4 in range(4):
                j = j4 * 4 + q4
                pairs = [(stgb[:, kc * 512 + q4 * 128: kc * 512 + q4 * 128 + 128],
                          self.sT[:, 2 * kc:2 * kc + 2]) for kc in range(8)]
                self.mm_group(self.PS[b][:, 2 * j:2 * j + 2], pairs, reads=[self.STGR[s_], self.sTR], writes=[psr])

        def final():
            c0, c1 = rows[0] * 8, (rows[-1] + 1) * 8
            pv = self.PS[b][:, 0:144].rearrange("q (j t) -> q j t", t=2)
            for stream in range(2):
                base = (l * 2 + stream) * 72
                mr = self.MODR[l * 2 + stream]
                dst = self.MOD[:, base + c0:base + c1]
                p.op("dve", lambda e, dst=dst, stream=stream: e.tensor_tensor(out=dst, in0=pv[:, c0:c1, stream],
                                                                              in1=self.bada[:, l, c0:c1], op=ALU.add),
                     reads=[psr, self.constR], writes=[mr])
                for r in (1, 4, 7):
                    if r in rows:
                        d2 = self.MOD[:, base + r * 8:base + r * 8 + 8]
                        p.op("dve", lambda e, d2=d2: e.tensor_scalar_add(out=d2, in0=d2, scalar1=1.0), reads=[mr],
                             writes=[mr])
                for r in (2, 8):
                    if r in rows:
                        d2 = self.MOD[:, base + r * 8:base + r * 8 + 8]
                        p.op("dve", lambda e, d2=d2: e.tensor_scalar_mul(out=d2, in0=d2, scalar1=0.5), reads=[mr],
                             writes=[mr])

        return [(lambda idx=idx: step(idx)) for idx in range(len(pieces))] + [final]

    def norm_mod(self, l, r_shift, r_scale, tgs=range(5)):
        p = self.p
        for g in tgs:
            t0, n = TGS[g]
            stream = 0 if g < 4 else 1
            mr = self.MODR[l * 2 + stream]
            b = self.bank()
            sqs = []
            for c in range(KC):
                ti = self.tmp(0)
                xs = self.X[:, c, t0:t0 + n]
                sq = self.TMP[ti][:, 0:n]
                p.op("pool", lambda e, sq=sq, xs=xs: e.tensor_tensor(out=sq, in0=xs, in1=xs, op=ALU.mult),
                     reads=[self.XR[c][g]], writes=[self.TMPR[ti]])
                p.op("pe", lambda e, sq=sq, c=c, b=b, n=n: e.matmul(self.PS[b][:, 0:n], self.onesf[:], sq,
                                                                    start=(c == 0), stop=(c == KC - 1)),
                     reads=[self.TMPR[ti], self.constR], writes=[self.PSR[b]])
            ri = self.tmp(1)
            rs = self.TMP[ri][:, 0:n]
            p.op("act", lambda e, rs=rs, b=b, n=n: e.activation(out=rs, in_=self.PS[b][:, 0:n], func=AF.Sqrt,
                                                                scale=1.0 / D, bias=self.epsc[:, 0:1]),
                 reads=[self.PSR[b], self.constR], writes=[self.TMPR[ri]])
            p.op("dve", lambda e, rs=rs: e.reciprocal(out=rs, in_=rs), reads=[self.TMPR[ri]], writes=[self.TMPR[ri]])
            for c in range(KC):
                ti = self.tmp(2)
                tt = self.TMP[ti][:, 0:n]
                xs = self.X[:, c, t0:t0 + n]
                p.op("dve", lambda e, tt=tt, xs=xs, rs=rs: e.tensor_tensor(out=tt, in0=xs, in1=rs, op=ALU.mult),
                     reads=[self.XR[c][g], self.TMPR[ri]], writes=[self.TMPR[ti]])
                hd = self.H[:, c, t0:t0 + n]
                sc = self.modap(l, stream, r_scale, c)
                sh = self.modap(l, stream, r_shift, c)
                p.op("act", lambda e, hd=hd, tt=tt, sc=sc, sh=sh: e.activation(out=hd, in_=tt, func=AF.Identity,
                                                                              scale=sc, bias=sh),
                     reads=[self.TMPR[ti], mr], writes=[self.HR[c][g]])

    def load_wa(self, src):
        i = self.wa_rr
        self.wa_rr = (self.wa_rr + 1) % len(self.WA)
        srcv = src.rearrange("(kc q) n -> q kc n", q=P)
        dst = self.WA[i][:, :, :]
        self.p.dma("pool", lambda e: e.dma_start(out=dst, in_=srcv), self.wa_d[i], writes=[self.WAR[i]])
        return i

    def load_wb(self, src):
        i = self.wb_rr
        self.wb_rr = (self.wb_rr + 1) % len(self.WB)
        dst = self.WB[i][:, :]
        self.p.dma("pool", lambda e: e.dma_start(out=dst, in_=src), self.wb_d[i], writes=[self.WBR[i]])
        return i

    def ffn(self, l, which, r_gate, ctx=True):
        p = self.p
        wg, wu, wd = self.wg_d[which][l], self.wu_d[which][l], self.wd_d[which][l]
        tgs = list(range(5)) if ctx else list(range(4))
        U = self.BIG[:, 0:4 * T].rearrange("q (f t) -> q f t", f=4)
        UR = [[Res(f"U{f}_{g}") for g in range(5)] for f in range(4)]
        npieces = FC // 2
        slabs = [list(range(i, min(i + 2, npieces))) for i in range(0, npieces, 2)]

        def issue_piece(j):
            return (self.load_wa(wg[:, j * 256:(j + 1) * 256]), self.load_wa(wu[:, j * 256:(j + 1) * 256]))

        def issue_down(slab):
            return [self.load_wb(wd[f * P:(f + 1) * P, :]) for j in slab for f in (2 * j, 2 * j + 1)]

        pend = {0: issue_piece(0)}
        pend_d = {0: issue_down(slabs[0])}
        for si, slab in enumerate(slabs):
            for j in slab:
                if j + 1 < npieces:
                    pend[j + 1] = issue_piece(j + 1)
                ig, iu = pend.pop(j)
                for half in range(2):
                    fl = (j - slab[0]) * 2 + half
                    self.bg_step()
                    for g in tgs:
                        t0, n = TGS[g]
                        hreads = [self.HR[kc][g] for kc in range(KC)]
                        bg = self.bank()
                        self.mm_group(self.PS[bg][:, 0:n],
                                      [(self.WA[ig][:, kc, half * P:(half + 1) * P], self.H[:, kc, t0:t0 + n])
                                       for kc in range(KC)], reads=hreads + [self.WAR[ig]], writes=[self.PSR[bg]])
                        bu = self.bank()
                        self.mm_group(self.PS[bu][:, 0:n],
                                      [(self.WA[iu][:, kc, half * P:(half + 1) * P], self.H[:, kc, t0:t0 + n])
                                       for kc in range(KC)], reads=hreads + [self.WAR[iu]], writes=[self.PSR[bu]])
                        ti = self.tmp(0)
                        sg = self.TMP[ti][:, 0:n]
                        p.op("act", lambda e, sg=sg, bg=bg, n=n: e.activation(out=sg, in_=self.PS[bg][:, 0:n],
                                                                              func=AF.Silu),
                             reads=[self.PSR[bg]], writes=[self.TMPR[ti]])
                        ud = U[:, fl, t0:t0 + n]
                        p.op("dve", lambda e, ud=ud, sg=sg, bu=bu, n=n: e.tensor_tensor(out=ud, in0=sg,
                                                                                        in1=self.PS[bu][:, 0:n],
                                                                                        op=ALU.mult),
                             reads=[self.TMPR[ti], self.PSR[bu]], writes=[UR[fl][g]])
            wbs = pend_d.pop(si)
            nfl = len(wbs)
            for g in tgs:
                t0, n = TGS[g]
                stream = 0 if g < 4 else 1
                mr = self.MODR[l * 2 + stream]
                for m in range(KC):
                    b = self.bank()
                    self.mm_group(self.PS[b][:, 0:n],
                                  [(self.WB[wbs[fl]][:, m * P:(m + 1) * P], U[:, fl, t0:t0 + n]) for fl in range(nfl)],
                                  reads=[UR[fl][g] for fl in range(nfl)] + [self.WBR[w] for w in wbs],
                                  writes=[self.PSR[b]])
                    xs = self.X[:, m, t0:t0 + n]
                    ga = self.modap(l, stream, r_gate, m)
                    p.op("dve", lambda e, xs=xs, b=b, n=n, ga=ga: e.scalar_tensor_tensor(
                        out=xs, in0=self.PS[b][:, 0:n], scalar=ga, in1=xs, op0=ALU.mult, op1=ALU.add),
                         reads=[self.PSR[b], mr, self.XR[m][g]], writes=[self.XR[m][g]])
            if si + 1 < len(slabs):
                pend_d[si + 1] = issue_down(slabs[si + 1])
        self.bg_flush()

    def lam_setup(self, l):
        p = self.p
        r = self.LAMR
        ti = self.tmp(0)
        lt = self.TMP[ti]
        sm = self.lams
        p.dma("sp", lambda e: e.dma_start(out=lt[0:1, 0:128], in_=self.lamv_d[l:l + 1].rearrange("o a d -> o (a d)")),
              self.cd, writes=[self.TMPR[ti], r])
        for t in range(2):
            a = lt[0:1, (2 * t) * 32:(2 * t) * 32 + 32]
            bb = lt[0:1, (2 * t + 1) * 32:(2 * t + 1) * 32 + 32]
            p.op("dve", lambda e, a=a, bb=bb: e.tensor_tensor(out=a, in0=a, in1=bb, op=ALU.mult),
                 reads=[self.constR, r], writes=[r, self.TMPR[ti]])
            p.op("dve", lambda e, a=a, t=t: e.reduce_sum(out=sm[0:1, 8 + t:9 + t], in_=a, axis=AX.X),
                 reads=[r, self.TMPR[ti]], writes=[r])
        p.op("act", lambda e: e.activation(out=sm[0:1, 8:10], in_=sm[0:1, 8:10], func=AF.Exp), reads=[r], writes=[r])
        lam_init = 0.8 - 0.6 * float(np.exp(-0.3 * l))
        p.op("dve", lambda e: e.tensor_tensor(out=sm[0:1, 10:11], in0=sm[0:1, 9:10], in1=sm[0:1, 8:9],
                                              op=ALU.subtract), reads=[r], writes=[r])
        p.op("dve", lambda e: e.tensor_scalar_add(out=sm[0:1, 10:11], in0=sm[0:1, 10:11], scalar1=-lam_init),
             reads=[r], writes=[r])
        b = self.pbank("st")
        p.op("pe", lambda e: e.matmul(self.PS[b][:, 0:1], self.onesf[0:1, :], sm[0:1, 10:11], start=True, stop=True),
             reads=[r, self.constR], writes=[self.PSR[b]])
        p.op("dve", lambda e: e.tensor_copy(out=sm[:, l:l + 1], in_=self.PS[b][:, 0:1]), reads=[self.PSR[b]],
             writes=[r])

    def load_cs(self, kind, g):
        t0, n = TGS[g]
        i = self.cs_rr
        self.cs_rr ^= 1
        src = self.cs_d[kind][:, :, t0:t0 + n]
        dst = self.CS[i][:, :, 0:n]
        self.p.dma("sp", lambda e: e.dma_start(out=dst, in_=src), self.cs_ds[i], writes=[self.CSR[i]])
        return i

    def perm_weights(self, wi, qd):
        wp = self.wa_rr
        self.wa_rr = (self.wa_rr + 1) % len(self.WA)
        sv = self.WA[wi][:, :, :].rearrange("q k (b t d) -> q k b t d", t=2, d=qd)
        dv = self.WA[wp][:, :, :].rearrange("q k (b t d) -> q k b t d", t=2, d=qd)
        self.p.op("pool", lambda e: e.tensor_copy(out=dv[:, :, :, 0, :], in_=sv[:, :, :, 1, :]),
                  reads=[self.WAR[wi]], writes=[self.WAR[wp]])
        self.p.op("pool", lambda e: e.tensor_copy(out=dv[:, :, :, 1, :], in_=sv[:, :, :, 0, :]),
                  reads=[self.WAR[wi]], writes=[self.WAR[wp]])
        return wp

    def proj_fm(self, l, wi, wpi, colsel, dsts, kind, tgs, norm_col=None, pad=False):
        p = self.p
        for g in tgs:
            t0, n = TGS[g]
            hreads = [self.HR[kc][g] for kc in range(KC)]
            bp = self.pbank("st")
            self.mm_parts(bp, n, colsel, wi, t0, hreads)
            if wpi is None:
                dst = dsts[0][0](g)
                self.evac_copy(dst, self.PS[bp][:, 0:n], [self.PSR[bp]], dsts[0][1](g))
                continue
            ci = self.load_cs(kind, g)
            cos = self.CS[ci][:, 0, 0:n]
            ssin = self.CS[ci][:, 1, 0:n]
            br = self.pbank("st")
            self.mm_parts(br, n, colsel, wpi, t0, hreads)
            i1 = self.tmp(0)
            i2 = self.tmp(1)
            t1 = self.TMP[i1][:, 0:n]
            t2 = self.TMP[i2][:, 0:n]
            pp = self.PS[bp][:, 0:n]
            pr = self.PS[br][:, 0:n]
            if norm_col is None:
                p.op("dve", lambda e, t1=t1, pp=pp, cos=cos: e.tensor_tensor(out=t1, in0=pp, in1=cos, op=ALU.mult),
                     reads=[self.PSR[bp], self.CSR[ci]], writes=[self.TMPR[i1]])
                p.op("dve", lambda e, t2=t2, pr=pr, ssin=ssin: e.tensor_tensor(out=t2, in0=pr, in1=ssin, op=ALU.mult),
                     reads=[self.PSR[br], self.CSR[ci]], writes=[self.TMPR[i2]])
                if not pad:
                    dst = dsts[0][0](g)
                    p.op("pool", lambda e, dst=dst, t1=t1, t2=t2: e.tensor_tensor(out=dst, in0=t1, in1=t2, op=ALU.add),
                         reads=[self.TMPR[i1], self.TMPR[i2]], writes=dsts[0][1](g))
                else:
                    p.op("pool", lambda e, t1=t1, t2=t2: e.tensor_tensor(out=t1, in0=t1, in1=t2, op=ALU.add),
                         reads=[self.TMPR[i1], self.TMPR[i2]], writes=[self.TMPR[i1]])
                    for k2 in range(2):
                        dst = dsts[k2][0](g)
                        m = self.pmask[:, k2:k2 + 1]
                        eng = "dve" if k2 == 0 else "pool"
                        p.op(eng, lambda e, dst=dst, t1=t1, m=m: e.tensor_scalar_mul(out=dst, in0=t1, scalar1=m),
                             reads=[self.TMPR[i1], self.constR], writes=dsts[k2][1](g))
            else:
                i3 = self.tmp(2)
                i4 = self.tmp(2)
                sq = self.TMP[i3][:, 0:n]
                rs = self.TMP[i4][:, 0:n]
                p.op("act", lambda e, sq=sq, pp=pp: e.activation(out=sq, in_=pp, func=AF.Square),
                     reads=[self.PSR[bp]], writes=[self.TMPR[i3]])
                bs = self.pbank("st")
                p.op("pe", lambda e, bs=bs, sq=sq, n=n: e.matmul(self.PS[bs][:, 0:n], self.blockones[:], sq,
                                                                 start=True, stop=True),
                     reads=[self.TMPR[i3], self.constR], writes=[self.PSR[bs]])
                p.op("act", lambda e, rs=rs, bs=bs, n=n: e.activation(out=rs, in_=self.PS[bs][:, 0:n], func=AF.Sqrt,
                                                                      scale=1.0 / 64, bias=self.epsc[:, 0:1]),
                     reads=[self.PSR[bs], self.constR], writes=[self.TMPR[i4]])
                p.op("dve", lambda e, rs=rs: e.reciprocal(out=rs, in_=rs), reads=[self.TMPR[i4]],
                     writes=[self.TMPR[i4]])
                g1 = self.qkg[:, l, norm_col:norm_col + 1]
                g2 = self.qkg[:, l, norm_col + 1:norm_col + 2]
                p.op("dve", lambda e, t1=t1, pp=pp, cos=cos, g1=g1: e.scalar_tensor_tensor(
                    out=t1, in0=pp, scalar=g1, in1=cos, op0=ALU.mult, op1=ALU.mult),
                     reads=[self.PSR[bp], self.CSR[ci], self.constR], writes=[self.TMPR[i1]])
                p.op("dve", lambda e, t2=t2, pr=pr, ssin=ssin, g2=g2: e.scalar_tensor_tensor(
                    out=t2, in0=pr, scalar=g2, in1=ssin, op0=ALU.mult, op1=ALU.mult),
                     reads=[self.PSR[br], self.CSR[ci], self.constR], writes=[self.TMPR[i2]])
                p.op("pool", lambda e, t1=t1, t2=t2: e.tensor_tensor(out=t1, in0=t1, in1=t2, op=ALU.add),
                     reads=[self.TMPR[i1], self.TMPR[i2]], writes=[self.TMPR[i1]])
                dst = dsts[0][0](g)
                p.op("dve", lambda e, dst=dst, t1=t1, rs=rs: e.tensor_tensor(out=dst, in0=t1, in1=rs, op=ALU.mult),
                     reads=[self.TMPR[i1], self.TMPR[i4]], writes=dsts[0][1](g))

    def mm_parts(self, b, n, colsel, slot, t0, hreads):
        parts = colsel(slot, 0)

        def fn(e):
            ins = None
            for pi_ in range(len(parts)):
                for kc in range(KC):
                    psl, lhsT = colsel(slot, kc)[pi_]
                    ins = e.matmul(self.PS[b][psl, 0:n], lhsT, self.H[:, kc, t0:t0 + n], start=(kc == 0),
                                   stop=(kc == KC - 1))
            return ins

        self.p.op("pe", fn, reads=hreads + [self.WAR[slot]], writes=[self.PSR[b]])

    def proj_v(self, wi, col0):
        p = self.p
        Vv = self.V.rearrange("q b (h c) -> q b h c", c=65)
        for b0 in range(0, 18, 4):
            nb = min(4, 18 - b0)
            bk = self.pbank("st")
            for j in range(nb):
                blk = b0 + j
                g = min(blk // 4, 4)
                self.mm_group(self.PS[bk][:, j * P:(j + 1) * P],
                              [(self.H[:, kc, blk * P:(blk + 1) * P], self.WA[wi][:, kc, col0:col0 + P])
                               for kc in range(KC)],
                              reads=[self.HR[kc][g] for kc in range(KC)] + [self.WAR[wi]], writes=[self.PSR[bk]])
            dst = Vv[:, b0:b0 + nb, :, 0:64]
            src = self.PS[bk][:, 0:nb * P].rearrange("q (b h c) -> q b h c", h=2, c=64)
            self.evac_copy(dst, src, [self.PSR[bk]], [self.VR[b0 + j] for j in range(nb)])

    def next_pt(self):
        i = self.pt_rr
        self.pt_rr = (self.pt_rr + 1) % len(self.PT)
        return i

    def attn_multi(self, streams, n, scale, hooks=None):
        p = self.p
        ns = len(streams)
        ni = len(streams[0]["items"])
        seq = []
        for i in range(ni):
            for sidx in range(ns):
                seq.append((sidx, i))
        LA = len(self.PT) - 1
        pend = []

        def issue_qk(sidx, i):
            it = streams[sidx]["items"][i]
            st = self.pbank("st")
            pairs = [(it["kT"], it["q"])]
            reads = list(it["reads"])
            if it.get("extra") is not None:
                pairs.append(it["extra"])
                reads.append(self.constR)
            self.mm_group(self.PS[st][:, 0:n], pairs, reads=reads, writes=[self.PSR[st]])
            pi = self.next_pt()
            pt = self.PT[pi][:, 0:n]
            p.op("act", lambda e, pt=pt, st=st: e.activation(out=pt, in_=self.PS[st][:, 0:n], func=AF.Exp, scale=scale),
                 reads=[self.PSR[st]], writes=[self.PTR[pi]])
            return pi

        LA = min(LA, len(self.pools["st"]), len(self.PT) - 2)
        for k in range(min(LA, len(seq))):
            pend.append(issue_qk(*seq[k]))
        hooks = list(hooks) if hooks else []
        G = 2
        for k0 in range(0, len(seq), G):
            if hooks and k0 >= 4 and (k0 - 4) % 4 == 0:
                hooks.pop(0)()
            for k in range(k0, min(k0 + G, len(seq))):
                if k + LA < len(seq):
                    pend.append(issue_qk(*seq[k + LA]))
            for k in range(k0, min(k0 + G, len(seq))):
                pi = pend.pop(0)
                sidx, i = seq[k]
                stt = streams[sidx]
                blk = stt["items"][i]["vblk"]
                lhsT = stt["vsel"](blk)
                rhs = self.PT[pi][:, 0:n]
                acc = stt["acc"]
                p.op("pe", lambda e, lhsT=lhsT, rhs=rhs, i=i, acc=acc: e.matmul(self.PS[acc][0:65, 0:n], lhsT, rhs,
                                                                                start=(i == 0), stop=(i == ni - 1)),
                     reads=[self.PTR[pi], self.VR[blk]], writes=[self.PSR[acc]])
        while hooks:
            hooks.pop(0)()

    def tmp_any(self):
        i = self.tmp_any_rr
        self.tmp_any_rr = (self.tmp_any_rr + 1) % len(self.TMP)
        return i

    def fin1(self, acc, n, add_ap=None, mul_ap=None):
        p = self.p
        ti = self.tmp_any()
        t = self.TMP[ti]
        tr = self.TMPR[ti]
        p.op("act", lambda e: e.copy(out=t[0:65, 0:n], in_=self.PS[acc][0:65, 0:n]), reads=[self.PSR[acc]], writes=[tr])
        d = t[64:65, 0:n]
        if add_ap is not None:
            p.op("dve", lambda e: e.tensor_scalar_add(out=d, in0=d, scalar1=add_ap), reads=[tr, self.constR], writes=[tr])
        p.op("dve", lambda e: e.reciprocal(out=d, in_=d), reads=[tr], writes=[tr])
        if mul_ap is not None:
            p.op("dve", lambda e: e.tensor_scalar_mul(out=d, in0=d, scalar1=mul_ap), reads=[tr, self.LAMR], writes=[tr])
        j = self.rb_rr
        self.rb_rr = (self.rb_rr + 1) % len(self.RB)
        self.rb_of[ti] = j
        rb = self.RB[j][:, 0:n]
        p.op("dve", lambda e: e.tensor_copy(out=rb, in_=d), reads=[tr], writes=[self.RBR[j]])
        return ti

    def fin_bc(self, ti, n):
        p = self.p
        bc = self.pbank("st")
        t = self.TMP[ti]
        j = self.rb_of[ti]
        rb = self.RB[j][:, 0:n]
        p.op("pe", lambda e: e.matmul(self.PS[bc][0:64, 0:n], self.onesb[64:65, 0:64], rb, start=True,
                                      stop=True), reads=[self.RBR[j], self.constR], writes=[self.PSR[bc]])
        return bc

    def fin2_simple(self, ti, n, ydst, yres):
        p = self.p
        bc = self.fin_bc(ti, n)
        t = self.TMP[ti]
        p.op("dve", lambda e: e.tensor_tensor(out=ydst, in0=t[0:64, 0:n], in1=self.PS[bc][0:64, 0:n], op=ALU.mult),
             reads=[self.TMPR[ti], self.PSR[bc]], writes=yres)

    def fin2_diff_a(self, i0, i1, n):
        p = self.p
        bc0 = self.fin_bc(i0, n)
        bc1 = self.fin_bc(i1, n)
        t0_ = self.TMP[i0][0:64, 0:n]
        t1_ = self.TMP[i1][0:64, 0:n]
        p.op("dve", lambda e: e.tensor_tensor(out=t0_, in0=t0_, in1=self.PS[bc0][0:64, 0:n], op=ALU.mult),
             reads=[self.TMPR[i0], self.PSR[bc0]], writes=[self.TMPR[i0]])
        p.op("dve", lambda e: e.tensor_tensor(out=t1_, in0=t1_, in1=self.PS[bc1][0:64, 0:n], op=ALU.mult),
             reads=[self.TMPR[i1], self.PSR[bc1]], writes=[self.TMPR[i1]])
        p.op("pool", lambda e: e.tensor_tensor(out=t0_, in0=t0_, in1=t1_, op=ALU.add),
             reads=[self.TMPR[i0], self.TMPR[i1]], writes=[self.TMPR[i0]])
        p.op("pool", lambda e: e.tensor_tensor(out=t1_, in0=t0_, in1=t0_, op=ALU.mult),
             reads=[self.TMPR[i0], self.TMPR[i1]], writes=[self.TMPR[i1]])

    def fin2_diff_b(self, l, i0, i1, n, ydst, yres):
        p = self.p
        t0_ = self.TMP[i0][0:64, 0:n]
        t1_ = self.TMP[i1][0:64, 0:n]
        bs = self.pbank("st")
        p.op("pe", lambda e: e.matmul(self.PS[bs][0:64, 0:n], self.onesf[0:64, 0:64], t1_, start=True, stop=True),
             reads=[self.TMPR[i1], self.constR], writes=[self.PSR[bs]])
        p.op("act", lambda e: e.activation(out=t1_, in_=self.PS[bs][0:64, 0:n], func=AF.Sqrt, scale=1.0 / 64,
                                           bias=self.epsc[0:64, 0:1]),
             reads=[self.PSR[bs], self.constR], writes=[self.TMPR[i1]])
        p.op("dve", lambda e: e.reciprocal(out=t1_, in_=t1_), reads=[self.TMPR[i1]], writes=[self.TMPR[i1]])
        p.op("dve", lambda e: e.tensor_tensor(out=t0_, in0=t0_, in1=t1_, op=ALU.mult),
             reads=[self.TMPR[i0], self.TMPR[i1]], writes=[self.TMPR[i0]])
        p.op("act", lambda e: e.activation(out=ydst, in_=t0_, func=AF.Identity, scale=self.subg[:, l:l + 1]),
             reads=[self.TMPR[i0], self.constR], writes=yres)

    def recip_bcast(self, acc, n, add_ap=None, mul_ap=None):
        p = self.p
        ri = 0
        rec = self.REC[ri][64:65, 0:n]
        rr = self.RECR[ri]
        den = self.PS[acc][64:65, 0:n]
        if add_ap is not None:
            p.op("dve", lambda e: e.tensor_scalar_add(out=rec, in0=den, scalar1=add_ap),
                 reads=[self.PSR[acc], self.constR], writes=[rr])
            p.op("dve", lambda e: e.reciprocal(out=rec, in_=rec), reads=[rr], writes=[rr])
        else:
            p.op("dve", lambda e: e.reciprocal(out=rec, in_=den), reads=[self.PSR[acc]], writes=[rr])
        if mul_ap is not None:
            p.op("dve", lambda e: e.tensor_scalar_mul(out=rec, in0=rec, scalar1=mul_ap), reads=[rr, self.LAMR],
                 writes=[rr])
        bc = self.pbank("st")
        p.op("pe", lambda e: e.matmul(self.PS[bc][0:64, 0:n], self.onesf[64:65, 0:64], rec, start=True, stop=True),
             reads=[rr, self.constR], writes=[self.PSR[bc]])
        return bc

    def finish_simple(self, acc, n, ydst, yres, add_ap=None):
        p = self.p
        bc = self.recip_bcast(acc, n, add_ap=add_ap)
        ti = self.tmp(0)
        t = self.TMP[ti][0:64, 0:n]
        p.op("act", lambda e: e.copy(out=t, in_=self.PS[acc][0:64, 0:n]), reads=[self.PSR[acc]], writes=[self.TMPR[ti]])
        p.op("dve", lambda e: e.tensor_tensor(out=ydst, in0=t, in1=self.PS[bc][0:64, 0:n], op=ALU.mult),
             reads=[self.TMPR[ti], self.PSR[bc]], writes=yres)

    def finish_c(self, accs, n, ydst, yres):
        p = self.p
        ti = self.tmp(0)
        t = self.TMP[ti][0:65, 0:n]
        p.op("act", lambda e: e.copy(out=t, in_=self.PS[accs[1]][0:65, 0:n]), reads=[self.PSR[accs[1]]],
             writes=[self.TMPR[ti]])
        p.op("dve", lambda e: e.tensor_tensor(out=t, in0=t, in1=self.PS[accs[0]][0:65, 0:n], op=ALU.add),
             reads=[self.TMPR[ti], self.PSR[accs[0]]], writes=[self.TMPR[ti]])
        rec = self.REC[0][64:65, 0:n]
        rr = self.RECR[0]
        p.op("dve", lambda e: e.reciprocal(out=rec, in_=self.TMP[ti][64:65, 0:n]), reads=[self.TMPR[ti]], writes=[rr])
        bc = self.pbank("st")
        p.op("pe", lambda e: e.matmul(self.PS[bc][0:64, 0:n], self.onesf[64:65, 0:64], rec, start=True, stop=True),
             reads=[rr, self.constR], writes=[self.PSR[bc]])
        p.op("dve", lambda e: e.tensor_tensor(out=ydst, in0=self.TMP[ti][0:64, 0:n], in1=self.PS[bc][0:64, 0:n],
                                              op=ALU.mult),
             reads=[self.TMPR[ti], self.PSR[bc]], writes=yres)

    def out_proj_load(self, l, heads_rows):
        p = self.p
        wbs = []
        for r0 in heads_rows:
            i = self.wb_rr
            self.wb_rr = (self.wb_rr + 1) % len(self.WB)
            dst = self.WB[i][0:64, :]
            src = self.wout_d[l, r0:r0 + 64, :]
            p.dma("pool", lambda e, dst=dst, src=src: e.dma_start(out=dst, in_=src), self.wb_d[i],
                  writes=[self.WBR[i]])
            wbs.append(i)
        return wbs

    def out_proj_group(self, l, wbs, g, ms, dve_only=False):
        p = self.p
        t0, n = TGS[g]
        stream = 0 if g < 4 else 1
        mr = self.MODR[l * 2 + stream]
        for m in ms:
            b = self.pbank("st")
            self.mm_group(self.PS[b][:, 0:n],
                          [(self.WB[wbs[h]][0:64, m * P:(m + 1) * P], self.Y[0:64, h, t0:t0 + n]) for h in range(2)],
                          reads=[self.YR[h][g] for h in range(2)] + [self.WBR[w] for w in wbs],
                          writes=[self.PSR[b]])
            xs = self.X[:, m, t0:t0 + n]
            ga = self.modap(l, stream, 5, m)
            if dve_only or m % 2 == 0:
                p.op("dve", lambda e, xs=xs, b=b, n=n, ga=ga: e.scalar_tensor_tensor(
                    out=xs, in0=self.PS[b][:, 0:n], scalar=ga, in1=xs, op0=ALU.mult, op1=ALU.add),
                     reads=[self.PSR[b], mr, self.XR[m][g]], writes=[self.XR[m][g]])
            else:
                ti = self.tmp(2)
                tt = self.TMP[ti][:, 0:n]
                p.op("act", lambda e, tt=tt, b=b, n=n, ga=ga: e.activation(out=tt, in_=self.PS[b][:, 0:n],
                                                                           func=AF.Identity, scale=ga),
                     reads=[self.PSR[b], mr], writes=[self.TMPR[ti]])
                p.op("pool", lambda e, xs=xs, tt=tt: e.tensor_tensor(out=xs, in0=xs, in1=tt, op=ALU.add),
                     reads=[self.TMPR[ti], self.XR[m][g]], writes=[self.XR[m][g]])

    def mixers(self, l, nmix):
        p = self.p
        win = self.win_d[l]
        qtgs = list(range(5)) if l == 0 else list(range(4))
        alltg = list(range(5))
        self.QR = [[Res(f"Q{g}_{e}") for e in range(2)] for g in range(5)]
        self.KR = [[Res(f"K{c}_{g}") for g in range(5)] for c in range(2)]
        self.VR = [Res(f"V{b}") for b in range(18)]
        self.YR = [[Res(f"Y{h}_{g}") for g in range(5)] for h in range(2)]
        Vv = self.V.rearrange("q b (h c) -> q b h c", c=65)

        def set_ones():
            p.op("pool", lambda e: e.memset(Vv[:, :, :, 64:65], 1.0), writes=self.VR)

        def qdst():
            return [(lambda g: self.QT[:, TGS[g][0]:TGS[g][0] + TGS[g][1]], lambda g: [self.QR[g][0], self.QR[g][1]])]

        def kdst(c):
            return (lambda g: self.KT[:, c, TGS[g][0]:TGS[g][0] + TGS[g][1]], lambda g: [self.KR[c][g]])

        nat = lambda c0: (lambda slot, kc: [(slice(0, P), self.WA[slot][:, kc, c0:c0 + P])])
        pair = lambda a: (lambda slot, kc: [(slice(0, 64), self.WA[slot][:, kc, a * 64:a * 64 + 64]),
                                            (slice(64, P), self.WA[slot][:, kc, (a + 2) * 64:(a + 2) * 64 + 64])])

        for mix in range(nmix):
            name = "ABCD"[mix]
            if name in self.skipmix:
                continue
            kind = 32 if name == "D" else 64
            qd = 8 if name == "D" else 16
            scale = (32 ** -0.5) if name == "D" else 0.125
            if name in "AB":
                self.pools = {"st": (0, 1, 2, 3, 4, 5), "acc": (6, 7)}
            else:
                self.pools = {"st": (0, 1, 2, 3), "acc": (4, 5, 6, 7)}
            self.pool_rr = {"st": 0, "acc": 0}
            for a in range(2):
                if name in "AB":
                    if a == 0:
                        set_ones()
                        wkv = self.load_wa(win[:, mix * 512 + 256: mix * 512 + 512])
                        wkvp = self.perm_weights(wkv, qd)
                        self.proj_fm(l, wkv, wkvp, nat(0), [kdst(0)], kind, alltg, norm_col=(2 if name == "B" else None))
                        self.proj_v(wkv, 128)
                    wq = self.load_wa(win[:, mix * 512: mix * 512 + 256])
                    wqp = self.perm_weights(wq, qd)
                    self.proj_fm(l, wq, wqp, pair(a), qdst(), kind, qtgs, norm_col=(0 if name == "B" else None))
                    heads = [a, a + 2]
                    kvh = [0, 1]
                elif name == "C":
                    set_ones()
                    wk = self.load_wa(win[:, 1280:1536])
                    self.proj_fm(l, wk, None, nat(a * P), [kdst(0)], kind, alltg)
                    wv = self.load_wa(win[:, 1536:1792])
                    self.proj_v(wv, a * P)
                    wq = self.load_wa(win[:, 1024:1280])
                    self.proj_fm(l, wq, None, nat(a * P), qdst(), kind, qtgs)
                    heads = [2 * a, 2 * a + 1]
                    kvh = [0, 1]
                    self.build_tc(l, a)
                else:
                    set_ones()
                    wk = self.load_wa(win[:, 2048:2304])
                    wkp = self.perm_weights(wk, qd)
                    self.proj_fm(l, wk, wkp, nat(a * P), [kdst(0), kdst(1)], kind, alltg, pad=True)
                    wv = self.load_wa(win[:, 2304:2560])
                    self.proj_v(wv, a * P)
                    wq = self.load_wa(win[:, 1792:2048])
                    wqp = self.perm_weights(wq, qd)
                    self.proj_fm(l, wq, wqp, nat(a * P), qdst(), kind, qtgs)
                    heads = [2 * a, 2 * a + 1]
                    kvh = [0, 1]
                wbs = self.out_proj_load(l, [mix * 256 + h * 64 for h in heads])
                pending = []

                def queue_outproj(g, last):
                    for ms in ((0, 1, 2, 3), (4, 5, 6, 7)):
                        pending.append(lambda g=g, ms=ms, last=last: self.out_proj_group(l, wbs, g, ms,
                                                                                        dve_only=not last))

                def run_pending():
                    while pending:
                        pending.pop(0)()

                for g in qtgs:
                    t0, n = TGS[g]
                    if g == 4:
                        kbs = [16, 17]
                    elif name == "A":
                        kbs = [kb for kb in range(4 * g - 1, 4 * g + 5) if 0 <= kb < 16] + [16, 17]
                    else:
                        kbs = list(range(18))

                    def mk(e, kb, c=0, mask=False, g=g, t0=t0, n=n):
                        pr = slice(64 * e, 64 * e + 64)
                        it = {"kT": self.KT[pr, c, kb * P:(kb + 1) * P], "q": self.QT[pr, t0:t0 + n],
                              "reads": [self.KR[c][min(kb // 4, 4)], self.QR[g][e]], "vblk": kb}
                        if mask:
                            o = kb - 4 * g
                            it["extra"] = (self.identb[:, :], self.strip[:, (4 - o) * P:(4 - o) * P + n])
                        return it

                    vsels = [(lambda blk, e=e: self.V[:, blk, kvh[e] * 65:kvh[e] * 65 + 65]) for e in range(2)]
                    ydsts = [self.Y[0:64, e, t0:t0 + n] for e in range(2)]
                    yress = [[self.YR[e][g]] for e in range(2)]
                    if name in "AB" or (name == "C" and g == 4):
                        accs = [self.pbank("acc"), self.pbank("acc")]
                        streams = [{"items": [mk(e, kb, 0, mask=(name == "A" and g < 4 and kb < 16)) for kb in kbs],
                                    "acc": accs[e], "vsel": vsels[e]} for e in range(2)]
                        hk = list(pending)
                        del pending[:]
                        self.attn_multi(streams, n, scale, hooks=hk)
                        for e in range(2):
                            add_ap = None
                            if name == "A":
                                add_ap = self.SK[64:65, l * 4 + heads[e]:l * 4 + heads[e] + 1]
                            ti = self.fin1(accs[e], n, add_ap=add_ap)
                            pending.append(lambda ti=ti, n=n, yd=ydsts[e], yr=yress[e]: self.fin2_simple(ti, n, yd, yr))
                        queue_outproj(g, g == qtgs[-1])
                    elif name == "C":
                        run_pending()
                        for e in range(2):
                            pr = slice(64 * e, 64 * e + 64)
                            accs = [self.pbank("acc"), self.pbank("acc")]
                            self.attn_c(accs, e, g, pr, vsels[e])
                            self.finish_c(accs, n, ydsts[e], yress[e])
                        queue_outproj(g, g == qtgs[-1])
                    else:
                        accs = [self.pbank("acc") for _ in range(4)]
                        streams = [{"items": [mk(e, kb, c) for kb in kbs], "acc": accs[c * 2 + e], "vsel": vsels[e]}
                                   for c in range(2) for e in range(2)]
                        hk = list(pending)
                        del pending[:]
                        self.attn_multi(streams, n, scale, hooks=hk)
                        for e in range(2):
                            i0 = self.fin1(accs[e], n)
                            i1 = self.fin1(accs[2 + e], n, mul_ap=self.lams[64:65, l:l + 1])
                            pending.append(lambda i0=i0, i1=i1, n=n: self.fin2_diff_a(i0, i1, n))
                            pending.append(lambda i0=i0, i1=i1, n=n, yd=ydsts[e], yr=yress[e]:
                                           self.fin2_diff_b(l, i0, i1, n, yd, yr))
                        queue_outproj(g, g == qtgs[-1])
                run_pending()
                if mix == 0 and a == 0 and l == 0:
                    self.dump(0, self.QT[:, 0:2048], [])
                    self.dump(1, self.KT[:, 0, 0:2048], [])
                    self.dump(2, self.BIG[:, 3 * T:3 * T + 2048], [])
                    self.dump(3, self.Y[0:64, 0, 0:2048], [], np_=64)
                    self.dump(4, self.Y[0:64, 1, 0:2048], [], np_=64)
                    self.dump(5, self.H[:, 0, 0:2048], [])

    def finish_diff(self, l, acc0, acc1, n, ydst, yres):
        p = self.p
        bc0 = self.recip_bcast(acc0, n)
        i0 = self.tmp(0)
        t0_ = self.TMP[i0][0:64, 0:n]
        p.op("act", lambda e: e.copy(out=t0_, in_=self.PS[acc0][0:64, 0:n]), reads=[self.PSR[acc0]],
             writes=[self.TMPR[i0]])
        p.op("dve", lambda e: e.tensor_tensor(out=t0_, in0=t0_, in1=self.PS[bc0][0:64, 0:n], op=ALU.mult),
             reads=[self.TMPR[i0], self.PSR[bc0]], writes=[self.TMPR[i0]])
        bc1 = self.recip_bcast(acc1, n, mul_ap=self.lams[64:65, l:l + 1])
        i1 = self.tmp(1)
        t1_ = self.TMP[i1][0:64, 0:n]
        p.op("act", lambda e: e.copy(out=t1_, in_=self.PS[acc1][0:64, 0:n]), reads=[self.PSR[acc1]],
             writes=[self.TMPR[i1]])
        p.op("dve", lambda e: e.tensor_tensor(out=t1_, in0=t1_, in1=self.PS[bc1][0:64, 0:n], op=ALU.mult),
             reads=[self.TMPR[i1], self.PSR[bc1]], writes=[self.TMPR[i1]])
        p.op("pool", lambda e: e.tensor_tensor(out=t0_, in0=t0_, in1=t1_, op=ALU.add),
             reads=[self.TMPR[i0], self.TMPR[i1]], writes=[self.TMPR[i0]])
        p.op("pool", lambda e: e.tensor_tensor(out=t1_, in0=t0_, in1=t0_, op=ALU.mult),
             reads=[self.TMPR[i0], self.TMPR[i1]], writes=[self.TMPR[i1]])
        bs = self.pbank("st")
        p.op("pe", lambda e: e.matmul(self.PS[bs][0:64, 0:n], self.onesf[0:64, 0:64], t1_, start=True, stop=True),
             reads=[self.TMPR[i1], self.constR], writes=[self.PSR[bs]])
        p.op("act", lambda e: e.activation(out=t1_, in_=self.PS[bs][0:64, 0:n], func=AF.Sqrt, scale=1.0 / 64,
                                           bias=self.epsc[0:64, 0:1]),
             reads=[self.PSR[bs], self.constR], writes=[self.TMPR[i1]])
        p.op("dve", lambda e: e.reciprocal(out=t1_, in_=t1_), reads=[self.TMPR[i1]], writes=[self.TMPR[i1]])
        p.op("dve", lambda e: e.tensor_tensor(out=t0_, in0=t0_, in1=t1_, op=ALU.mult),
             reads=[self.TMPR[i0], self.TMPR[i1]], writes=[self.TMPR[i0]])
        p.op("act", lambda e: e.activation(out=ydst, in_=t0_, func=AF.Identity, scale=self.subg[:, l:l + 1]),
             reads=[self.TMPR[i0], self.constR], writes=yres)

    def build_tc(self, l, a):
        p = self.p
        p.dma("sp", lambda e: e.dma_start(out=self.STG[0][:, 0:960], in_=self.relT_d[l, a]), self.stg_d[0],
              writes=[self.STGR[0]])
        p.dma("sp", lambda e: e.dma_start(out=self.STG[1][:, 0:960], in_=self.cmask_d), self.stg_d[1],
              writes=[self.STGR[1]])
        p.op("dve", lambda e: e.scalar_tensor_tensor(out=self.Tc, in0=self.STG[0][:, 0:960], scalar=8.0,
                                                     in1=self.STG[1][:, 0:960], op0=ALU.mult, op1=ALU.add),
             reads=[self.STGR[0], self.STGR[1]], writes=[self.CSR[0]])

    def attn_c(self, accs, e, g, pr, vsel):
        p = self.p
        t0, n = TGS[g]
        scale = 0.125
        acc = accs[0]
        sts = []
        for kb in (16, 17):
            st = self.pbank("st")
            self.mm_group(self.PS[st][:, 0:n], [(self.KT[pr, 0, kb * P:(kb + 1) * P], self.QT[pr, t0:t0 + n])],
                          reads=[self.KR[0][4], self.QR[g][e]], writes=[self.PSR[st]])
            pi = self.next_pt()
            pt = self.PT[pi][:, 0:n]
            p.op("act", lambda e_, pt=pt, st=st: e_.activation(out=pt, in_=self.PS[st][:, 0:n], func=AF.Exp, scale=scale),
                 reads=[self.PSR[st]], writes=[self.PTR[pi]])
            sts.append((pi, kb))
        for i, (pi, kb) in enumerate(sts):
            lhsT = vsel(kb)
            rhs = self.PT[pi][:, 0:n]
            p.op("pe", lambda e_, lhsT=lhsT, rhs=rhs, i=i: e_.matmul(self.PS[acc][0:65, 0:n], lhsT, rhs, start=(i == 0),
                                                                     stop=False, skip_group_check=True),
                 reads=[self.PTR[pi], self.VR[kb]], writes=[self.PSR[acc]])
        Tcv = self.Tc[pr, :].rearrange("q (b c) -> q b c", c=64)
        pend = []

        def issue_row(rr):
            r = 8 * g + rr
            rs_ = min(max(r - 4, 0), 24)
            st = self.pbank("st")
            stv = self.PS[st]

            def fn(e_, r=r, rs_=rs_, stv=stv):
                ins = None
                first = {0: True, 1: True}
                for j in range(8):
                    rk = rs_ + j
                    par = rk % 2
                    ins = e_.matmul(stv[par * 64:(par + 1) * 64, j * 64:(j + 1) * 64],
                                    self.KT[pr, 0, rk * 64:(rk + 1) * 64], self.QT[pr, r * 64:(r + 1) * 64],
                                    start=first[par], stop=False, skip_group_check=True)
                    first[par] = False
                for par in range(2):
                    j0 = (par - rs_) % 2
                    b0 = rs_ + j0 - r + 7
                    ov = stv[par * 64:(par + 1) * 64, :].rearrange("q (j c) -> q j c", c=64)[:, j0:8:2, :]
                    ins = e_.matmul(ov, self.identb[pr, pr], Tcv[:, b0:b0 + 7:2, :], start=False, stop=True,
                                    skip_group_check=True)
                return ins

            kg = sorted(set(min((rs_ + j) // 8, 3) for j in range(8)))
            p.op("pe", fn, reads=[self.KR[0][k] for k in kg] + [self.QR[g][e], self.CSR[0], self.constR],
                 writes=[self.PSR[st]])
            pi = self.next_pt()
            pt = self.PT[pi][:, :]
            p.op("act", lambda e_, pt=pt, stv=stv: e_.activation(out=pt, in_=stv[:, :], func=AF.Exp, scale=scale),
                 reads=[self.PSR[st]], writes=[self.PTR[pi]])
            return (pi, rr, rs_)

        LA = 2
        for rr in range(min(LA, 8)):
            pend.append(issue_row(rr))
        for rr in range(8):
            if rr + LA < 8:
                pend.append(issue_row(rr + LA))
            pi, rr_, rs_ = pend.pop(0)

            def fn2(e_, pi=pi, rr_=rr_, rs_=rs_):
                ins = None
                for j in range(8):
                    rk = rs_ + j
                    par = rk % 2
                    ins = e_.matmul(self.PS[accs[par]][0:65, rr_ * 64:(rr_ + 1) * 64],
                                    self.V[par * 64:(par + 1) * 64, rk // 2, e * 65:e * 65 + 65],
                                    self.PT[pi][par * 64:(par + 1) * 64, j * 64:(j + 1) * 64],
                                    start=(par == 1 and rr_ == 0 and j < 2), stop=(rr_ == 7 and j >= 6),
                                    skip_group_check=True)
                return ins

            vb = sorted(set((rs_ + j) // 2 for j in range(8)))
            p.op("pe", fn2, reads=[self.PTR[pi]] + [self.VR[b] for b in vb],
                 writes=[self.PSR[accs[0]], self.PSR[accs[1]]])


_NC_CACHE = {}


def _get_nc(stage, debug=False, skipmix=""):
    if (stage, debug, skipmix) not in _NC_CACHE:
        _NC_CACHE[(stage, debug, skipmix)] = Builder(stage, debug, skipmix).build()
    return _NC_CACHE[(stage, debug, skipmix)]


_HC = {}


def _rope_tables(dim):
    t = np.arange(S, dtype=np.int32)
    row = (t // 64).astype(np.float32)
    col = (t % 64).astype(np.float32)
    half = dim // 2
    freqs = (np.float32(10000.0) ** (-np.arange(0, half, 2, dtype=np.float32) / np.float32(half))).astype(np.float32)
    ang_r = row[:, None] * freqs[None, :]
    ang_c = col[:, None] * freqs[None, :]
    ang = np.concatenate([ang_r, ang_r, ang_c, ang_c], axis=-1).astype(np.float32)
    cos = np.cos(ang).astype(np.float32)
    sin = np.sin(ang).astype(np.float32)
    qd = half // 2
    sign = np.where((np.arange(dim) % half) < qd, -1.0, 1.0).astype(np.float32)
    tab = np.zeros((P, 2, T), np.float32)
    reps = P // dim
    tab[:, 0, :S] = np.tile(cos.T, (reps, 1))
    tab[:, 1, :S] = np.tile((sin * sign[None, :]).T, (reps, 1))
    tab[:, 0, S:] = 1.0
    return tab


def _host_consts():
    if _HC:
        return _HC
    _HC["cs64"] = _rope_tables(64)
    _HC["cs32"] = _rope_tables(32)
    NEG = -30000.0
    kk = np.arange(P)[:, None]; qq = np.arange(P)[None, :]
    Mb = np.full((P, P), NEG, np.float32)
    Ub = np.where(kk <= qq, 0.0, NEG).astype(np.float32)
    Lb = np.where(kk >= qq, 0.0, NEG).astype(np.float32)
    Zb = np.zeros((P, P), np.float32)
    _HC["strip"] = np.ascontiguousarray(np.concatenate([Mb, Mb, Mb, Ub, Zb, Lb, Mb, Mb, Mb], axis=1))
    col = np.arange(64)
    c_start = np.clip(col - 8, 0, 48)
    ok = (col[:, None] >= c_start[None, :]) & (col[:, None] < c_start[None, :] + 16)
    cm = np.where(ok, 0.0, NEG).astype(np.float32)
    _HC["cmask"] = np.ascontiguousarray(np.tile(cm, (2, 15)))
    pm = np.zeros((P, 2), np.float32)
    pm[:, 0] = ((np.arange(P) % 64) < 32)
    pm[:, 1] = ((np.arange(P) % 64) >= 32)
    _HC["pmask"] = pm
    return _HC


def kernel(stage=99, debug=False, skipmix="", **inputs):
    nc = _get_nc(stage, debug, skipmix)
    f = lambda a: np.ascontiguousarray(np.asarray(a), dtype=np.float32)
    x = f(inputs["x"]); ctx = f(inputs["ctx"]); c = f(inputs["c"]); c_ctx = f(inputs["c_ctx"])
    common = {
        "final_norm_g": f(inputs["final_norm_g"]).reshape(1, D),
        "identf": np.eye(P, dtype=np.float32),
        "w_ada": f(inputs["w_ada"]),
        "b_adaT": np.ascontiguousarray(f(inputs["b_ada"]).reshape(2, 72, P).transpose(0, 2, 1)),
    }
    for n in ("w_ffn1_gate", "w_ffn1_up", "w_ffn1_down", "w_ffn2_gate", "w_ffn2_up", "w_ffn2_down", "w_in", "w_out",
              "sink_logit"):
        common[n] = f(inputs[n])
    common.update(_host_consts())
    perm64 = np.array([(d + 16) if (d % 32) < 16 else (d - 16) for d in range(64)])
    qg = f(inputs["q_norm_g"]); kg = f(inputs["k_norm_g"])
    qkg = np.stack([np.tile(qg, (1, 2)), np.tile(qg[:, perm64], (1, 2)),
                    np.tile(kg, (1, 2)), np.tile(kg[:, perm64], (1, 2))], axis=-1)
    common["qkg"] = np.ascontiguousarray(qkg)
    common["lamv"] = np.ascontiguousarray(np.stack([f(inputs["lam_q1"]), f(inputs["lam_k1"]), f(inputs["lam_q2"]),
                                                    f(inputs["lam_k2"])], axis=1))
    common["subg"] = np.ascontiguousarray(f(inputs["subln_g"]).T)
    rel = f(inputs["rel_pos_bias"])
    ck = np.arange(64)[:, None]; cq = np.arange(64)[None, :]
    dc = np.clip(ck - cq + 15, 0, 30)
    relT = rel[:, :, :, dc]
    relT = relT.transpose(0, 1, 3, 2, 4).reshape(2, 2, P, 960)
    common["relT"] = np.ascontiguousarray(relT)
    in_maps = []
    for b in range(8):
        cc = np.stack([c[b].reshape(KC, P).T, c_ctx.reshape(KC, P).T], axis=-1).reshape(P, 16)
        m = dict(common)
        m.update({"x": x[b], "ctx": ctx[b], "cc": np.ascontiguousarray(cc)})
        in_maps.append(m)
    res = run_bass_kernel_spmd(nc, in_maps, core_ids=list(range(8)))
    if debug:
        return np.stack([r["out"] for r in res.results], axis=0), res.results[0]["dbg"]
    return np.stack([r["out"] for r in res.results], axis=0)
```

```python
import numpy as np
import concourse.bass as bass
import concourse.mybir as mybir
from concourse.bass_utils import run_bass_kernel_spmd
from contextlib import ExitStack

F32 = mybir.dt.float32
BF16 = mybir.dt.bfloat16
AF = mybir.ActivationFunctionType
ALU = mybir.AluOpType
AX = mybir.AxisListType

P = 128
D = 1024
KC = 8
S = 2048
L = 256
T = S + L
TGS = [(0, 512), (512, 512), (1024, 512), (1536, 512), (2048, 256)]
DFF = 2816
FC = 22
EPS = 1e-6
SAME_ENGINE_SYNC = True


class Res:
    __slots__ = ("name", "w", "r")

    def __init__(self, name=""):
        self.name = name
        self.w = None
        self.r = {}


class DSem:
    def __init__(self, sem, name):
        self.sem = sem
        self.count = 0
        self.name = name


class Prog:
    COMPUTE = ("pe", "act", "dve", "pool")
    ALL = ("pe", "act", "dve", "pool", "sp")

    def __init__(self, nc, es):
        self.nc = nc
        self.es = es
        self.streams = {e: [] for e in self.ALL}
        self.cnt = {e: 0 for e in self.COMPUTE}
        self.sem = {e: es.enter_context(nc.semaphore("c_" + e)) for e in self.COMPUTE}
        self.seen = {e: {} for e in self.ALL}
        self.dsems = []
        self.n_ins = 0

    def dsem(self, name):
        d = DSem(self.es.enter_context(self.nc.semaphore("d_" + name)), name)
        self.dsems.append(d)
        return d

    def _collect(self, eng, reads, writes):
        deps = {}

        def add(tok):
            if tok is None:
                return
            k, v = tok
            if deps.get(k, 0) < v:
                deps[k] = v

        for r in reads:
            add(r.w)
        for w in writes:
            add(w.w)
            for k, v in w.r.items():
                add((k, v))
        waits = []
        for k, v in deps.items():
            if k == eng and (eng == "pe" or not SAME_ENGINE_SYNC):
                continue
            if self.seen[eng].get(k, 0) >= v:
                continue
            self.seen[eng][k] = v
            waits.append((k, v))
        return waits

    def _update(self, tok, reads, writes):
        k, v = tok
        for r in reads:
            if r.r.get(k, 0) < v:
                r.r[k] = v
        for w in writes:
            w.w = tok
            w.r = {}

    def op(self, eng, fn, reads=(), writes=()):
        waits = self._collect(eng, reads, writes)
        self.cnt[eng] += 1
        tok = (eng, self.cnt[eng])
        self.streams[eng].append((waits, fn, (self.sem[eng], 1)))
        self._update(tok, reads, writes)
        return tok

    def dma(self, q, fn, dsem, reads=(), writes=()):
        waits = self._collect(q, reads, writes)
        dsem.count += 16
        tok = (dsem, dsem.count)
        self.streams[q].append((waits, fn, (dsem.sem, 16)))
        self._update(tok, reads, writes)
        return tok

    def wait_all(self, eng):
        waits = []
        for e in self.COMPUTE:
            if e != eng and self.cnt[e] > self.seen[eng].get(e, 0):
                waits.append((e, self.cnt[e]))
                self.seen[eng][e] = self.cnt[e]
        for d in self.dsems:
            if d.count > self.seen[eng].get(d, 0):
                waits.append((d, d.count))
                self.seen[eng][d] = d.count
        self.streams[eng].append((waits, None, None))

    def barrier(self):
        for e in self.ALL:
            self.wait_all(e)

    def emit(self):
        nc = self.nc
        handles = {"pe": "tensor", "act": "scalar", "dve": "vector", "pool": "gpsimd", "sp": "sync"}
        with nc.Block() as block:
            for e in self.ALL:
                recs = self.streams[e]

                def body(eng, recs=recs):
                    for waits, fn, inc in recs:
                        for k, v in waits:
                            s = self.sem[k] if isinstance(k, str) else k.sem
                            eng.wait_ge(s, v)
                        if fn is not None:
                            ins = fn(eng)
                            ins.then_inc(inc[0], inc[1])

                getattr(block, handles[e])(body)


class Builder:
    def __init__(self, stage, debug=False, skipmix=""):
        self.stage = stage
        self.debug = debug
        self.skipmix = skipmix
        self.nc = bass.Bass("TRN2", target_bir_lowering=False)
        self.es = ExitStack()

    def dram_in(self, name, shape, dt=F32):
        return self.nc.dram_tensor(name, list(shape), dt, kind="ExternalInput").ap()

    def sb(self, name, shape, dt):
        return self.es.enter_context(self.nc.sbuf_tensor("sb_" + name, list(shape), dt))

    def build(self):
        nc = self.nc
        with self.es:
            self.p = Prog(nc, self.es)
            self._build()
            self.p.emit()
        return nc

    def _build(self):
        nc, p = self.nc, self.p
        st = self.stage
        self.x_d = self.dram_in("x", [S, D])
        self.ctx_d = self.dram_in("ctx", [L, D])
        self.fng_d = self.dram_in("final_norm_g", [1, D])
        self.identf_d = self.dram_in("identf", [P, P])
        self.cc_d = self.dram_in("cc", [P, 16])
        self.wada_d = self.dram_in("w_ada", [2, D, 9 * D])
        self.bada_d = self.dram_in("b_adaT", [2, P, 72])
        self.wg_d = [self.dram_in("w_ffn1_gate", [2, D, DFF]), self.dram_in("w_ffn2_gate", [2, D, DFF])]
        self.wu_d = [self.dram_in("w_ffn1_up", [2, D, DFF]), self.dram_in("w_ffn2_up", [2, D, DFF])]
        self.wd_d = [self.dram_in("w_ffn1_down", [2, DFF, D]), self.dram_in("w_ffn2_down", [2, DFF, D])]
        self.win_d = self.dram_in("w_in", [2, D, 2560])
        self.wout_d = self.dram_in("w_out", [2, D, D])
        self.cs_d = {64: self.dram_in("cs64", [P, 2, T]), 32: self.dram_in("cs32", [P, 2, T])}
        self.strip_d = self.dram_in("strip", [P, 1152])
        self.sink_d = self.dram_in("sink_logit", [2, 4])
        self.qkg_d = self.dram_in("qkg", [2, P, 4])
        self.lamv_d = self.dram_in("lamv", [2, 4, 32])
        self.subg_d = self.dram_in("subg", [64, 2])
        self.relT_d = self.dram_in("relT", [2, 2, P, 960])
        self.cmask_d = self.dram_in("cmask", [P, 960])
        self.pmask_d = self.dram_in("pmask", [P, 2])
        self.out_d = nc.dram_tensor("out", [S, D], F32, kind="ExternalOutput").ap()
        self.dbg_d = nc.dram_tensor("dbg", [P, 6, 2048], F32, kind="ExternalOutput").ap() if self.debug else None

        self.X = self.sb("X", [P, KC, T], F32)
        self.XR = [[Res(f"X{c}_{g}") for g in range(5)] for c in range(KC)]
        self.H = self.sb("H", [P, KC, T], BF16)
        self.HR = [[Res(f"H{c}_{g}") for g in range(5)] for c in range(KC)]
        self.BIG = self.sb("BIG", [P, 5 * T + 2340], BF16)
        self.STG = [self.sb(f"stg{i}", [P, 2048], F32) for i in range(2)]
        self.STGR = [Res(f"stg{i}") for i in range(2)]
        self.stg_d = [p.dsem(f"stg{i}") for i in range(2)]
        self.stg_rr = 0
        self.stgb_d = [p.dsem(f"stgb{i}") for i in range(2)]
        NWA, NWB = 4, 4
        self.WA = [self.sb(f"wa{i}", [P, 8, 256], BF16) for i in range(NWA)]
        self.WAR = [Res(f"wa{i}") for i in range(NWA)]
        self.wa_d = [p.dsem(f"wa{i}") for i in range(NWA)]
        self.wa_rr = 0
        self.WB = [self.sb(f"wb{i}", [P, 1024], BF16) for i in range(NWB)]
        self.WBR = [Res(f"wb{i}") for i in range(NWB)]
        self.wb_d = [p.dsem(f"wb{i}") for i in range(NWB)]
        self.wb_rr = 0
        NTMP = 6
        self.TMP = [self.sb(f"tmp{i}", [P, 512], F32) for i in range(NTMP)]
        self.TMPR = [Res(f"tmp{i}") for i in range(NTMP)]
        self.tmp_rr = [0, 0, 0]
        self.tmp_any_rr = 0
        self.identf = self.sb("identf", [P, P], F32)
        self.onesf = self.sb("onesf", [P, P], F32)
        self.small = self.sb("small", [P, 64], F32)
        self.SMR = [Res(f"sm{i}") for i in range(64)]
        self.constR = Res("const")
        self.epsc = self.sb("epsc", [P, 1], F32)
        self.cc = self.sb("cc", [P, 16], F32)
        self.sT = self.sb("sT", [P, 16], BF16)
        self.sTR = Res("sT")
        self.bada = self.sb("bada", [P, 2, 72], F32)
        self.MOD = self.sb("MOD", [P, 4 * 72], F32)
        self.MODR = [Res(f"mod{i}") for i in range(4)]
        self.QT = self.BIG[:, 0:T]
        self.KT = self.BIG[:, T:3 * T].rearrange("q (c t) -> q c t", c=2)
        self.V = self.BIG[:, 3 * T:3 * T + 2340].rearrange("q (b c) -> q b c", c=130)
        self.Y = self.BIG[:, 3 * T + 2340:5 * T + 2340].rearrange("q (h t) -> q h t", h=2)
        self.PT = [self.sb(f"pt{i}", [P, 512], BF16) for i in range(6)]
        self.PTR = [Res(f"pt{i}") for i in range(6)]
        self.pt_rr = 0
        self.CS = [self.sb(f"cs{i}", [P, 2, 512], F32) for i in range(2)]
        self.CSR = [Res(f"cs{i}") for i in range(2)]
        self.cs_ds = [p.dsem(f"cs{i}") for i in range(2)]
        self.cs_rr = 0
        self.Tc = self.CS[0][:, :, :].rearrange("q a n -> q (a n)").bitcast(BF16)[:, 0:960]
        self.identb = self.sb("identb", [P, P], BF16)
        self.onesb = self.sb("onesb", [P, 64], BF16)
        self.blockones = self.sb("blockones", [P, P], F32)
        self.strip = self.sb("strip", [P, 1152], BF16)
        yoff = 3 * T + 2340
        self.yoff = yoff
        self.SK = self.sb("SK", [P, 8], F32)
        self.qkg = self.sb("qkg", [P, 2, 4], F32)
        self.lams = self.sb("lams", [P, 16], F32)
        self.LAMR = Res("lam")
        self.subg = self.sb("subg", [64, 2], F32)
        self.pmask = self.sb("pmask", [P, 2], F32)
        self.Y128 = self.BIG[:, yoff:yoff + T]
        self.RB = [self.BIG[64:65, yoff + T + j * 512:yoff + T + (j + 1) * 512] for j in range(4)]
        self.RBR = [Res(f"rb{j}") for j in range(4)]
        self.shiftI = self.sb("shiftI", [64, P], BF16)
        self.rb_rr = 0
        self.rb_of = {}
        self.RECR = [Res(f"rec{i}") for i in range(1)]
        self.rec_rr = 0
        self.PS = [self.es.enter_context(nc.psum_tensor(f"ps{i}", [P, 512], F32)) for i in range(8)]
        self.pool_rr = {"st": 0, "acc": 0}
        self.pools = {"st": (0, 1, 2, 3), "acc": (4, 5, 6, 7)}
        self.PSR = [Res(f"ps{i}") for i in range(8)]
        self.ps_rr = 0
        self.bg = []
        self.evac_rr = 0
        self.cd = p.dsem("const")
        self.out_ds = p.dsem("out")

        p.op("pool", lambda e: e.memset(self.epsc[:], EPS), writes=[self.constR])
        p.op("pool", lambda e: e.memset(self.onesf[:], 1.0), writes=[self.constR])
        self.cdma(self.identf[:], self.identf_d)
        self.cdma(self.cc[:], self.cc_d)
        self.cdma(self.bada[:], self.bada_d.rearrange("l p j -> p l j"))
        self.cdma(self.qkg[:], self.qkg_d.rearrange("l p j -> p l j"))
        self.cdma(self.pmask[:], self.pmask_d)
        self.cdma(self.SK[64:65, 0:8], self.sink_d.rearrange("(o l) h -> o (l h)", o=1))
        self.cdma(self.subg[:], self.subg_d)
        p.dma("pool", lambda e: e.dma_start(out=self.strip[:], in_=self.strip_d), p.dsem("strip"), writes=[self.constR])
        p.op("pool", lambda e: e.memset(self.onesb[:], 1.0), writes=[self.constR])
        p.op("pool", lambda e: e.memset(self.blockones[:], 0.0), writes=[self.constR])
        p.op("pool", lambda e: e.memset(self.blockones[0:64, 0:64], 1.0), writes=[self.constR])
        p.op("pool", lambda e: e.memset(self.blockones[64:128, 64:128], 1.0), writes=[self.constR])
        p.op("act", lambda e: e.copy(out=self.identb[:], in_=self.identf[:]), reads=[self.constR], writes=[self.constR])
        p.op("pool", lambda e: e.memset(self.shiftI[:], 0.0), writes=[self.constR])
        p.op("act", lambda e: e.copy(out=self.shiftI[:, 64:128], in_=self.identf[0:64, 0:64]), reads=[self.constR],
             writes=[self.constR])
        p.op("act", lambda e: e.activation(out=self.SK[64:65, 0:8], in_=self.SK[64:65, 0:8], func=AF.Exp),
             reads=[self.constR], writes=[self.constR])
        for l in range(2):
            li = 1.0 - (0.8 - 0.6 * float(np.exp(-0.3 * l)))
            p.op("dve", lambda e, l=l, li=li: e.tensor_scalar_mul(out=self.subg[:, l:l + 1], in0=self.subg[:, l:l + 1],
                                                                   scalar1=li), reads=[self.constR], writes=[self.constR])
        p.op("act", lambda e: e.activation(out=self.sT[:], in_=self.cc[:], func=AF.Silu),
             reads=[self.constR], writes=[self.sTR])

        self.load_x()
        k = 0
        for l in range(2):
            if st >= k + 1:
                if l == 0:
                    for f_ in self.adaln_steps(0, list(range(0, 6)), [0, 1, 2]):
                        f_()
                    self.bg = self.adaln_steps(0, list(range(6, 18)), [3, 4, 5, 6, 7, 8])
                self.norm_mod(l, 0, 1)
                self.ffn(l, 0, 2, ctx=True)
            if st >= k + 2:
                p.barrier()
                self.norm_mod(l, 3, 4)
                self.lam_setup(l)
                self.mixers(l, min(4, st - k - 1))
                p.barrier()
            if st >= k + 6:
                self.norm_mod(l, 6, 7, tgs=range(5) if l == 0 else range(4))
                if l == 0 and st >= 7:
                    self.bg = self.adaln_steps(1, list(range(18)), list(range(9)))
                self.ffn(l, 1, 8, ctx=(l == 0))
            k += 6
        self.final_norm()
        p.barrier()

    def dump(self, slot, ap, reads, np_=P, w=2048):
        if not self.debug:
            return
        p = self.p
        p.barrier()
        p.op("act", lambda e: e.copy(out=self.STG[0][0:np_, 0:w], in_=ap), reads=reads, writes=[self.STGR[0]])
        p.dma("sp", lambda e: e.dma_start(out=self.dbg_d[0:np_, slot, 0:w], in_=self.STG[0][0:np_, 0:w]), self.out_ds,
              reads=[self.STGR[0]])
        p.barrier()

    def cdma(self, dst, src):
        self.p.dma("sp", lambda e: e.dma_start(out=dst, in_=src), self.cd, writes=[self.constR])

    def bank(self):
        b = self.ps_rr
        self.ps_rr = (self.ps_rr + 1) % 7
        return b

    def bg_step(self):
        if self.bg:
            self.bg.pop(0)()

    def bg_flush(self):
        while self.bg:
            self.bg.pop(0)()

    def pbank(self, pool):
        lst = self.pools[pool]
        self.pool_rr[pool] = (self.pool_rr[pool] + 1) % len(lst)
        return lst[self.pool_rr[pool]]

    def tmp(self, role):
        i = role * 2 + self.tmp_rr[role]
        self.tmp_rr[role] ^= 1
        return i

    def modap(self, l, stream, r, c):
        col = (l * 2 + stream) * 72 + r * 8 + c
        return self.MOD[:, col:col + 1]

    def mm_group(self, out, pairs, reads, writes, **kw):
        n = len(pairs)

        def fn(e):
            ins = None
            for i, (lhsT, rhs) in enumerate(pairs):
                ins = e.matmul(out, lhsT, rhs, start=(i == 0), stop=(i == n - 1), **kw)
            return ins

        return self.p.op("pe", fn, reads=reads, writes=writes)

    def load_x(self):
        p = self.p
        for g, (t0, n) in enumerate(TGS):
            ntile = n // P
            for s in range(ntile // 2):
                if g < 4:
                    src = self.x_d[t0 + s * 256: t0 + s * 256 + 256, :]
                else:
                    src = self.ctx_d[s * 256: s * 256 + 256, :]
                src = src.rearrange("(j p) d -> p j d", p=P)
                dst = self.STG[s][:, :].rearrange("p (j d) -> p j d", j=2)
                p.dma("sp", lambda e, dst=dst, src=src: e.dma_start(out=dst, in_=src), self.stg_d[s],
                      writes=[self.STGR[s]])
            for c in range(KC):
                b = self.bank()

                def tr(e, b=b, c=c, ntile=ntile):
                    ins = None
                    for j in range(ntile):
                        src = self.STG[j // 2][:, (j % 2) * 1024 + c * P:(j % 2) * 1024 + (c + 1) * P]
                        ins = e.transpose(out=self.PS[b][:, j * P:(j + 1) * P], in_=src, identity=self.identf[:])
                    return ins

                p.op("pe", tr, reads=[self.STGR[s] for s in range(ntile // 2)] + [self.constR], writes=[self.PSR[b]])
                self.evac_copy(self.X[:, c, t0:t0 + n], self.PS[b][:, 0:n], [self.PSR[b]], [self.XR[c][g]])

    def evac_copy(self, dst, src, reads, writes):
        p = self.p
        self.evac_rr ^= 1
        if self.evac_rr:
            p.op("act", lambda e: e.copy(out=dst, in_=src), reads=reads, writes=writes)
        else:
            p.op("dve", lambda e: e.tensor_copy(out=dst, in_=src), reads=reads, writes=writes)

    def final_norm(self):
        p = self.p
        self.gbc = self.WA[0][:, :, :].rearrange("q k n -> q (k n)").bitcast(F32)
        p.dma("sp", lambda e: e.dma_start(out=self.gbc, in_=self.fng_d.partition_broadcast(P)), self.cd,
              writes=[self.WAR[0]])
        for tt in range(S // P):
            g = tt // 4
            s = tt % 2
            stg = self.STG[s]
            sr = self.STGR[s]
            for h in range(2):
                b = self.bank()

                def tr(e, b=b, h=h, tt=tt):
                    ins = None
                    for j in range(4):
                        c = h * 4 + j
                        ins = e.transpose(out=self.PS[b][:, j * P:(j + 1) * P],
                                          in_=self.X[:, c, tt * P:(tt + 1) * P], identity=self.identf[:])
                    return ins

                p.op("pe", tr, reads=[self.XR[h * 4 + j][g] for j in range(4)] + [self.constR],
                     writes=[self.PSR[b]])
                dst = stg[:, h * 512:(h + 1) * 512]
                src = self.PS[b][:, :]
                if h == 0:
                    p.op("act", lambda e, dst=dst, src=src: e.copy(out=dst, in_=src), reads=[self.PSR[b]], writes=[sr])
                else:
                    p.op("dve", lambda e, dst=dst, src=src: e.tensor_copy(out=dst, in_=src), reads=[self.PSR[b]],
                         writes=[sr])
            ss = self.small[:, 2 * s:2 * s + 1]
            rs = self.small[:, 2 * s + 1:2 * s + 2]
            smr = self.SMR[s]
            p.op("act", lambda e, stg=stg, ss=ss: e.activation(out=stg[:, 1024:2048], in_=stg[:, 0:1024],
                                                              func=AF.Square, accum_out=ss),
                 reads=[sr], writes=[sr, smr])
            p.op("act", lambda e, ss=ss, rs=rs: e.activation(out=rs, in_=ss, func=AF.Sqrt, scale=1.0 / D,
                                                             bias=self.epsc[:, 0:1]),
                 reads=[smr, self.constR], writes=[smr])
            p.op("dve", lambda e, rs=rs: e.reciprocal(out=rs, in_=rs), reads=[smr], writes=[smr])
            p.op("dve", lambda e, stg=stg, rs=rs: e.scalar_tensor_tensor(out=stg[:, 1024:2048], in0=stg[:, 0:1024],
                                                                        scalar=rs, in1=self.gbc,
                                                                        op0=ALU.mult, op1=ALU.mult),
                 reads=[sr, smr, self.WAR[0]], writes=[sr])
            p.dma("sp", lambda e, stg=stg, tt=tt: e.dma_start(out=self.out_d[tt * P:(tt + 1) * P, :],
                                                             in_=stg[:, 1024:2048]),
                  self.out_ds, reads=[sr])

    def adaln_steps(self, l, pieces, rows):
        p = self.p
        b = 7
        psr = self.PSR[b]
        slots = {}

        def issue(j4):
            s_ = self.stg_rr
            self.stg_rr ^= 1
            slots[j4] = s_
            src = self.wada_d[l, :, j4 * 512:(j4 + 1) * 512].rearrange("(kc q) n -> q kc n", q=P)
            stgb = self.STG[s_][:, :].bitcast(BF16)
            dst = stgb.rearrange("q (kc n) -> q kc n", kc=8)
            p.dma("pool", lambda e: e.dma_start(out=dst, in_=src), self.stgb_d[s_], writes=[self.STGR[s_]])

        def step(idx):
            j4 = pieces[idx]
            if idx == 0:
                issue(j4)
            if idx + 1 < len(pieces):
                issue(pieces[idx + 1])
            s_ = slots[j4]
            stgb = self.STG[s_][:, :].bitcast(BF16)
            for q4 in range(4):
                j = j4 * 4 + q4
                pairs = [(stgb[:, kc * 512 + q4 * 128: kc * 512 + q4 * 128 + 128],
                          self.sT[:, 2 * kc:2 * kc + 2]) for kc in range(8)]
                self.mm_group(self.PS[b][:, 2 * j:2 * j + 2], pairs, reads=[self.STGR[s_], self.sTR], writes=[psr])

        def final():
            c0, c1 = rows[0] * 8, (rows[-1] + 1) * 8
            pv = self.PS[b][:, 0:144].rearrange("q (j t) -> q j t", t=2)
            for stream in range(2):
                base = (l * 2 + stream) * 72
                mr = self.MODR[l * 2 + stream]
                dst = self.MOD[:, base + c0:base + c1]
                p.op("dve", lambda e, dst=dst, stream=stream: e.tensor_tensor(out=dst, in0=pv[:, c0:c1, stream],
                                                                              in1=self.bada[:, l, c0:c1], op=ALU.add),
                     reads=[psr, self.constR], writes=[mr])
                for r in (1, 4, 7):
                    if r in rows:
                        d2 = self.MOD[:, base + r * 8:base + r * 8 + 8]
                        p.op("dve", lambda e, d2=d2: e.tensor_scalar_add(out=d2, in0=d2, scalar1=1.0), reads=[mr],
                             writes=[mr])
                for r in (2, 8):
                    if r in rows:
                        d2 = self.MOD[:, base + r * 8:base + r * 8 + 8]
                        p.op("dve", lambda e, d2=d2: e.tensor_scalar_mul(out=d2, in0=d2, scalar1=0.5), reads=[mr],
                             writes=[mr])

        return [(lambda idx=idx: step(idx)) for idx in range(len(pieces))] + [final]

    def norm_mod(self, l, r_shift, r_scale, tgs=range(5)):
        p = self.p
        for g in tgs:
            t0, n = TGS[g]
            stream = 0 if g < 4 else 1
            mr = self.MODR[l * 2 + stream]
            b = self.bank()
            sqs = []
            for c in range(KC):
                ti = self.tmp(0)
                xs = self.X[:, c, t0:t0 + n]
                sq = self.TMP[ti][:, 0:n]
                p.op("pool", lambda e, sq=sq, xs=xs: e.tensor_tensor(out=sq, in0=xs, in1=xs, op=ALU.mult),
                     reads=[self.XR[c][g]], writes=[self.TMPR[ti]])
                p.op("pe", lambda e, sq=sq, c=c, b=b, n=n: e.matmul(self.PS[b][:, 0:n], self.onesf[:], sq,
                                                                    start=(c == 0), stop=(c == KC - 1)),
                     reads=[self.TMPR[ti], self.constR], writes=[self.PSR[b]])
            ri = self.tmp(1)
            rs = self.TMP[ri][:, 0:n]
            p.op("act", lambda e, rs=rs, b=b, n=n: e.activation(out=rs, in_=self.PS[b][:, 0:n], func=AF.Sqrt,
                                                                scale=1.0 / D, bias=self.epsc[:, 0:1]),
                 reads=[self.PSR[b], self.constR], writes=[self.TMPR[ri]])
            p.op("dve", lambda e, rs=rs: e.reciprocal(out=rs, in_=rs), reads=[self.TMPR[ri]], writes=[self.TMPR[ri]])
            for c in range(KC):
                ti = self.tmp(2)
                tt = self.TMP[ti][:, 0:n]
                xs = self.X[:, c, t0:t0 + n]
                p.op("dve", lambda e, tt=tt, xs=xs, rs=rs: e.tensor_tensor(out=tt, in0=xs, in1=rs, op=ALU.mult),
                     reads=[self.XR[c][g], self.TMPR[ri]], writes=[self.TMPR[ti]])
                hd = self.H[:, c, t0:t0 + n]
                sc = self.modap(l, stream, r_scale, c)
                sh = self.modap(l, stream, r_shift, c)
                p.op("act", lambda e, hd=hd, tt=tt, sc=sc, sh=sh: e.activation(out=hd, in_=tt, func=AF.Identity,
                                                                              scale=sc, bias=sh),
                     reads=[self.TMPR[ti], mr], writes=[self.HR[c][g]])

    def load_wa(self, src):
        i = self.wa_rr
        self.wa_rr = (self.wa_rr + 1) % len(self.WA)
        srcv = src.rearrange("(kc q) n -> q kc n", q=P)
        dst = self.WA[i][:, :, :]
        self.p.dma("pool", lambda e: e.dma_start(out=dst, in_=srcv), self.wa_d[i], writes=[self.WAR[i]])
        return i

    def load_wb(self, src):
        i = self.wb_rr
        self.wb_rr = (self.wb_rr + 1) % len(self.WB)
        dst = self.WB[i][:, :]
        self.p.dma("pool", lambda e: e.dma_start(out=dst, in_=src), self.wb_d[i], writes=[self.WBR[i]])
        return i

    def ffn(self, l, which, r_gate, ctx=True):
        p = self.p
        wg, wu, wd = self.wg_d[which][l], self.wu_d[which][l], self.wd_d[which][l]
        tgs = list(range(5)) if ctx else list(range(4))
        U = self.BIG[:, 0:4 * T].rearrange("q (f t) -> q f t", f=4)
        UR = [[Res(f"U{f}_{g}") for g in range(5)] for f in range(4)]
        npieces = FC // 2
        slabs = [list(range(i, min(i + 2, npieces))) for i in range(0, npieces, 2)]

        def issue_piece(j):
            return (self.load_wa(wg[:, j * 256:(j + 1) * 256]), self.load_wa(wu[:, j * 256:(j + 1) * 256]))

        def issue_down(slab):
            return [self.load_wb(wd[f * P:(f + 1) * P, :]) for j in slab for f in (2 * j, 2 * j + 1)]

        pend = {0: issue_piece(0)}
        pend_d = {0: issue_down(slabs[0])}
        for si, slab in enumerate(slabs):
            for j in slab:
                if j + 1 < npieces:
                    pend[j + 1] = issue_piece(j + 1)
                ig, iu = pend.pop(j)
                for half in range(2):
                    fl = (j - slab[0]) * 2 + half
                    self.bg_step()
                    for g in tgs:
                        t0, n = TGS[g]
                        hreads = [self.HR[kc][g] for kc in range(KC)]
                        bg = self.bank()
                        self.mm_group(self.PS[bg][:, 0:n],
                                      [(self.WA[ig][:, kc, half * P:(half + 1) * P], self.H[:, kc, t0:t0 + n])
                                       for kc in range(KC)], reads=hreads + [self.WAR[ig]], writes=[self.PSR[bg]])
                        bu = self.bank()
                        self.mm_group(self.PS[bu][:, 0:n],
                                      [(self.WA[iu][:, kc, half * P:(half + 1) * P], self.H[:, kc, t0:t0 + n])
                                       for kc in range(KC)], reads=hreads + [self.WAR[iu]], writes=[self.PSR[bu]])
                        ti = self.tmp(0)
                        sg = self.TMP[ti][:, 0:n]
                        p.op("act", lambda e, sg=sg, bg=bg, n=n: e.activation(out=sg, in_=self.PS[bg][:, 0:n],
                                                                              func=AF.Silu),
                             reads=[self.PSR[bg]], writes=[self.TMPR[ti]])
                        ud = U[:, fl, t0:t0 + n]
                        p.op("dve", lambda e, ud=ud, sg=sg, bu=bu, n=n: e.tensor_tensor(out=ud, in0=sg,
                                                                                        in1=self.PS[bu][:, 0:n],
                                                                                        op=ALU.mult),
                             reads=[self.TMPR[ti], self.PSR[bu]], writes=[UR[fl][g]])
            wbs = pend_d.pop(si)
            nfl = len(wbs)
            for g in tgs:
                t0, n = TGS[g]
                stream = 0 if g < 4 else 1
                mr = self.MODR[l * 2 + stream]
                for m in range(KC):
                    b = self.bank()
                    self.mm_group(self.PS[b][:, 0:n],
                                  [(self.WB[wbs[fl]][:, m * P:(m + 1) * P], U[:, fl, t0:t0 + n]) for fl in range(nfl)],
                                  reads=[UR[fl][g] for fl in range(nfl)] + [self.WBR[w] for w in wbs],
                                  writes=[self.PSR[b]])
                    xs = self.X[:, m, t0:t0 + n]
                    ga = self.modap(l, stream, r_gate, m)
                    p.op("dve", lambda e, xs=xs, b=b, n=n, ga=ga: e.scalar_tensor_tensor(
                        out=xs, in0=self.PS[b][:, 0:n], scalar=ga, in1=xs, op0=ALU.mult, op1=ALU.add),
                         reads=[self.PSR[b], mr, self.XR[m][g]], writes=[self.XR[m][g]])
            if si + 1 < len(slabs):
                pend_d[si + 1] = issue_down(slabs[si + 1])
        self.bg_flush()

    def lam_setup(self, l):
        p = self.p
        r = self.LAMR
        ti = self.tmp(0)
        lt = self.TMP[ti]
        sm = self.lams
        p.dma("sp", lambda e: e.dma_start(out=lt[0:1, 0:128], in_=self.lamv_d[l:l + 1].rearrange("o a d -> o (a d)")),
              self.cd, writes=[self.TMPR[ti], r])
        for t in range(2):
            a = lt[0:1, (2 * t) * 32:(2 * t) * 32 + 32]
            bb = lt[0:1, (2 * t + 1) * 32:(2 * t + 1) * 32 + 32]
            p.op("dve", lambda e, a=a, bb=bb: e.tensor_tensor(out=a, in0=a, in1=bb, op=ALU.mult),
                 reads=[self.constR, r], writes=[r, self.TMPR[ti]])
            p.op("dve", lambda e, a=a, t=t: e.reduce_sum(out=sm[0:1, 8 + t:9 + t], in_=a, axis=AX.X),
                 reads=[r, self.TMPR[ti]], writes=[r])
        p.op("act", lambda e: e.activation(out=sm[0:1, 8:10], in_=sm[0:1, 8:10], func=AF.Exp), reads=[r], writes=[r])
        lam_init = 0.8 - 0.6 * float(np.exp(-0.3 * l))
        p.op("dve", lambda e: e.tensor_tensor(out=sm[0:1, 10:11], in0=sm[0:1, 9:10], in1=sm[0:1, 8:9],
                                              op=ALU.subtract), reads=[r], writes=[r])
        p.op("dve", lambda e: e.tensor_scalar_add(out=sm[0:1, 10:11], in0=sm[0:1, 10:11], scalar1=-lam_init),
             reads=[r], writes=[r])
        b = self.pbank("st")
        p.op("pe", lambda e: e.matmul(self.PS[b][:, 0:1], self.onesf[0:1, :], sm[0:1, 10:11], start=True, stop=True),
             reads=[r, self.constR], writes=[self.PSR[b]])
        p.op("dve", lambda e: e.tensor_copy(out=sm[:, l:l + 1], in_=self.PS[b][:, 0:1]), reads=[self.PSR[b]],
             writes=[r])

    def load_cs(self, kind, g):
        t0, n = TGS[g]
        i = self.cs_rr
        self.cs_rr ^= 1
        src = self.cs_d[kind][:, :, t0:t0 + n]
        dst = self.CS[i][:, :, 0:n]
        self.p.dma("sp", lambda e: e.dma_start(out=dst, in_=src), self.cs_ds[i], writes=[self.CSR[i]])
        return i

    def perm_weights(self, wi, qd):
        wp = self.wa_rr
        self.wa_rr = (self.wa_rr + 1) % len(self.WA)
        sv = self.WA[wi][:, :, :].rearrange("q k (b t d) -> q k b t d", t=2, d=qd)
        dv = self.WA[wp][:, :, :].rearrange("q k (b t d) -> q k b t d", t=2, d=qd)
        self.p.op("pool", lambda e: e.tensor_copy(out=dv[:, :, :, 0, :], in_=sv[:, :, :, 1, :]),
                  reads=[self.WAR[wi]], writes=[self.WAR[wp]])
        self.p.op("pool", lambda e: e.tensor_copy(out=dv[:, :, :, 1, :], in_=sv[:, :, :, 0, :]),
                  reads=[self.WAR[wi]], writes=[self.WAR[wp]])
        return wp

    def proj_fm(self, l, wi, wpi, colsel, dsts, kind, tgs, norm_col=None, pad=False):
        p = self.p
        for g in tgs:
            t0, n = TGS[g]
            hreads = [self.HR[kc][g] for kc in range(KC)]
            bp = self.pbank("st")
            self.mm_parts(bp, n, colsel, wi, t0, hreads)
            if wpi is None:
                dst = dsts[0][0](g)
                self.evac_copy(dst, self.PS[bp][:, 0:n], [self.PSR[bp]], dsts[0][1](g))
                continue
            ci = self.load_cs(kind, g)
            cos = self.CS[ci][:, 0, 0:n]
            ssin = self.CS[ci][:, 1, 0:n]
            br = self.pbank("st")
            self.mm_parts(br, n, colsel, wpi, t0, hreads)
            i1 = self.tmp(0)
            i2 = self.tmp(1)
            t1 = self.TMP[i1][:, 0:n]
            t2 = self.TMP[i2][:, 0:n]
            pp = self.PS[bp][:, 0:n]
            pr = self.PS[br][:, 0:n]
            if norm_col is None:
                p.op("dve", lambda e, t1=t1, pp=pp, cos=cos: e.tensor_tensor(out=t1, in0=pp, in1=cos, op=ALU.mult),
                     reads=[self.PSR[bp], self.CSR[ci]], writes=[self.TMPR[i1]])
                p.op("dve", lambda e, t2=t2, pr=pr, ssin=ssin: e.tensor_tensor(out=t2, in0=pr, in1=ssin, op=ALU.mult),
                     reads=[self.PSR[br], self.CSR[ci]], writes=[self.TMPR[i2]])
                if not pad:
                    dst = dsts[0][0](g)
                    p.op("pool", lambda e, dst=dst, t1=t1, t2=t2: e.tensor_tensor(out=dst, in0=t1, in1=t2, op=ALU.add),
                         reads=[self.TMPR[i1], self.TMPR[i2]], writes=dsts[0][1](g))
                else:
                    p.op("pool", lambda e, t1=t1, t2=t2: e.tensor_tensor(out=t1, in0=t1, in1=t2, op=ALU.add),
                         reads=[self.TMPR[i1], self.TMPR[i2]], writes=[self.TMPR[i1]])
                    for k2 in range(2):
                        dst = dsts[k2][0](g)
                        m = self.pmask[:, k2:k2 + 1]
                        eng = "dve" if k2 == 0 else "pool"
                        p.op(eng, lambda e, dst=dst, t1=t1, m=m: e.tensor_scalar_mul(out=dst, in0=t1, scalar1=m),
                             reads=[self.TMPR[i1], self.constR], writes=dsts[k2][1](g))
            else:
                i3 = self.tmp(2)
                i4 = self.tmp(2)
                sq = self.TMP[i3][:, 0:n]
                rs = self.TMP[i4][:, 0:n]
                p.op("act", lambda e, sq=sq, pp=pp: e.activation(out=sq, in_=pp, func=AF.Square),
                     reads=[self.PSR[bp]], writes=[self.TMPR[i3]])
                bs = self.pbank("st")
                p.op("pe", lambda e, bs=bs, sq=sq, n=n: e.matmul(self.PS[bs][:, 0:n], self.blockones[:], sq,
                                                                 start=True, stop=True),
                     reads=[self.TMPR[i3], self.constR], writes=[self.PSR[bs]])
                p.op("act", lambda e, rs=rs, bs=bs, n=n: e.activation(out=rs, in_=self.PS[bs][:, 0:n], func=AF.Sqrt,
                                                                      scale=1.0 / 64, bias=self.epsc[:, 0:1]),
                     reads=[self.PSR[bs], self.constR], writes=[self.TMPR[i4]])
                p.op("dve", lambda e, rs=rs: e.reciprocal(out=rs, in_=rs), reads=[self.TMPR[i4]],
                     writes=[self.TMPR[i4]])
                g1 = self.qkg[:, l, norm_col:norm_col + 1]
                g2 = self.qkg[:, l, norm_col + 1:norm_col + 2]
                p.op("dve", lambda e, t1=t1, pp=pp, cos=cos, g1=g1: e.scalar_tensor_tensor(
                    out=t1, in0=pp, scalar=g1, in1=cos, op0=ALU.mult, op1=ALU.mult),
                     reads=[self.PSR[bp], self.CSR[ci], self.constR], writes=[self.TMPR[i1]])
                p.op("dve", lambda e, t2=t2, pr=pr, ssin=ssin, g2=g2: e.scalar_tensor_tensor(
                    out=t2, in0=pr, scalar=g2, in1=ssin, op0=ALU.mult, op1=ALU.mult),
                     reads=[self.PSR[br], self.CSR[ci], self.constR], writes=[self.TMPR[i2]])
                p.op("pool", lambda e, t1=t1, t2=t2: e.tensor_tensor(out=t1, in0=t1, in1=t2, op=ALU.add),
                     reads=[self.TMPR[i1], self.TMPR[i2]], writes=[self.TMPR[i1]])
                dst = dsts[0][0](g)
                p.op("dve", lambda e, dst=dst, t1=t1, rs=rs: e.tensor_tensor(out=dst, in0=t1, in1=rs, op=ALU.mult),
                     reads=[self.TMPR[i1], self.TMPR[i4]], writes=dsts[0][1](g))

    def mm_parts(self, b, n, colsel, slot, t0, hreads):
        parts = colsel(slot, 0)

        def fn(e):
            ins = None
            for pi_ in range(len(parts)):
                for kc in range(KC):
                    psl, lhsT = colsel(slot, kc)[pi_]
                    ins = e.matmul(self.PS[b][psl, 0:n], lhsT, self.H[:, kc, t0:t0 + n], start=(kc == 0),
                                   stop=(kc == KC - 1))
            return ins

        self.p.op("pe", fn, reads=hreads + [self.WAR[slot]], writes=[self.PSR[b]])

    def proj_v(self, wi, col0):
        p = self.p
        Vv = self.V.rearrange("q b (h c) -> q b h c", c=65)
        for b0 in range(0, 18, 4):
            nb = min(4, 18 - b0)
            bk = self.pbank("st")
            for j in range(nb):
                blk = b0 + j
                g = min(blk // 4, 4)
                self.mm_group(self.PS[bk][:, j * P:(j + 1) * P],
                              [(self.H[:, kc, blk * P:(blk + 1) * P], self.WA[wi][:, kc, col0:col0 + P])
                               for kc in range(KC)],
                              reads=[self.HR[kc][g] for kc in range(KC)] + [self.WAR[wi]], writes=[self.PSR[bk]])
            dst = Vv[:, b0:b0 + nb, :, 0:64]
            src = self.PS[bk][:, 0:nb * P].rearrange("q (b h c) -> q b h c", h=2, c=64)
            self.evac_copy(dst, src, [self.PSR[bk]], [self.VR[b0 + j] for j in range(nb)])

    def next_pt(self):
        i = self.pt_rr
        self.pt_rr = (self.pt_rr + 1) % len(self.PT)
        return i

    def attn_multi(self, streams, n, scale, hooks=None):
        p = self.p
        ns = len(streams)
        ni = len(streams[0]["items"])
        seq = []
        for i in range(ni):
            for sidx in range(ns):
                seq.append((sidx, i))
        LA = len(self.PT) - 1
        pend = []

        def issue_qk(sidx, i):
            it = streams[sidx]["items"][i]
            st = self.pbank("st")
            pairs = [(it["kT"], it["q"])]
            reads = list(it["reads"])
            if it.get("extra") is not None:
                pairs.append(it["extra"])
                reads.append(self.constR)
            self.mm_group(self.PS[st][:, 0:n], pairs, reads=reads, writes=[self.PSR[st]])
            pi = self.next_pt()
            pt = self.PT[pi][:, 0:n]
            p.op("act", lambda e, pt=pt, st=st: e.activation(out=pt, in_=self.PS[st][:, 0:n], func=AF.Exp, scale=scale),
                 reads=[self.PSR[st]], writes=[self.PTR[pi]])
            return pi

        LA = min(LA, len(self.pools["st"]), len(self.PT) - 2)
        for k in range(min(LA, len(seq))):
            pend.append(issue_qk(*seq[k]))
        hooks = list(hooks) if hooks else []
        G = 2
        for k0 in range(0, len(seq), G):
            if hooks and k0 >= 4 and (k0 - 4) % 4 == 0:
                hooks.pop(0)()
            for k in range(k0, min(k0 + G, len(seq))):
                if k + LA < len(seq):
                    pend.append(issue_qk(*seq[k + LA]))
            for k in range(k0, min(k0 + G, len(seq))):
                pi = pend.pop(0)
                sidx, i = seq[k]
                stt = streams[sidx]
                blk = stt["items"][i]["vblk"]
                lhsT = stt["vsel"](blk)
                rhs = self.PT[pi][:, 0:n]
                acc = stt["acc"]
                p.op("pe", lambda e, lhsT=lhsT, rhs=rhs, i=i, acc=acc: e.matmul(self.PS[acc][0:65, 0:n], lhsT, rhs,
                                                                                start=(i == 0), stop=(i == ni - 1)),
                     reads=[self.PTR[pi], self.VR[blk]], writes=[self.PSR[acc]])
        while hooks:
            hooks.pop(0)()

    def tmp_any(self):
        i = self.tmp_any_rr
        self.tmp_any_rr = (self.tmp_any_rr + 1) % len(self.TMP)
        return i

    def fin1(self, acc, n, add_ap=None, mul_ap=None):
        p = self.p
        ti = self.tmp_any()
        t = self.TMP[ti]
        tr = self.TMPR[ti]
        p.op("act", lambda e: e.copy(out=t[0:65, 0:n], in_=self.PS[acc][0:65, 0:n]), reads=[self.PSR[acc]], writes=[tr])
        d = t[64:65, 0:n]
        if add_ap is not None:
            p.op("dve", lambda e: e.tensor_scalar_add(out=d, in0=d, scalar1=add_ap), reads=[tr, self.constR], writes=[tr])
        p.op("dve", lambda e: e.reciprocal(out=d, in_=d), reads=[tr], writes=[tr])
        if mul_ap is not None:
            p.op("dve", lambda e: e.tensor_scalar_mul(out=d, in0=d, scalar1=mul_ap), reads=[tr, self.LAMR], writes=[tr])
        j = self.rb_rr
        self.rb_rr = (self.rb_rr + 1) % len(self.RB)
        self.rb_of[ti] = j
        rb = self.RB[j][:, 0:n]
        p.op("dve", lambda e: e.tensor_copy(out=rb, in_=d), reads=[tr], writes=[self.RBR[j]])
        return ti

    def fin_bc(self, ti, n):
        p = self.p
        bc = self.pbank("st")
        t = self.TMP[ti]
        j = self.rb_of[ti]
        rb = self.RB[j][:, 0:n]
        p.op("pe", lambda e: e.matmul(self.PS[bc][0:64, 0:n], self.onesb[64:65, 0:64], rb, start=True,
                                      stop=True), reads=[self.RBR[j], self.constR], writes=[self.PSR[bc]])
        return bc

    def y_dst(self, e, g):
        t0, n = TGS[g]
        if e == 0:
            return self.Y128[0:64, t0:t0 + n], [self.YR[0][g]]
        return self.BIG[0:64, self.yoff + T + t0:self.yoff + T + t0 + n], [self.YTR[g]]

    def y_shift(self, e, g):
        if e == 0:
            return
        p = self.p
        t0, n = TGS[g]
        b = self.pbank("st")
        src = self.BIG[0:64, self.yoff + T + t0:self.yoff + T + t0 + n]
        p.op("pe", lambda e_: e_.matmul(self.PS[b][:, 0:n], self.shiftI[:, :], src, start=True, stop=True),
             reads=[self.YTR[g], self.constR], writes=[self.PSR[b]])
        dst = self.Y128[64:128, t0:t0 + n]
        p.op("dve", lambda e_: e_.tensor_copy(out=dst, in_=self.PS[b][64:128, 0:n]), reads=[self.PSR[b]],
             writes=[self.YR[1][g]])

    def fin2_simple(self, ti, n, e, g):
        p = self.p
        bc = self.fin_bc(ti, n)
        t = self.TMP[ti]
        ydst, yres = self.y_dst(e, g)
        p.op("dve", lambda e_: e_.tensor_tensor(out=ydst, in0=t[0:64, 0:n], in1=self.PS[bc][0:64, 0:n], op=ALU.mult),
             reads=[self.TMPR[ti], self.PSR[bc]], writes=yres)
        self.y_shift(e, g)

    def fin2_diff_a(self, i0, i1, n):
        p = self.p
        bc0 = self.fin_bc(i0, n)
        bc1 = self.fin_bc(i1, n)
        t0_ = self.TMP[i0][0:64, 0:n]
        t1_ = self.TMP[i1][0:64, 0:n]
        p.op("dve", lambda e: e.tensor_tensor(out=t0_, in0=t0_, in1=self.PS[bc0][0:64, 0:n], op=ALU.mult),
             reads=[self.TMPR[i0], self.PSR[bc0]], writes=[self.TMPR[i0]])
        p.op("dve", lambda e: e.tensor_tensor(out=t1_, in0=t1_, in1=self.PS[bc1][0:64, 0:n], op=ALU.mult),
             reads=[self.TMPR[i1], self.PSR[bc1]], writes=[self.TMPR[i1]])
        p.op("pool", lambda e: e.tensor_tensor(out=t0_, in0=t0_, in1=t1_, op=ALU.add),
             reads=[self.TMPR[i0], self.TMPR[i1]], writes=[self.TMPR[i0]])
        p.op("pool", lambda e: e.tensor_tensor(out=t1_, in0=t0_, in1=t0_, op=ALU.mult),
             reads=[self.TMPR[i0], self.TMPR[i1]], writes=[self.TMPR[i1]])

    def fin2_diff_b(self, l, i0, i1, n, e, g):
        p = self.p
        ydst, yres = self.y_dst(e, g)
        t0_ = self.TMP[i0][0:64, 0:n]
        t1_ = self.TMP[i1][0:64, 0:n]
        bs = self.pbank("st")
        p.op("pe", lambda e: e.matmul(self.PS[bs][0:64, 0:n], self.onesf[0:64, 0:64], t1_, start=True, stop=True),
             reads=[self.TMPR[i1], self.constR], writes=[self.PSR[bs]])
        p.op("act", lambda e: e.activation(out=t1_, in_=self.PS[bs][0:64, 0:n], func=AF.Sqrt, scale=1.0 / 64,
                                           bias=self.epsc[0:64, 0:1]),
             reads=[self.PSR[bs], self.constR], writes=[self.TMPR[i1]])
        p.op("dve", lambda e: e.reciprocal(out=t1_, in_=t1_), reads=[self.TMPR[i1]], writes=[self.TMPR[i1]])
        p.op("dve", lambda e: e.tensor_tensor(out=t0_, in0=t0_, in1=t1_, op=ALU.mult),
             reads=[self.TMPR[i0], self.TMPR[i1]], writes=[self.TMPR[i0]])
        p.op("act", lambda e_: e_.activation(out=ydst, in_=t0_, func=AF.Identity, scale=self.subg[:, l:l + 1]),
             reads=[self.TMPR[i0], self.constR], writes=yres)
        self.y_shift(e, g)

    def recip_bcast(self, acc, n, add_ap=None, mul_ap=None):
        p = self.p
        ri = 0
        rec = self.REC[ri][64:65, 0:n]
        rr = self.RECR[ri]
        den = self.PS[acc][64:65, 0:n]
        if add_ap is not None:
            p.op("dve", lambda e: e.tensor_scalar_add(out=rec, in0=den, scalar1=add_ap),
                 reads=[self.PSR[acc], self.constR], writes=[rr])
            p.op("dve", lambda e: e.reciprocal(out=rec, in_=rec), reads=[rr], writes=[rr])
        else:
            p.op("dve", lambda e: e.reciprocal(out=rec, in_=den), reads=[self.PSR[acc]], writes=[rr])
        if mul_ap is not None:
            p.op("dve", lambda e: e.tensor_scalar_mul(out=rec, in0=rec, scalar1=mul_ap), reads=[rr, self.LAMR],
                 writes=[rr])
        bc = self.pbank("st")
        p.op("pe", lambda e: e.matmul(self.PS[bc][0:64, 0:n], self.onesf[64:65, 0:64], rec, start=True, stop=True),
             reads=[rr, self.constR], writes=[self.PSR[bc]])
        return bc

    def finish_simple(self, acc, n, ydst, yres, add_ap=None):
        p = self.p
        bc = self.recip_bcast(acc, n, add_ap=add_ap)
        ti = self.tmp(0)
        t = self.TMP[ti][0:64, 0:n]
        p.op("act", lambda e: e.copy(out=t, in_=self.PS[acc][0:64, 0:n]), reads=[self.PSR[acc]], writes=[self.TMPR[ti]])
        p.op("dve", lambda e: e.tensor_tensor(out=ydst, in0=t, in1=self.PS[bc][0:64, 0:n], op=ALU.mult),
             reads=[self.TMPR[ti], self.PSR[bc]], writes=yres)

    def finish_c(self, accs, n, e, g):
        p = self.p
        ti = self.tmp_any()
        t = self.TMP[ti]
        tr = self.TMPR[ti]
        p.op("act", lambda e_: e_.copy(out=t[0:65, 0:n], in_=self.PS[accs[1]][0:65, 0:n]), reads=[self.PSR[accs[1]]],
             writes=[tr])
        p.op("dve", lambda e_: e_.tensor_tensor(out=t[0:65, 0:n], in0=t[0:65, 0:n], in1=self.PS[accs[0]][0:65, 0:n],
                                                op=ALU.add), reads=[tr, self.PSR[accs[0]]], writes=[tr])
        d = t[64:65, 0:n]
        p.op("dve", lambda e_: e_.reciprocal(out=d, in_=d), reads=[tr], writes=[tr])
        j = self.rb_rr
        self.rb_rr = (self.rb_rr + 1) % len(self.RB)
        self.rb_of[ti] = j
        rb = self.RB[j][:, 0:n]
        p.op("dve", lambda e_: e_.tensor_copy(out=rb, in_=d), reads=[tr], writes=[self.RBR[j]])
        self.fin2_simple(ti, n, e, g)

    def out_proj_load(self, l, heads_rows):
        p = self.p
        i = self.wb_rr
        self.wb_rr = (self.wb_rr + 1) % len(self.WB)
        for h, r0 in enumerate(heads_rows):
            dst = self.WB[i][64 * h:64 * h + 64, :]
            src = self.wout_d[l, r0:r0 + 64, :]
            p.dma("pool", lambda e, dst=dst, src=src: e.dma_start(out=dst, in_=src), self.wb_d[i],
                  writes=[self.WBR[i]])
        return i

    def out_proj_group(self, l, wb, g, ms, dve_only=False):
        p = self.p
        t0, n = TGS[g]
        stream = 0 if g < 4 else 1
        mr = self.MODR[l * 2 + stream]
        for m in ms:
            b = self.pbank("st")
            self.mm_group(self.PS[b][:, 0:n], [(self.WB[wb][:, m * P:(m + 1) * P], self.Y128[:, t0:t0 + n])],
                          reads=[self.YR[h][g] for h in range(2)] + [self.WBR[wb]], writes=[self.PSR[b]])
            xs = self.X[:, m, t0:t0 + n]
            ga = self.modap(l, stream, 5, m)
            if dve_only or m % 2 == 0:
                p.op("dve", lambda e, xs=xs, b=b, n=n, ga=ga: e.scalar_tensor_tensor(
                    out=xs, in0=self.PS[b][:, 0:n], scalar=ga, in1=xs, op0=ALU.mult, op1=ALU.add),
                     reads=[self.PSR[b], mr, self.XR[m][g]], writes=[self.XR[m][g]])
            else:
                ti = self.tmp(2)
                tt = self.TMP[ti][:, 0:n]
                p.op("act", lambda e, tt=tt, b=b, n=n, ga=ga: e.activation(out=tt, in_=self.PS[b][:, 0:n],
                                                                           func=AF.Identity, scale=ga),
                     reads=[self.PSR[b], mr], writes=[self.TMPR[ti]])
                p.op("pool", lambda e, xs=xs, tt=tt: e.tensor_tensor(out=xs, in0=xs, in1=tt, op=ALU.add),
                     reads=[self.TMPR[ti], self.XR[m][g]], writes=[self.XR[m][g]])

    def mixers(self, l, nmix):
        p = self.p
        win = self.win_d[l]
        qtgs = list(range(5)) if l == 0 else list(range(4))
        alltg = list(range(5))
        self.QR = [[Res(f"Q{g}_{e}") for e in range(2)] for g in range(5)]
        self.KR = [[Res(f"K{c}_{g}") for g in range(5)] for c in range(2)]
        self.VR = [Res(f"V{b}") for b in range(18)]
        self.YR = [[Res(f"Y{h}_{g}") for g in range(5)] for h in range(2)]
        self.YTR = [Res(f"Yt_{g}") for g in range(5)]
        Vv = self.V.rearrange("q b (h c) -> q b h c", c=65)

        def set_ones():
            p.op("pool", lambda e: e.memset(Vv[:, :, :, 64:65], 1.0), writes=self.VR)

        def qdst():
            return [(lambda g: self.QT[:, TGS[g][0]:TGS[g][0] + TGS[g][1]], lambda g: [self.QR[g][0], self.QR[g][1]])]

        def kdst(c):
            return (lambda g: self.KT[:, c, TGS[g][0]:TGS[g][0] + TGS[g][1]], lambda g: [self.KR[c][g]])

        nat = lambda c0: (lambda slot, kc: [(slice(0, P), self.WA[slot][:, kc, c0:c0 + P])])
        pair = lambda a: (lambda slot, kc: [(slice(0, 64), self.WA[slot][:, kc, a * 64:a * 64 + 64]),
                                            (slice(64, P), self.WA[slot][:, kc, (a + 2) * 64:(a + 2) * 64 + 64])])

        for mix in range(nmix):
            name = "ABCD"[mix]
            if name in self.skipmix:
                continue
            kind = 32 if name == "D" else 64
            qd = 8 if name == "D" else 16
            scale = (32 ** -0.5) if name == "D" else 0.125
            if name in "AB":
                self.pools = {"st": (0, 1, 2, 3, 4, 5), "acc": (6, 7)}
            else:
                self.pools = {"st": (0, 1, 2, 3), "acc": (4, 5, 6, 7)}
            self.pool_rr = {"st": 0, "acc": 0}
            for a in range(2):
                if name in "AB":
                    if a == 0:
                        set_ones()
                        wkv = self.load_wa(win[:, mix * 512 + 256: mix * 512 + 512])
                        wkvp = self.perm_weights(wkv, qd)
                        self.proj_fm(l, wkv, wkvp, nat(0), [kdst(0)], kind, alltg, norm_col=(2 if name == "B" else None))
                        self.proj_v(wkv, 128)
                    wq = self.load_wa(win[:, mix * 512: mix * 512 + 256])
                    wqp = self.perm_weights(wq, qd)
                    self.proj_fm(l, wq, wqp, pair(a), qdst(), kind, qtgs, norm_col=(0 if name == "B" else None))
                    heads = [a, a + 2]
                    kvh = [0, 1]
                elif name == "C":
                    set_ones()
                    wk = self.load_wa(win[:, 1280:1536])
                    self.proj_fm(l, wk, None, nat(a * P), [kdst(0)], kind, alltg)
                    wv = self.load_wa(win[:, 1536:1792])
                    self.proj_v(wv, a * P)
                    wq = self.load_wa(win[:, 1024:1280])
                    self.proj_fm(l, wq, None, nat(a * P), qdst(), kind, qtgs)
                    heads = [2 * a, 2 * a + 1]
                    kvh = [0, 1]
                    self.build_tc(l, a)
                else:
                    set_ones()
                    wk = self.load_wa(win[:, 2048:2304])
                    wkp = self.perm_weights(wk, qd)
                    self.proj_fm(l, wk, wkp, nat(a * P), [kdst(0), kdst(1)], kind, alltg, pad=True)
                    wv = self.load_wa(win[:, 2304:2560])
                    self.proj_v(wv, a * P)
                    wq = self.load_wa(win[:, 1792:2048])
                    wqp = self.perm_weights(wq, qd)
                    self.proj_fm(l, wq, wqp, nat(a * P), qdst(), kind, qtgs)
                    heads = [2 * a, 2 * a + 1]
                    kvh = [0, 1]
                wbs = self.out_proj_load(l, [mix * 256 + h * 64 for h in heads])
                pending = []

                def queue_outproj(g, last):
                    for ms in ((0, 1, 2, 3), (4, 5, 6, 7)):
                        pending.append(lambda g=g, ms=ms, last=last: self.out_proj_group(l, wbs, g, ms,
                                                                                        dve_only=not last))

                def run_pending():
                    while pending:
                        pending.pop(0)()

                for g in qtgs:
                    t0, n = TGS[g]
                    if g == 4:
                        kbs = [16, 17]
                    elif name == "A":
                        kbs = [kb for kb in range(4 * g - 1, 4 * g + 5) if 0 <= kb < 16] + [16, 17]
                    else:
                        kbs = list(range(18))

                    def mk(e, kb, c=0, mask=False, g=g, t0=t0, n=n):
                        pr = slice(64 * e, 64 * e + 64)
                        it = {"kT": self.KT[pr, c, kb * P:(kb + 1) * P], "q": self.QT[pr, t0:t0 + n],
                              "reads": [self.KR[c][min(kb // 4, 4)], self.QR[g][e]], "vblk": kb}
                        if mask:
                            o = kb - 4 * g
                            it["extra"] = (self.identb[:, :], self.strip[:, (4 - o) * P:(4 - o) * P + n])
                        return it

                    vsels = [(lambda blk, e=e: self.V[:, blk, kvh[e] * 65:kvh[e] * 65 + 65]) for e in range(2)]
                    if name in "AB" or (name == "C" and g == 4):
                        accs = [self.pbank("acc"), self.pbank("acc")]
                        streams = [{"items": [mk(e, kb, 0, mask=(name == "A" and g < 4 and kb < 16)) for kb in kbs],
                                    "acc": accs[e], "vsel": vsels[e]} for e in range(2)]
                        hk = list(pending)
                        del pending[:]
                        self.attn_multi(streams, n, scale, hooks=hk)
                        for e in range(2):
                            add_ap = None
                            if name == "A":
                                add_ap = self.SK[64:65, l * 4 + heads[e]:l * 4 + heads[e] + 1]
                            ti = self.fin1(accs[e], n, add_ap=add_ap)
                            pending.append(lambda ti=ti, n=n, e=e, g=g: self.fin2_simple(ti, n, e, g))
                        queue_outproj(g, g == qtgs[-1])
                    elif name == "C":
                        run_pending()
                        for e in range(2):
                            pr = slice(64 * e, 64 * e + 64)
                            accs = [self.pbank("acc"), self.pbank("acc")]
                            self.attn_c(accs, e, g, pr, vsels[e])
                            self.finish_c(accs, n, e, g)
                        queue_outproj(g, g == qtgs[-1])
                    else:
                        accs = [self.pbank("acc") for _ in range(4)]
                        streams = [{"items": [mk(e, kb, c) for kb in kbs], "acc": accs[c * 2 + e], "vsel": vsels[e]}
                                   for c in range(2) for e in range(2)]
                        hk = list(pending)
                        del pending[:]
                        self.attn_multi(streams, n, scale, hooks=hk)
                        for e in range(2):
                            i0 = self.fin1(accs[e], n)
                            i1 = self.fin1(accs[2 + e], n, mul_ap=self.lams[64:65, l:l + 1])
                            pending.append(lambda i0=i0, i1=i1, n=n: self.fin2_diff_a(i0, i1, n))
                            pending.append(lambda i0=i0, i1=i1, n=n, e=e, g=g:
                                           self.fin2_diff_b(l, i0, i1, n, e, g))
                        queue_outproj(g, g == qtgs[-1])
                run_pending()
                if mix == 0 and a == 0 and l == 0:
                    self.dump(0, self.QT[:, 0:2048], [])
                    self.dump(1, self.KT[:, 0, 0:2048], [])
                    self.dump(2, self.BIG[:, 3 * T:3 * T + 2048], [])
                    self.dump(3, self.Y[0:64, 0, 0:2048], [], np_=64)
                    self.dump(4, self.Y[0:64, 1, 0:2048], [], np_=64)
                    self.dump(5, self.H[:, 0, 0:2048], [])

    def finish_diff(self, l, acc0, acc1, n, ydst, yres):
        p = self.p
        bc0 = self.recip_bcast(acc0, n)
        i0 = self.tmp(0)
        t0_ = self.TMP[i0][0:64, 0:n]
        p.op("act", lambda e: e.copy(out=t0_, in_=self.PS[acc0][0:64, 0:n]), reads=[self.PSR[acc0]],
             writes=[self.TMPR[i0]])
        p.op("dve", lambda e: e.tensor_tensor(out=t0_, in0=t0_, in1=self.PS[bc0][0:64, 0:n], op=ALU.mult),
             reads=[self.TMPR[i0], self.PSR[bc0]], writes=[self.TMPR[i0]])
        bc1 = self.recip_bcast(acc1, n, mul_ap=self.lams[64:65, l:l + 1])
        i1 = self.tmp(1)
        t1_ = self.TMP[i1][0:64, 0:n]
        p.op("act", lambda e: e.copy(out=t1_, in_=self.PS[acc1][0:64, 0:n]), reads=[self.PSR[acc1]],
             writes=[self.TMPR[i1]])
        p.op("dve", lambda e: e.tensor_tensor(out=t1_, in0=t1_, in1=self.PS[bc1][0:64, 0:n], op=ALU.mult),
             reads=[self.TMPR[i1], self.PSR[bc1]], writes=[self.TMPR[i1]])
        p.op("pool", lambda e: e.tensor_tensor(out=t0_, in0=t0_, in1=t1_, op=ALU.add),
             reads=[self.TMPR[i0], self.TMPR[i1]], writes=[self.TMPR[i0]])
        p.op("pool", lambda e: e.tensor_tensor(out=t1_, in0=t0_, in1=t0_, op=ALU.mult),
             reads=[self.TMPR[i0], self.TMPR[i1]], writes=[self.TMPR[i1]])
        bs = self.pbank("st")
        p.op("pe", lambda e: e.matmul(self.PS[bs][0:64, 0:n], self.onesf[0:64, 0:64], t1_, start=True, stop=True),
             reads=[self.TMPR[i1], self.constR], writes=[self.PSR[bs]])
        p.op("act", lambda e: e.activation(out=t1_, in_=self.PS[bs][0:64, 0:n], func=AF.Sqrt, scale=1.0 / 64,
                                           bias=self.epsc[0:64, 0:1]),
             reads=[self.PSR[bs], self.constR], writes=[self.TMPR[i1]])
        p.op("dve", lambda e: e.reciprocal(out=t1_, in_=t1_), reads=[self.TMPR[i1]], writes=[self.TMPR[i1]])
        p.op("dve", lambda e: e.tensor_tensor(out=t0_, in0=t0_, in1=t1_, op=ALU.mult),
             reads=[self.TMPR[i0], self.TMPR[i1]], writes=[self.TMPR[i0]])
        p.op("act", lambda e: e.activation(out=ydst, in_=t0_, func=AF.Identity, scale=self.subg[:, l:l + 1]),
             reads=[self.TMPR[i0], self.constR], writes=yres)

    def build_tc(self, l, a):
        p = self.p
        p.dma("sp", lambda e: e.dma_start(out=self.STG[0][:, 0:960], in_=self.relT_d[l, a]), self.stg_d[0],
              writes=[self.STGR[0]])
        p.dma("sp", lambda e: e.dma_start(out=self.STG[1][:, 0:960], in_=self.cmask_d), self.stg_d[1],
              writes=[self.STGR[1]])
        p.op("dve", lambda e: e.scalar_tensor_tensor(out=self.Tc, in0=self.STG[0][:, 0:960], scalar=8.0,
                                                     in1=self.STG[1][:, 0:960], op0=ALU.mult, op1=ALU.add),
             reads=[self.STGR[0], self.STGR[1]], writes=[self.CSR[0]])

    def attn_c(self, accs, e, g, pr, vsel):
        p = self.p
        t0, n = TGS[g]
        scale = 0.125
        acc = accs[0]
        sts = []
        for kb in (16, 17):
            st = self.pbank("st")
            self.mm_group(self.PS[st][:, 0:n], [(self.KT[pr, 0, kb * P:(kb + 1) * P], self.QT[pr, t0:t0 + n])],
                          reads=[self.KR[0][4], self.QR[g][e]], writes=[self.PSR[st]])
            pi = self.next_pt()
            pt = self.PT[pi][:, 0:n]
            p.op("act", lambda e_, pt=pt, st=st: e_.activation(out=pt, in_=self.PS[st][:, 0:n], func=AF.Exp, scale=scale),
                 reads=[self.PSR[st]], writes=[self.PTR[pi]])
            sts.append((pi, kb))
        for i, (pi, kb) in enumerate(sts):
            lhsT = vsel(kb)
            rhs = self.PT[pi][:, 0:n]
            p.op("pe", lambda e_, lhsT=lhsT, rhs=rhs, i=i: e_.matmul(self.PS[acc][0:65, 0:n], lhsT, rhs, start=(i == 0),
                                                                     stop=False, skip_group_check=True),
                 reads=[self.PTR[pi], self.VR[kb]], writes=[self.PSR[acc]])
        Tcv = self.Tc[pr, :].rearrange("q (b c) -> q b c", c=64)
        pend = []

        def issue_row(rr):
            r = 8 * g + rr
            rs_ = min(max(r - 4, 0), 24)
            st = self.pbank("st")
            stv = self.PS[st]

            def fn(e_, r=r, rs_=rs_, stv=stv):
                ins = None
                first = {0: True, 1: True}
                for j in range(8):
                    rk = rs_ + j
                    par = rk % 2
                    ins = e_.matmul(stv[par * 64:(par + 1) * 64, j * 64:(j + 1) * 64],
                                    self.KT[pr, 0, rk * 64:(rk + 1) * 64], self.QT[pr, r * 64:(r + 1) * 64],
                                    start=first[par], stop=False, skip_group_check=True)
                    first[par] = False
                for par in range(2):
                    j0 = (par - rs_) % 2
                    b0 = rs_ + j0 - r + 7
                    ov = stv[par * 64:(par + 1) * 64, :].rearrange("q (j c) -> q j c", c=64)[:, j0:8:2, :]
                    ins = e_.matmul(ov, self.identb[pr, pr], Tcv[:, b0:b0 + 7:2, :], start=False, stop=True,
                                    skip_group_check=True)
                return ins

            kg = sorted(set(min((rs_ + j) // 8, 3) for j in range(8)))
            p.op("pe", fn, reads=[self.KR[0][k] for k in kg] + [self.QR[g][e], self.CSR[0], self.constR],
                 writes=[self.PSR[st]])
            pi = self.next_pt()
            pt = self.PT[pi][:, :]
            p.op("act", lambda e_, pt=pt, stv=stv: e_.activation(out=pt, in_=stv[:, :], func=AF.Exp, scale=scale),
                 reads=[self.PSR[st]], writes=[self.PTR[pi]])
            return (pi, rr, rs_)

        LA = 2
        for rr in range(min(LA, 8)):
            pend.append(issue_row(rr))
        for rr in range(8):
            if rr + LA < 8:
                pend.append(issue_row(rr + LA))
            pi, rr_, rs_ = pend.pop(0)

            def fn2(e_, pi=pi, rr_=rr_, rs_=rs_):
                ins = None
                for j in range(8):
                    rk = rs_ + j
                    par = rk % 2
                    ins = e_.matmul(self.PS[accs[par]][0:65, rr_ * 64:(rr_ + 1) * 64],
                                    self.V[par * 64:(par + 1) * 64, rk // 2, e * 65:e * 65 + 65],
                                    self.PT[pi][par * 64:(par + 1) * 64, j * 64:(j + 1) * 64],
                                    start=(par == 1 and rr_ == 0 and j < 2), stop=(rr_ == 7 and j >= 6),
                                    skip_group_check=True)
                return ins

            vb = sorted(set((rs_ + j) // 2 for j in range(8)))
            p.op("pe", fn2, reads=[self.PTR[pi]] + [self.VR[b] for b in vb],
                 writes=[self.PSR[accs[0]], self.PSR[accs[1]]])


_NC_CACHE = {}


def _get_nc(stage, debug=False, skipmix=""):
    if (stage, debug, skipmix) not in _NC_CACHE:
        _NC_CACHE[(stage, debug, skipmix)] = Builder(stage, debug, skipmix).build()
    return _NC_CACHE[(stage, debug, skipmix)]


_HC = {}


def _rope_tables(dim):
    t = np.arange(S, dtype=np.int32)
    row = (t // 64).astype(np.float32)
    col = (t % 64).astype(np.float32)
    half = dim // 2
    freqs = (np.float32(10000.0) ** (-np.arange(0, half, 2, dtype=np.float32) / np.float32(half))).astype(np.float32)
    ang_r = row[:, None] * freqs[None, :]
    ang_c = col[:, None] * freqs[None, :]
    ang = np.concatenate([ang_r, ang_r, ang_c, ang_c], axis=-1).astype(np.float32)
    cos = np.cos(ang).astype(np.float32)
    sin = np.sin(ang).astype(np.float32)
    qd = half // 2
    sign = np.where((np.arange(dim) % half) < qd, -1.0, 1.0).astype(np.float32)
    tab = np.zeros((P, 2, T), np.float32)
    reps = P // dim
    tab[:, 0, :S] = np.tile(cos.T, (reps, 1))
    tab[:, 1, :S] = np.tile((sin * sign[None, :]).T, (reps, 1))
    tab[:, 0, S:] = 1.0
    return tab


def _host_consts():
    if _HC:
        return _HC
    _HC["cs64"] = _rope_tables(64)
    _HC["cs32"] = _rope_tables(32)
    NEG = -30000.0
    kk = np.arange(P)[:, None]; qq = np.arange(P)[None, :]
    Mb = np.full((P, P), NEG, np.float32)
    Ub = np.where(kk <= qq, 0.0, NEG).astype(np.float32)
    Lb = np.where(kk >= qq, 0.0, NEG).astype(np.float32)
    Zb = np.zeros((P, P), np.float32)
    _HC["strip"] = np.ascontiguousarray(np.concatenate([Mb, Mb, Mb, Ub, Zb, Lb, Mb, Mb, Mb], axis=1))
    col = np.arange(64)
    c_start = np.clip(col - 8, 0, 48)
    ok = (col[:, None] >= c_start[None, :]) & (col[:, None] < c_start[None, :] + 16)
    cm = np.where(ok, 0.0, NEG).astype(np.float32)
    _HC["cmask"] = np.ascontiguousarray(np.tile(cm, (2, 15)))
    pm = np.zeros((P, 2), np.float32)
    pm[:, 0] = ((np.arange(P) % 64) < 32)
    pm[:, 1] = ((np.arange(P) % 64) >= 32)
    _HC["pmask"] = pm
    return _HC


def kernel(stage=99, debug=False, skipmix="", **inputs):
    nc = _get_nc(stage, debug, skipmix)
    f = lambda a: np.ascontiguousarray(np.asarray(a), dtype=np.float32)
    x = f(inputs["x"]); ctx = f(inputs["ctx"]); c = f(inputs["c"]); c_ctx = f(inputs["c_ctx"])
    common = {
        "final_norm_g": f(inputs["final_norm_g"]).reshape(1, D),
        "identf": np.eye(P, dtype=np.float32),
        "w_ada": f(inputs["w_ada"]),
        "b_adaT": np.ascontiguousarray(f(inputs["b_ada"]).reshape(2, 72, P).transpose(0, 2, 1)),
    }
    for n in ("w_ffn1_gate", "w_ffn1_up", "w_ffn1_down", "w_ffn2_gate", "w_ffn2_up", "w_ffn2_down", "w_in", "w_out",
              "sink_logit"):
        common[n] = f(inputs[n])
    common.update(_host_consts())
    perm64 = np.array([(d + 16) if (d % 32) < 16 else (d - 16) for d in range(64)])
    qg = f(inputs["q_norm_g"]); kg = f(inputs["k_norm_g"])
    qkg = np.stack([np.tile(qg, (1, 2)), np.tile(qg[:, perm64], (1, 2)),
                    np.tile(kg, (1, 2)), np.tile(kg[:, perm64], (1, 2))], axis=-1)
    common["qkg"] = np.ascontiguousarray(qkg)
    common["lamv"] = np.ascontiguousarray(np.stack([f(inputs["lam_q1"]), f(inputs["lam_k1"]), f(inputs["lam_q2"]),
                                                    f(inputs["lam_k2"])], axis=1))
    common["subg"] = np.ascontiguousarray(f(inputs["subln_g"]).T)
    rel = f(inputs["rel_pos_bias"])
    ck = np.arange(64)[:, None]; cq = np.arange(64)[None, :]
    dc = np.clip(ck - cq + 15, 0, 30)
    relT = rel[:, :, :, dc]
    relT = relT.transpose(0, 1, 3, 2, 4).reshape(2, 2, P, 960)
    common["relT"] = np.ascontiguousarray(relT)
    in_maps = []
    for b in range(8):
        cc = np.stack([c[b].reshape(KC, P).T, c_ctx.reshape(KC, P).T], axis=-1).reshape(P, 16)
        m = dict(common)
        m.update({"x": x[b], "ctx": ctx[b], "cc": np.ascontiguousarray(cc)})
        in_maps.append(m)
    res = run_bass_kernel_spmd(nc, in_maps, core_ids=list(range(8)))
    if debug:
        return np.stack([r["out"] for r in res.results], axis=0), res.results[0]["dbg"]
    return np.stack([r["out"] for r in res.results], axis=0)
```

```python
import numpy as np
import concourse.bass as bass
import concourse.mybir as mybir
from concourse.bass_utils import run_bass_kernel_spmd
from contextlib import ExitStack

F32 = mybir.dt.float32
BF16 = mybir.dt.bfloat16
AF = mybir.ActivationFunctionType
ALU = mybir.AluOpType
AX = mybir.AxisListType

P = 128
D = 1024
KC = 8
S = 2048
L = 256
T = S + L
TGS = [(0, 512), (512, 512), (1024, 512), (1536, 512), (2048, 256)]
DFF = 2816
FC = 22
EPS = 1e-6
SAME_ENGINE_SYNC = True


class Res:
    __slots__ = ("name", "w", "r")

    def __init__(self, name=""):
        self.name = name
        self.w = None
        self.r = {}


class DSem:
    def __init__(self, sem, name):
        self.sem = sem
        self.count = 0
        self.name = name


class Prog:
    COMPUTE = ("pe", "act", "dve", "pool")
    ALL = ("pe", "act", "dve", "pool", "sp")

    def __init__(self, nc, es):
        self.nc = nc
        self.es = es
        self.streams = {e: [] for e in self.ALL}
        self.cnt = {e: 0 for e in self.COMPUTE}
        self.sem = {e: es.enter_context(nc.semaphore("c_" + e)) for e in self.COMPUTE}
        self.seen = {e: {} for e in self.ALL}
        self.dsems = []
        self.n_ins = 0

    def dsem(self, name):
        d = DSem(self.es.enter_context(self.nc.semaphore("d_" + name)), name)
        self.dsems.append(d)
        return d

    def _collect(self, eng, reads, writes):
        deps = {}

        def add(tok):
            if tok is None:
                return
            k, v = tok
            if deps.get(k, 0) < v:
                deps[k] = v

        for r in reads:
            add(r.w)
        for w in writes:
            add(w.w)
            for k, v in w.r.items():
                add((k, v))
        waits = []
        for k, v in deps.items():
            if k == eng and (eng == "pe" or not SAME_ENGINE_SYNC):
                continue
            if self.seen[eng].get(k, 0) >= v:
                continue
            self.seen[eng][k] = v
            waits.append((k, v))
        return waits

    def _update(self, tok, reads, writes):
        k, v = tok
        for r in reads:
            if r.r.get(k, 0) < v:
                r.r[k] = v
        for w in writes:
            w.w = tok
            w.r = {}

    def op(self, eng, fn, reads=(), writes=()):
        waits = self._collect(eng, reads, writes)
        self.cnt[eng] += 1
        tok = (eng, self.cnt[eng])
        self.streams[eng].append((waits, fn, (self.sem[eng], 1)))
        self._update(tok, reads, writes)
        return tok

    def dma(self, q, fn, dsem, reads=(), writes=()):
        waits = self._collect(q, reads, writes)
        dsem.count += 16
        tok = (dsem, dsem.count)
        self.streams[q].append((waits, fn, (dsem.sem, 16)))
        self._update(tok, reads, writes)
        return tok

    def wait_all(self, eng):
        waits = []
        for e in self.COMPUTE:
            if e != eng and self.cnt[e] > self.seen[eng].get(e, 0):
                waits.append((e, self.cnt[e]))
                self.seen[eng][e] = self.cnt[e]
        for d in self.dsems:
            if d.count > self.seen[eng].get(d, 0):
                waits.append((d, d.count))
                self.seen[eng][d] = d.count
        self.streams[eng].append((waits, None, None))

    def barrier(self):
        for e in self.ALL:
            self.wait_all(e)

    def emit(self):
        nc = self.nc
        handles = {"pe": "tensor", "act": "scalar", "dve": "vector", "pool": "gpsimd", "sp": "sync"}
        with nc.Block() as block:
            for e in self.ALL:
                recs = self.streams[e]

                def body(eng, recs=recs):
                    for waits, fn, inc in recs:
                        for k, v in waits:
                            s = self.sem[k] if isinstance(k, str) else k.sem
                            eng.wait_ge(s, v)
                        if fn is not None:
                            ins = fn(eng)
                            ins.then_inc(inc[0], inc[1])

                getattr(block, handles[e])(body)


class Builder:
    def __init__(self, stage, debug=False, skipmix=""):
        self.stage = stage
        self.debug = debug
        self.skipmix = skipmix
        self.nc = bass.Bass("TRN2", target_bir_lowering=False)
        self.es = ExitStack()

    def dram_in(self, name, shape, dt=F32):
        return self.nc.dram_tensor(name, list(shape), dt, kind="ExternalInput").ap()

    def sb(self, name, shape, dt):
        return self.es.enter_context(self.nc.sbuf_tensor("sb_" + name, list(shape), dt))

    def build(self):
        nc = self.nc
        with self.es:
            self.p = Prog(nc, self.es)
            self._build()
            self.p.emit()
        return nc

    def _build(self):
        nc, p = self.nc, self.p
        st = self.stage
        self.x_d = self.dram_in("x", [S, D])
        self.ctx_d = self.dram_in("ctx", [L, D])
        self.fng_d = self.dram_in("final_norm_g", [1, D])
        self.identf_d = self.dram_in("identf", [P, P])
        self.cc_d = self.dram_in("cc", [P, 16])
        self.wada_d = self.dram_in("w_ada", [2, D, 9 * D])
        self.bada_d = self.dram_in("b_adaT", [2, P, 72])
        self.wg_d = [self.dram_in("w_ffn1_gate", [2, D, DFF]), self.dram_in("w_ffn2_gate", [2, D, DFF])]
        self.wu_d = [self.dram_in("w_ffn1_up", [2, D, DFF]), self.dram_in("w_ffn2_up", [2, D, DFF])]
        self.wd_d = [self.dram_in("w_ffn1_down", [2, DFF, D]), self.dram_in("w_ffn2_down", [2, DFF, D])]
        self.win_d = self.dram_in("w_in", [2, D, 2560])
        self.wout_d = self.dram_in("w_out", [2, D, D])
        self.cs_d = {64: self.dram_in("cs64", [P, 2, T]), 32: self.dram_in("cs32", [P, 2, T])}
        self.strip_d = self.dram_in("strip", [P, 1152])
        self.sink_d = self.dram_in("sink_logit", [2, 4])
        self.qkg_d = self.dram_in("qkg", [2, P, 4])
        self.lamv_d = self.dram_in("lamv", [2, 4, 32])
        self.subg_d = self.dram_in("subg", [64, 2])
        self.relT_d = self.dram_in("relT", [2, 2, P, 960])
        self.cmask_d = self.dram_in("cmask", [P, 960])
        self.pmask_d = self.dram_in("pmask", [P, 2])
        self.out_d = nc.dram_tensor("out", [S, D], F32, kind="ExternalOutput").ap()
        self.dbg_d = nc.dram_tensor("dbg", [P, 6, 2048], F32, kind="ExternalOutput").ap() if self.debug else None

        self.X = self.sb("X", [P, KC, T], F32)
        self.XR = [[Res(f"X{c}_{g}") for g in range(5)] for c in range(KC)]
        self.H = self.sb("H", [P, KC, T], BF16)
        self.HR = [[Res(f"H{c}_{g}") for g in range(5)] for c in range(KC)]
        self.BIG = self.sb("BIG", [P, 5 * T + 2340], BF16)
        self.STG = [self.sb(f"stg{i}", [P, 2048], F32) for i in range(2)]
        self.STGR = [Res(f"stg{i}") for i in range(2)]
        self.stg_d = [p.dsem(f"stg{i}") for i in range(2)]
        self.stg_rr = 0
        self.stgb_d = [p.dsem(f"stgb{i}") for i in range(2)]
        NWA, NWB = 4, 4
        self.WA = [self.sb(f"wa{i}", [P, 8, 256], BF16) for i in range(NWA)]
        self.WAR = [Res(f"wa{i}") for i in range(NWA)]
        self.wa_d = [p.dsem(f"wa{i}") for i in range(NWA)]
        self.wa_rr = 0
        self.WB = [self.sb(f"wb{i}", [P, 1024], BF16) for i in range(NWB)]
        self.WBR = [Res(f"wb{i}") for i in range(NWB)]
        self.wb_d = [p.dsem(f"wb{i}") for i in range(NWB)]
        self.wb_rr = 0
        NTMP = 6
        self.TMP = [self.sb(f"tmp{i}", [P, 512], F32) for i in range(NTMP)]
        self.TMPR = [Res(f"tmp{i}") for i in range(NTMP)]
        self.tmp_rr = [0, 0, 0]
        self.tmp_any_rr = 0
        self.identf = self.sb("identf", [P, P], F32)
        self.onesf = self.sb("onesf", [P, P], F32)
        self.small = self.sb("small", [P, 64], F32)
        self.SMR = [Res(f"sm{i}") for i in range(64)]
        self.constR = Res("const")
        self.epsc = self.sb("epsc", [P, 1], F32)
        self.cc = self.sb("cc", [P, 16], F32)
        self.sT = self.sb("sT", [P, 16], BF16)
        self.sTR = Res("sT")
        self.bada = self.sb("bada", [P, 2, 72], F32)
        self.MOD = self.sb("MOD", [P, 4 * 72], F32)
        self.MODR = [Res(f"mod{i}") for i in range(4)]
        self.QT = self.BIG[:, 0:T]
        self.KT = self.BIG[:, T:3 * T].rearrange("q (c t) -> q c t", c=2)
        self.V = self.BIG[:, 3 * T:3 * T + 2340].rearrange("q (b c) -> q b c", c=130)
        self.Y = self.BIG[:, 3 * T + 2340:5 * T + 2340].rearrange("q (h t) -> q h t", h=2)
        self.PT = [self.sb(f"pt{i}", [P, 512], BF16) for i in range(6)]
        self.PTR = [Res(f"pt{i}") for i in range(6)]
        self.pt_rr = 0
        self.CS = [self.sb(f"cs{i}", [P, 2, 512], F32) for i in range(2)]
        self.CSR = [Res(f"cs{i}") for i in range(2)]
        self.cs_ds = [p.dsem(f"cs{i}") for i in range(2)]
        self.cs_rr = 0
        self.Tc = self.CS[0][:, :, :].rearrange("q a n -> q (a n)").bitcast(BF16)[:, 0:960]
        self.identb = self.sb("identb", [P, P], BF16)
        self.onesb = self.sb("onesb", [P, 64], BF16)
        self.blockones = self.sb("blockones", [P, P], F32)
        self.strip = self.sb("strip", [P, 1152], BF16)
        yoff = 3 * T + 2340
        self.yoff = yoff
        self.SK = self.sb("SK", [P, 8], F32)
        self.qkg = self.sb("qkg", [P, 2, 4], F32)
        self.lams = self.sb("lams", [P, 16], F32)
        self.LAMR = Res("lam")
        self.subg = self.sb("subg", [64, 2], F32)
        self.pmask = self.sb("pmask", [P, 2], F32)
        self.Y128 = self.BIG[:, yoff:yoff + T]
        self.RB = [self.BIG[64:65, yoff + T + j * 512:yoff + T + (j + 1) * 512] for j in range(4)]
        self.RBR = [Res(f"rb{j}") for j in range(4)]
        self.shiftI = self.sb("shiftI", [64, P], BF16)
        self.rb_rr = 0
        self.rb_of = {}
        self.RECR = [Res(f"rec{i}") for i in range(1)]
        self.rec_rr = 0
        self.PS = [self.es.enter_context(nc.psum_tensor(f"ps{i}", [P, 512], F32)) for i in range(8)]
        self.pool_rr = {"st": 0, "acc": 0}
        self.pools = {"st": (0, 1, 2, 3), "acc": (4, 5, 6, 7)}
        self.PSR = [Res(f"ps{i}") for i in range(8)]
        self.ps_rr = 0
        self.bg = []
        self.evac_rr = 0
        self.cd = p.dsem("const")
        self.out_ds = p.dsem("out")

        p.op("pool", lambda e: e.memset(self.epsc[:], EPS), writes=[self.constR])
        p.op("pool", lambda e: e.memset(self.onesf[:], 1.0), writes=[self.constR])
        self.cdma(self.identf[:], self.identf_d)
        self.cdma(self.cc[:], self.cc_d)
        self.cdma(self.bada[:], self.bada_d.rearrange("l p j -> p l j"))
        self.cdma(self.qkg[:], self.qkg_d.rearrange("l p j -> p l j"))
        self.cdma(self.pmask[:], self.pmask_d)
        self.cdma(self.SK[64:65, 0:8], self.sink_d.rearrange("(o l) h -> o (l h)", o=1))
        self.cdma(self.subg[:], self.subg_d)
        p.dma("pool", lambda e: e.dma_start(out=self.strip[:], in_=self.strip_d), p.dsem("strip"), writes=[self.constR])
        p.op("pool", lambda e: e.memset(self.onesb[:], 1.0), writes=[self.constR])
        p.op("pool", lambda e: e.memset(self.blockones[:], 0.0), writes=[self.constR])
        p.op("pool", lambda e: e.memset(self.blockones[0:64, 0:64], 1.0), writes=[self.constR])
        p.op("pool", lambda e: e.memset(self.blockones[64:128, 64:128], 1.0), writes=[self.constR])
        p.op("act", lambda e: e.copy(out=self.identb[:], in_=self.identf[:]), reads=[self.constR], writes=[self.constR])
        p.op("pool", lambda e: e.memset(self.shiftI[:], 0.0), writes=[self.constR])
        p.op("act", lambda e: e.copy(out=self.shiftI[:, 64:128], in_=self.identf[0:64, 0:64]), reads=[self.constR],
             writes=[self.constR])
        p.op("act", lambda e: e.activation(out=self.SK[64:65, 0:8], in_=self.SK[64:65, 0:8], func=AF.Exp),
             reads=[self.constR], writes=[self.constR])
        for l in range(2):
            li = 1.0 - (0.8 - 0.6 * float(np.exp(-0.3 * l)))
            p.op("dve", lambda e, l=l, li=li: e.tensor_scalar_mul(out=self.subg[:, l:l + 1], in0=self.subg[:, l:l + 1],
                                                                   scalar1=li), reads=[self.constR], writes=[self.constR])
        p.op("act", lambda e: e.activation(out=self.sT[:], in_=self.cc[:], func=AF.Silu),
             reads=[self.constR], writes=[self.sTR])

        self.load_x()
        k = 0
        for l in range(2):
            if st >= k + 1:
                if l == 0:
                    for f_ in self.adaln_steps(0, list(range(0, 6)), [0, 1, 2]):
                        f_()
                    self.bg = self.adaln_steps(0, list(range(6, 18)), [3, 4, 5, 6, 7, 8])
                self.norm_mod(l, 0, 1)
                self.ffn(l, 0, 2, ctx=True)
            if st >= k + 2:
                p.barrier()
                self.norm_mod(l, 3, 4)
                self.lam_setup(l)
                self.mixers(l, min(4, st - k - 1))
                p.barrier()
            if st >= k + 6:
                self.norm_mod(l, 6, 7, tgs=range(5) if l == 0 else range(4))
                if l == 0 and st >= 7:
                    self.bg = self.adaln_steps(1, list(range(18)), list(range(9)))
                self.ffn(l, 1, 8, ctx=(l == 0))
            k += 6
        self.final_norm()
        p.barrier()

    def dump(self, slot, ap, reads, np_=P, w=2048):
        if not self.debug:
            return
        p = self.p
        p.barrier()
        p.op("act", lambda e: e.copy(out=self.STG[0][0:np_, 0:w], in_=ap), reads=reads, writes=[self.STGR[0]])
        p.dma("sp", lambda e: e.dma_start(out=self.dbg_d[0:np_, slot, 0:w], in_=self.STG[0][0:np_, 0:w]), self.out_ds,
              reads=[self.STGR[0]])
        p.barrier()

    def cdma(self, dst, src):
        self.p.dma("sp", lambda e: e.dma_start(out=dst, in_=src), self.cd, writes=[self.constR])

    def bank(self):
        b = self.ps_rr
        self.ps_rr = (self.ps_rr + 1) % 7
        return b

    def bg_step(self):
        if self.bg:
            self.bg.pop(0)()

    def bg_flush(self):
        while self.bg:
            self.bg.pop(0)()

    def pbank(self, pool):
        lst = self.pools[pool]
        self.pool_rr[pool] = (self.pool_rr[pool] + 1) % len(lst)
        return lst[self.pool_rr[pool]]

    def tmp(self, role):
        i = role * 2 + self.tmp_rr[role]
        self.tmp_rr[role] ^= 1
        return i

    def modap(self, l, stream, r, c):
        col = (l * 2 + stream) * 72 + r * 8 + c
        return self.MOD[:, col:col + 1]

    def mm_group(self, out, pairs, reads, writes, **kw):
        n = len(pairs)

        def fn(e):
            ins = None
            for i, (lhsT, rhs) in enumerate(pairs):
                ins = e.matmul(out, lhsT, rhs, start=(i == 0), stop=(i == n - 1), **kw)
            return ins

        return self.p.op("pe", fn, reads=reads, writes=writes)

    def load_x(self):
        p = self.p
        for g, (t0, n) in enumerate(TGS):
            ntile = n // P
            for s in range(ntile // 2):
                if g < 4:
                    src = self.x_d[t0 + s * 256: t0 + s * 256 + 256, :]
                else:
                    src = self.ctx_d[s * 256: s * 256 + 256, :]
                src = src.rearrange("(j p) d -> p j d", p=P)
                dst = self.STG[s][:, :].rearrange("p (j d) -> p j d", j=2)
                p.dma("sp", lambda e, dst=dst, src=src: e.dma_start(out=dst, in_=src), self.stg_d[s],
                      writes=[self.STGR[s]])
            for c in range(KC):
                b = self.bank()

                def tr(e, b=b, c=c, ntile=ntile):
                    ins = None
                    for j in range(ntile):
                        src = self.STG[j // 2][:, (j % 2) * 1024 + c * P:(j % 2) * 1024 + (c + 1) * P]
                        ins = e.transpose(out=self.PS[b][:, j * P:(j + 1) * P], in_=src, identity=self.identf[:])
                    return ins

                p.op("pe", tr, reads=[self.STGR[s] for s in range(ntile // 2)] + [self.constR], writes=[self.PSR[b]])
                self.evac_copy(self.X[:, c, t0:t0 + n], self.PS[b][:, 0:n], [self.PSR[b]], [self.XR[c][g]])

    def evac_copy(self, dst, src, reads, writes):
        p = self.p
        self.evac_rr ^= 1
        if self.evac_rr:
            p.op("act", lambda e: e.copy(out=dst, in_=src), reads=reads, writes=writes)
        else:
            p.op("dve", lambda e: e.tensor_copy(out=dst, in_=src), reads=reads, writes=writes)

    def final_norm(self):
        p = self.p
        self.gbc = self.WA[0][:, :, :].rearrange("q k n -> q (k n)").bitcast(F32)
        p.dma("sp", lambda e: e.dma_start(out=self.gbc, in_=self.fng_d.partition_broadcast(P)), self.cd,
              writes=[self.WAR[0]])
        for tt in range(S // P):
            g = tt // 4
            s = tt % 2
            stg = self.STG[s]
            sr = self.STGR[s]
            for h in range(2):
                b = self.bank()

                def tr(e, b=b, h=h, tt=tt):
                    ins = None
                    for j in range(4):
                        c = h * 4 + j
                        ins = e.transpose(out=self.PS[b][:, j * P:(j + 1) * P],
                                          in_=self.X[:, c, tt * P:(tt + 1) * P], identity=self.identf[:])
                    return ins

                p.op("pe", tr, reads=[self.XR[h * 4 + j][g] for j in range(4)] + [self.constR],
                     writes=[self.PSR[b]])
                dst = stg[:, h * 512:(h + 1) * 512]
                src = self.PS[b][:, :]
                if h == 0:
                    p.op("act", lambda e, dst=dst, src=src: e.copy(out=dst, in_=src), reads=[self.PSR[b]], writes=[sr])
                else:
                    p.op("dve", lambda e, dst=dst, src=src: e.tensor_copy(out=dst, in_=src), reads=[self.PSR[b]],
                         writes=[sr])
            ss = self.small[:, 2 * s:2 * s + 1]
            rs = self.small[:, 2 * s + 1:2 * s + 2]
            smr = self.SMR[s]
            p.op("act", lambda e, stg=stg, ss=ss: e.activation(out=stg[:, 1024:2048], in_=stg[:, 0:1024],
                                                              func=AF.Square, accum_out=ss),
                 reads=[sr], writes=[sr, smr])
            p.op("act", lambda e, ss=ss, rs=rs: e.activation(out=rs, in_=ss, func=AF.Sqrt, scale=1.0 / D,
                                                             bias=self.epsc[:, 0:1]),
                 reads=[smr, self.constR], writes=[smr])
            p.op("dve", lambda e, rs=rs: e.reciprocal(out=rs, in_=rs), reads=[smr], writes=[smr])
            p.op("dve", lambda e, stg=stg, rs=rs: e.scalar_tensor_tensor(out=stg[:, 1024:2048], in0=stg[:, 0:1024],
                                                                        scalar=rs, in1=self.gbc,
                                                                        op0=ALU.mult, op1=ALU.mult),
                 reads=[sr, smr, self.WAR[0]], writes=[sr])
            p.dma("sp", lambda e, stg=stg, tt=tt: e.dma_start(out=self.out_d[tt * P:(tt + 1) * P, :],
                                                             in_=stg[:, 1024:2048]),
                  self.out_ds, reads=[sr])

    def adaln_steps(self, l, pieces, rows):
        p = self.p
        b = 7
        psr = self.PSR[b]
        slots = {}

        def issue(j4):
            s_ = self.stg_rr
            self.stg_rr ^= 1
            slots[j4] = s_
            src = self.wada_d[l, :, j4 * 512:(j4 + 1) * 512].rearrange("(kc q) n -> q kc n", q=P)
            stgb = self.STG[s_][:, :].bitcast(BF16)
            dst = stgb.rearrange("q (kc n) -> q kc n", kc=8)
            p.dma("pool", lambda e: e.dma_start(out=dst, in_=src), self.stgb_d[s_], writes=[self.STGR[s_]])

        def step(idx):
            j4 = pieces[idx]
            if idx == 0:
                issue(j4)
            if idx + 1 < len(pieces):
                issue(pieces[idx + 1])
            s_ = slots[j4]
            stgb = self.STG[s_][:, :].bitcast(BF16)
            for q4 in range(4):
                j = j4 * 4 + q4
                pairs = [(stgb[:, kc * 512 + q4 * 128: kc * 512 + q4 * 128 + 128],
                          self.sT[:, 2 * kc:2 * kc + 2]) for kc in range(8)]
                self.mm_group(self.PS[b][:, 2 * j:2 * j + 2], pairs, reads=[self.STGR[s_], self.sTR], writes=[psr])

        def final():
            c0, c1 = rows[0] * 8, (rows[-1] + 1) * 8
            pv = self.PS[b][:, 0:144].rearrange("q (j t) -> q j t", t=2)
            for stream in range(2):
                base = (l * 2 + stream) * 72
                mr = self.MODR[l * 2 + stream]
                dst = self.MOD[:, base + c0:base + c1]
                p.op("dve", lambda e, dst=dst, stream=stream: e.tensor_tensor(out=dst, in0=pv[:, c0:c1, stream],
                                                                              in1=self.bada[:, l, c0:c1], op=ALU.add),
                     reads=[psr, self.constR], writes=[mr])
                for r in (1, 4, 7):
                    if r in rows:
                        d2 = self.MOD[:, base + r * 8:base + r * 8 + 8]
                        p.op("dve", lambda e, d2=d2: e.tensor_scalar_add(out=d2, in0=d2, scalar1=1.0), reads=[mr],
                             writes=[mr])
                for r in (2, 8):
                    if r in rows:
                        d2 = self.MOD[:, base + r * 8:base + r * 8 + 8]
                        p.op("dve", lambda e, d2=d2: e.tensor_scalar_mul(out=d2, in0=d2, scalar1=0.5), reads=[mr],
                             writes=[mr])

        return [(lambda idx=idx: step(idx)) for idx in range(len(pieces))] + [final]

    def norm_mod(self, l, r_shift, r_scale, tgs=range(5)):
        p = self.p
        for g in tgs:
            t0, n = TGS[g]
            stream = 0 if g < 4 else 1
            mr = self.MODR[l * 2 + stream]
            b = self.bank()
            sqs = []
            for c in range(KC):
                ti = self.tmp(0)
                xs = self.X[:, c, t0:t0 + n]
                sq = self.TMP[ti][:, 0:n]
                p.op("pool", lambda e, sq=sq, xs=xs: e.tensor_tensor(out=sq, in0=xs, in1=xs, op=ALU.mult),
                     reads=[self.XR[c][g]], writes=[self.TMPR[ti]])
                p.op("pe", lambda e, sq=sq, c=c, b=b, n=n: e.matmul(self.PS[b][:, 0:n], self.onesf[:], sq,
                                                                    start=(c == 0), stop=(c == KC - 1)),
                     reads=[self.TMPR[ti], self.constR], writes=[self.PSR[b]])
            ri = self.tmp(1)
            rs = self.TMP[ri][:, 0:n]
            p.op("act", lambda e, rs=rs, b=b, n=n: e.activation(out=rs, in_=self.PS[b][:, 0:n], func=AF.Sqrt,
                                                                scale=1.0 / D, bias=self.epsc[:, 0:1]),
                 reads=[self.PSR[b], self.constR], writes=[self.TMPR[ri]])
            p.op("dve", lambda e, rs=rs: e.reciprocal(out=rs, in_=rs), reads=[self.TMPR[ri]], writes=[self.TMPR[ri]])
            for c in range(KC):
                ti = self.tmp(2)
                tt = self.TMP[ti][:, 0:n]
                xs = self.X[:, c, t0:t0 + n]
                p.op("dve", lambda e, tt=tt, xs=xs, rs=rs: e.tensor_tensor(out=tt, in0=xs, in1=rs, op=ALU.mult),
                     reads=[self.XR[c][g], self.TMPR[ri]], writes=[self.TMPR[ti]])
                hd = self.H[:, c, t0:t0 + n]
                sc = self.modap(l, stream, r_scale, c)
                sh = self.modap(l, stream, r_shift, c)
                p.op("act", lambda e, hd=hd, tt=tt, sc=sc, sh=sh: e.activation(out=hd, in_=tt, func=AF.Identity,
                                                                              scale=sc, bias=sh),
                     reads=[self.TMPR[ti], mr], writes=[self.HR[c][g]])

    def load_wa(self, src):
        i = self.wa_rr
        self.wa_rr = (self.wa_rr + 1) % len(self.WA)
        srcv = src.rearrange("(kc q) n -> q kc n", q=P)
        dst = self.WA[i][:, :, :]
        self.p.dma("pool", lambda e: e.dma_start(out=dst, in_=srcv), self.wa_d[i], writes=[self.WAR[i]])
        return i

    def load_wb(self, src):
        i = self.wb_rr
        self.wb_rr = (self.wb_rr + 1) % len(self.WB)
        dst = self.WB[i][:, :]
        self.p.dma("pool", lambda e: e.dma_start(out=dst, in_=src), self.wb_d[i], writes=[self.WBR[i]])
        return i

    def ffn(self, l, which, r_gate, ctx=True):
        p = self.p
        wg, wu, wd = self.wg_d[which][l], self.wu_d[which][l], self.wd_d[which][l]
        tgs = list(range(5)) if ctx else list(range(4))
        U = self.BIG[:, 0:4 * T].rearrange("q (f t) -> q f t", f=4)
        UR = [[Res(f"U{f}_{g}") for g in range(5)] for f in range(4)]
        npieces = FC // 2
        slabs = [list(range(i, min(i + 2, npieces))) for i in range(0, npieces, 2)]

        def issue_piece(j):
            return (self.load_wa(wg[:, j * 256:(j + 1) * 256]), self.load_wa(wu[:, j * 256:(j + 1) * 256]))

        def issue_down(slab):
            return [self.load_wb(wd[f * P:(f + 1) * P, :]) for j in slab for f in (2 * j, 2 * j + 1)]

        pend = {0: issue_piece(0)}
        pend_d = {0: issue_down(slabs[0])}
        for si, slab in enumerate(slabs):
            for j in slab:
                if j + 1 < npieces:
                    pend[j + 1] = issue_piece(j + 1)
                ig, iu = pend.pop(j)
                for half in range(2):
                    fl = (j - slab[0]) * 2 + half
                    self.bg_step()
                    for g in tgs:
                        t0, n = TGS[g]
                        hreads = [self.HR[kc][g] for kc in range(KC)]
                        bg = self.bank()
                        self.mm_group(self.PS[bg][:, 0:n],
                                      [(self.WA[ig][:, kc, half * P:(half + 1) * P], self.H[:, kc, t0:t0 + n])
                                       for kc in range(KC)], reads=hreads + [self.WAR[ig]], writes=[self.PSR[bg]])
                        bu = self.bank()
                        self.mm_group(self.PS[bu][:, 0:n],
                                      [(self.WA[iu][:, kc, half * P:(half + 1) * P], self.H[:, kc, t0:t0 + n])
                                       for kc in range(KC)], reads=hreads + [self.WAR[iu]], writes=[self.PSR[bu]])
                        ti = self.tmp(0)
                        sg = self.TMP[ti][:, 0:n]
                        p.op("act", lambda e, sg=sg, bg=bg, n=n: e.activation(out=sg, in_=self.PS[bg][:, 0:n],
                                                                              func=AF.Silu),
                             reads=[self.PSR[bg]], writes=[self.TMPR[ti]])
                        ud = U[:, fl, t0:t0 + n]
                        p.op("dve", lambda e, ud=ud, sg=sg, bu=bu, n=n: e.tensor_tensor(out=ud, in0=sg,
                                                                                        in1=self.PS[bu][:, 0:n],
                                                                                        op=ALU.mult),
                             reads=[self.TMPR[ti], self.PSR[bu]], writes=[UR[fl][g]])
            wbs = pend_d.pop(si)
            nfl = len(wbs)
            for g in tgs:
                t0, n = TGS[g]
                stream = 0 if g < 4 else 1
                mr = self.MODR[l * 2 + stream]
                for m in range(KC):
                    b = self.bank()
                    self.mm_group(self.PS[b][:, 0:n],
                                  [(self.WB[wbs[fl]][:, m * P:(m + 1) * P], U[:, fl, t0:t0 + n]) for fl in range(nfl)],
                                  reads=[UR[fl][g] for fl in range(nfl)] + [self.WBR[w] for w in wbs],
                                  writes=[self.PSR[b]])
                    xs = self.X[:, m, t0:t0 + n]
                    ga = self.modap(l, stream, r_gate, m)
                    p.op("dve", lambda e, xs=xs, b=b, n=n, ga=ga: e.scalar_tensor_tensor(
                        out=xs, in0=self.PS[b][:, 0:n], scalar=ga, in1=xs, op0=ALU.mult, op1=ALU.add),
                         reads=[self.PSR[b], mr, self.XR[m][g]], writes=[self.XR[m][g]])
            if si + 1 < len(slabs):
                pend_d[si + 1] = issue_down(slabs[si + 1])
        self.bg_flush()

    def lam_setup(self, l):
        p = self.p
        r = self.LAMR
        ti = self.tmp(0)
        lt = self.TMP[ti]
        sm = self.lams
        p.dma("sp", lambda e: e.dma_start(out=lt[0:1, 0:128], in_=self.lamv_d[l:l + 1].rearrange("o a d -> o (a d)")),
              self.cd, writes=[self.TMPR[ti], r])
        for t in range(2):
            a = lt[0:1, (2 * t) * 32:(2 * t) * 32 + 32]
            bb = lt[0:1, (2 * t + 1) * 32:(2 * t + 1) * 32 + 32]
            p.op("dve", lambda e, a=a, bb=bb: e.tensor_tensor(out=a, in0=a, in1=bb, op=ALU.mult),
                 reads=[self.constR, r], writes=[r, self.TMPR[ti]])
            p.op("dve", lambda e, a=a, t=t: e.reduce_sum(out=sm[0:1, 8 + t:9 + t], in_=a, axis=AX.X),
                 reads=[r, self.TMPR[ti]], writes=[r])
        p.op("act", lambda e: e.activation(out=sm[0:1, 8:10], in_=sm[0:1, 8:10], func=AF.Exp), reads=[r], writes=[r])
        lam_init = 0.8 - 0.6 * float(np.exp(-0.3 * l))
        p.op("dve", lambda e: e.tensor_tensor(out=sm[0:1, 10:11], in0=sm[0:1, 9:10], in1=sm[0:1, 8:9],
                                              op=ALU.subtract), reads=[r], writes=[r])
        p.op("dve", lambda e: e.tensor_scalar_add(out=sm[0:1, 10:11], in0=sm[0:1, 10:11], scalar1=-lam_init),
             reads=[r], writes=[r])
        b = self.pbank("st")
        p.op("pe", lambda e: e.matmul(self.PS[b][:, 0:1], self.onesf[0:1, :], sm[0:1, 10:11], start=True, stop=True),
             reads=[r, self.constR], writes=[self.PSR[b]])
        p.op("dve", lambda e: e.tensor_copy(out=sm[:, l:l + 1], in_=self.PS[b][:, 0:1]), reads=[self.PSR[b]],
             writes=[r])

    def load_cs(self, kind, g):
        t0, n = TGS[g]
        i = self.cs_rr
        self.cs_rr ^= 1
        src = self.cs_d[kind][:, :, t0:t0 + n]
        dst = self.CS[i][:, :, 0:n]
        self.p.dma("sp", lambda e: e.dma_start(out=dst, in_=src), self.cs_ds[i], writes=[self.CSR[i]])
        return i

    def perm_weights(self, wi, qd):
        wp = self.wa_rr
        self.wa_rr = (self.wa_rr + 1) % len(self.WA)
        sv = self.WA[wi][:, :, :].rearrange("q k (b t d) -> q k b t d", t=2, d=qd)
        dv = self.WA[wp][:, :, :].rearrange("q k (b t d) -> q k b t d", t=2, d=qd)
        self.p.op("pool", lambda e: e.tensor_copy(out=dv[:, :, :, 0, :], in_=sv[:, :, :, 1, :]),
                  reads=[self.WAR[wi]], writes=[self.WAR[wp]])
        self.p.op("pool", lambda e: e.tensor_copy(out=dv[:, :, :, 1, :], in_=sv[:, :, :, 0, :]),
                  reads=[self.WAR[wi]], writes=[self.WAR[wp]])
        return wp

    def proj_fm(self, l, wi, wpi, colsel, dsts, kind, tgs, norm_col=None, pad=False):
        p = self.p
        for g in tgs:
            t0, n = TGS[g]
            hreads = [self.HR[kc][g] for kc in range(KC)]
            bp = self.pbank("st")
            self.mm_parts(bp, n, colsel, wi, t0, hreads)
            if wpi is None:
                dst = dsts[0][0](g)
                self.evac_copy(dst, self.PS[bp][:, 0:n], [self.PSR[bp]], dsts[0][1](g))
                continue
            ci = self.load_cs(kind, g)
            cos = self.CS[ci][:, 0, 0:n]
            ssin = self.CS[ci][:, 1, 0:n]
            br = self.pbank("st")
            self.mm_parts(br, n, colsel, wpi, t0, hreads)
            i1 = self.tmp(0)
            i2 = self.tmp(1)
            t1 = self.TMP[i1][:, 0:n]
            t2 = self.TMP[i2][:, 0:n]
            pp = self.PS[bp][:, 0:n]
            pr = self.PS[br][:, 0:n]
            if norm_col is None:
                p.op("dve", lambda e, t1=t1, pp=pp, cos=cos: e.tensor_tensor(out=t1, in0=pp, in1=cos, op=ALU.mult),
                     reads=[self.PSR[bp], self.CSR[ci]], writes=[self.TMPR[i1]])
                p.op("dve", lambda e, t2=t2, pr=pr, ssin=ssin: e.tensor_tensor(out=t2, in0=pr, in1=ssin, op=ALU.mult),
                     reads=[self.PSR[br], self.CSR[ci]], writes=[self.TMPR[i2]])
                if not pad:
                    dst = dsts[0][0](g)
                    p.op("dve", lambda e, dst=dst, t1=t1, t2=t2: e.tensor_tensor(out=dst, in0=t1, in1=t2, op=ALU.add),
                         reads=[self.TMPR[i1], self.TMPR[i2]], writes=dsts[0][1](g))
                else:
                    p.op("dve", lambda e, t1=t1, t2=t2: e.tensor_tensor(out=t1, in0=t1, in1=t2, op=ALU.add),
                         reads=[self.TMPR[i1], self.TMPR[i2]], writes=[self.TMPR[i1]])
                    for k2 in range(2):
                        dst = dsts[k2][0](g)
                        m = self.pmask[:, k2:k2 + 1]
                        p.op("act", lambda e, dst=dst, t1=t1, m=m: e.activation(out=dst, in_=t1, func=AF.Identity,
                                                                                scale=m),
                             reads=[self.TMPR[i1], self.constR], writes=dsts[k2][1](g))
            else:
                i3 = self.tmp(2)
                i4 = self.tmp(2)
                sq = self.TMP[i3][:, 0:n]
                rs = self.TMP[i4][:, 0:n]
                p.op("act", lambda e, sq=sq, pp=pp: e.activation(out=sq, in_=pp, func=AF.Square),
                     reads=[self.PSR[bp]], writes=[self.TMPR[i3]])
                bs = self.pbank("st")
                p.op("pe", lambda e, bs=bs, sq=sq, n=n: e.matmul(self.PS[bs][:, 0:n], self.blockones[:], sq,
                                                                 start=True, stop=True),
                     reads=[self.TMPR[i3], self.constR], writes=[self.PSR[bs]])
                p.op("act", lambda e, rs=rs, bs=bs, n=n: e.activation(out=rs, in_=self.PS[bs][:, 0:n], func=AF.Sqrt,
                                                                      scale=1.0 / 64, bias=self.epsc[:, 0:1]),
                     reads=[self.PSR[bs], self.constR], writes=[self.TMPR[i4]])
                p.op("dve", lambda e, rs=rs: e.reciprocal(out=rs, in_=rs), reads=[self.TMPR[i4]],
                     writes=[self.TMPR[i4]])
                g1 = self.qkg[:, l, norm_col:norm_col + 1]
                g2 = self.qkg[:, l, norm_col + 1:norm_col + 2]
                p.op("dve", lambda e, t1=t1, pp=pp, cos=cos, g1=g1: e.scalar_tensor_tensor(
                    out=t1, in0=pp, scalar=g1, in1=cos, op0=ALU.mult, op1=ALU.mult),
                     reads=[self.PSR[bp], self.CSR[ci], self.constR], writes=[self.TMPR[i1]])
                p.op("dve", lambda e, t2=t2, pr=pr, ssin=ssin, g2=g2: e.scalar_tensor_tensor(
                    out=t2, in0=pr, scalar=g2, in1=ssin, op0=ALU.mult, op1=ALU.mult),
                     reads=[self.PSR[br], self.CSR[ci], self.constR], writes=[self.TMPR[i2]])
                p.op("pool", lambda e, t1=t1, t2=t2: e.tensor_tensor(out=t1, in0=t1, in1=t2, op=ALU.add),
                     reads=[self.TMPR[i1], self.TMPR[i2]], writes=[self.TMPR[i1]])
                dst = dsts[0][0](g)
                p.op("dve", lambda e, dst=dst, t1=t1, rs=rs: e.tensor_tensor(out=dst, in0=t1, in1=rs, op=ALU.mult),
                     reads=[self.TMPR[i1], self.TMPR[i4]], writes=dsts[0][1](g))

    def mm_parts(self, b, n, colsel, slot, t0, hreads):
        parts = colsel(slot, 0)

        def fn(e):
            ins = None
            for pi_ in range(len(parts)):
                for kc in range(KC):
                    psl, lhsT = colsel(slot, kc)[pi_]
                    ins = e.matmul(self.PS[b][psl, 0:n], lhsT, self.H[:, kc, t0:t0 + n], start=(kc == 0),
                                   stop=(kc == KC - 1))
            return ins

        self.p.op("pe", fn, reads=hreads + [self.WAR[slot]], writes=[self.PSR[b]])

    def proj_v(self, wi, col0):
        p = self.p
        Vv = self.V.rearrange("q b (h c) -> q b h c", c=65)
        for b0 in range(0, 18, 4):
            nb = min(4, 18 - b0)
            bk = self.pbank("st")
            for j in range(nb):
                blk = b0 + j
                g = min(blk // 4, 4)
                self.mm_group(self.PS[bk][:, j * P:(j + 1) * P],
                              [(self.H[:, kc, blk * P:(blk + 1) * P], self.WA[wi][:, kc, col0:col0 + P])
                               for kc in range(KC)],
                              reads=[self.HR[kc][g] for kc in range(KC)] + [self.WAR[wi]], writes=[self.PSR[bk]])
            dst = Vv[:, b0:b0 + nb, :, 0:64]
            src = self.PS[bk][:, 0:nb * P].rearrange("q (b h c) -> q b h c", h=2, c=64)
            self.evac_copy(dst, src, [self.PSR[bk]], [self.VR[b0 + j] for j in range(nb)])

    def next_pt(self):
        i = self.pt_rr
        self.pt_rr = (self.pt_rr + 1) % len(self.PT)
        return i

    def attn_multi(self, streams, n, scale, hooks=None):
        p = self.p
        ns = len(streams)
        ni = len(streams[0]["items"])
        seq = []
        for i in range(ni):
            for sidx in range(ns):
                seq.append((sidx, i))
        LA = len(self.PT) - 1
        pend = []

        def issue_qk(sidx, i):
            it = streams[sidx]["items"][i]
            st = self.pbank("st")
            pairs = [(it["kT"], it["q"])]
            reads = list(it["reads"])
            if it.get("extra") is not None:
                pairs.append(it["extra"])
                reads.append(self.constR)
            self.mm_group(self.PS[st][:, 0:n], pairs, reads=reads, writes=[self.PSR[st]])
            pi = self.next_pt()
            pt = self.PT[pi][:, 0:n]
            p.op("act", lambda e, pt=pt, st=st: e.activation(out=pt, in_=self.PS[st][:, 0:n], func=AF.Exp, scale=scale),
                 reads=[self.PSR[st]], writes=[self.PTR[pi]])
            return pi

        LA = min(LA, len(self.pools["st"]), len(self.PT) - 2)
        for k in range(min(LA, len(seq))):
            pend.append(issue_qk(*seq[k]))
        hooks = list(hooks) if hooks else []
        G = 2
        for k0 in range(0, len(seq), G):
            if hooks and k0 >= 4 and (k0 - 4) % 4 == 0:
                hooks.pop(0)()
            for k in range(k0, min(k0 + G, len(seq))):
                if k + LA < len(seq):
                    pend.append(issue_qk(*seq[k + LA]))
            for k in range(k0, min(k0 + G, len(seq))):
                pi = pend.pop(0)
                sidx, i = seq[k]
                stt = streams[sidx]
                blk = stt["items"][i]["vblk"]
                lhsT = stt["vsel"](blk)
                rhs = self.PT[pi][:, 0:n]
                acc = stt["acc"]
                p.op("pe", lambda e, lhsT=lhsT, rhs=rhs, i=i, acc=acc: e.matmul(self.PS[acc][0:65, 0:n], lhsT, rhs,
                                                                                start=(i == 0), stop=(i == ni - 1)),
                     reads=[self.PTR[pi], self.VR[blk]], writes=[self.PSR[acc]])
        while hooks:
            hooks.pop(0)()

    def tmp_any(self):
        i = self.tmp_any_rr
        self.tmp_any_rr = (self.tmp_any_rr + 1) % len(self.TMP)
        return i

    def fin1(self, acc, n, add_ap=None, mul_ap=None):
        p = self.p
        ti = self.tmp_any()
        t = self.TMP[ti]
        tr = self.TMPR[ti]
        p.op("act", lambda e: e.copy(out=t[0:65, 0:n], in_=self.PS[acc][0:65, 0:n]), reads=[self.PSR[acc]], writes=[tr])
        d = t[64:65, 0:n]
        if add_ap is not None:
            p.op("dve", lambda e: e.tensor_scalar_add(out=d, in0=d, scalar1=add_ap), reads=[tr, self.constR], writes=[tr])
        p.op("dve", lambda e: e.reciprocal(out=d, in_=d), reads=[tr], writes=[tr])
        if mul_ap is not None:
            p.op("dve", lambda e: e.tensor_scalar_mul(out=d, in0=d, scalar1=mul_ap), reads=[tr, self.LAMR], writes=[tr])
        j = self.rb_rr
        self.rb_rr = (self.rb_rr + 1) % len(self.RB)
        self.rb_of[ti] = j
        rb = self.RB[j][:, 0:n]
        p.op("dve", lambda e: e.tensor_copy(out=rb, in_=d), reads=[tr], writes=[self.RBR[j]])
        return ti

    def fin_bc(self, ti, n):
        p = self.p
        bc = self.pbank("st")
        t = self.TMP[ti]
        j = self.rb_of[ti]
        rb = self.RB[j][:, 0:n]
        p.op("pe", lambda e: e.matmul(self.PS[bc][0:64, 0:n], self.onesb[64:65, 0:64], rb, start=True,
                                      stop=True), reads=[self.RBR[j], self.constR], writes=[self.PSR[bc]])
        return bc

    def y_dst(self, e, g):
        t0, n = TGS[g]
        if e == 0:
            return self.Y128[0:64, t0:t0 + n], [self.YR[0][g]]
        return self.BIG[0:64, self.yoff + T + t0:self.yoff + T + t0 + n], [self.YTR[g]]

    def y_shift(self, e, g):
        if e == 0:
            return
        p = self.p
        t0, n = TGS[g]
        b = self.pbank("st")
        src = self.BIG[0:64, self.yoff + T + t0:self.yoff + T + t0 + n]
        p.op("pe", lambda e_: e_.matmul(self.PS[b][:, 0:n], self.shiftI[:, :], src, start=True, stop=True),
             reads=[self.YTR[g], self.constR], writes=[self.PSR[b]])
        dst = self.Y128[64:128, t0:t0 + n]
        p.op("dve", lambda e_: e_.tensor_copy(out=dst, in_=self.PS[b][64:128, 0:n]), reads=[self.PSR[b]],
             writes=[self.YR[1][g]])

    def fin2_simple(self, ti, n, e, g):
        p = self.p
        bc = self.fin_bc(ti, n)
        t = self.TMP[ti]
        ydst, yres = self.y_dst(e, g)
        p.op("dve", lambda e_: e_.tensor_tensor(out=ydst, in0=t[0:64, 0:n], in1=self.PS[bc][0:64, 0:n], op=ALU.mult),
             reads=[self.TMPR[ti], self.PSR[bc]], writes=yres)
        self.y_shift(e, g)

    def fin2_diff_a(self, i0, i1, n):
        p = self.p
        bc0 = self.fin_bc(i0, n)
        bc1 = self.fin_bc(i1, n)
        t0_ = self.TMP[i0][0:64, 0:n]
        t1_ = self.TMP[i1][0:64, 0:n]
        p.op("dve", lambda e: e.tensor_tensor(out=t0_, in0=t0_, in1=self.PS[bc0][0:64, 0:n], op=ALU.mult),
             reads=[self.TMPR[i0], self.PSR[bc0]], writes=[self.TMPR[i0]])
        p.op("dve", lambda e: e.tensor_tensor(out=t1_, in0=t1_, in1=self.PS[bc1][0:64, 0:n], op=ALU.mult),
             reads=[self.TMPR[i1], self.PSR[bc1]], writes=[self.TMPR[i1]])
        p.op("pool", lambda e: e.tensor_tensor(out=t0_, in0=t0_, in1=t1_, op=ALU.add),
             reads=[self.TMPR[i0], self.TMPR[i1]], writes=[self.TMPR[i0]])
        p.op("pool", lambda e: e.tensor_tensor(out=t1_, in0=t0_, in1=t0_, op=ALU.mult),
             reads=[self.TMPR[i0], self.TMPR[i1]], writes=[self.TMPR[i1]])

    def fin2_diff_b(self, l, i0, i1, n, e, g):
        p = self.p
        ydst, yres = self.y_dst(e, g)
        t0_ = self.TMP[i0][0:64, 0:n]
        t1_ = self.TMP[i1][0:64, 0:n]
        bs = self.pbank("st")
        p.op("pe", lambda e: e.matmul(self.PS[bs][0:64, 0:n], self.onesf[0:64, 0:64], t1_, start=True, stop=True),
             reads=[self.TMPR[i1], self.constR], writes=[self.PSR[bs]])
        p.op("act", lambda e: e.activation(out=t1_, in_=self.PS[bs][0:64, 0:n], func=AF.Sqrt, scale=1.0 / 64,
                                           bias=self.epsc[0:64, 0:1]),
             reads=[self.PSR[bs], self.constR], writes=[self.TMPR[i1]])
        p.op("dve", lambda e: e.reciprocal(out=t1_, in_=t1_), reads=[self.TMPR[i1]], writes=[self.TMPR[i1]])
        p.op("dve", lambda e: e.tensor_tensor(out=t0_, in0=t0_, in1=t1_, op=ALU.mult),
             reads=[self.TMPR[i0], self.TMPR[i1]], writes=[self.TMPR[i0]])
        p.op("act", lambda e_: e_.activation(out=ydst, in_=t0_, func=AF.Identity, scale=self.subg[:, l:l + 1]),
             reads=[self.TMPR[i0], self.constR], writes=yres)
        self.y_shift(e, g)

    def recip_bcast(self, acc, n, add_ap=None, mul_ap=None):
        p = self.p
        ri = 0
        rec = self.REC[ri][64:65, 0:n]
        rr = self.RECR[ri]
        den = self.PS[acc][64:65, 0:n]
        if add_ap is not None:
            p.op("dve", lambda e: e.tensor_scalar_add(out=rec, in0=den, scalar1=add_ap),
                 reads=[self.PSR[acc], self.constR], writes=[rr])
            p.op("dve", lambda e: e.reciprocal(out=rec, in_=rec), reads=[rr], writes=[rr])
        else:
            p.op("dve", lambda e: e.reciprocal(out=rec, in_=den), reads=[self.PSR[acc]], writes=[rr])
        if mul_ap is not None:
            p.op("dve", lambda e: e.tensor_scalar_mul(out=rec, in0=rec, scalar1=mul_ap), reads=[rr, self.LAMR],
                 writes=[rr])
        bc = self.pbank("st")
        p.op("pe", lambda e: e.matmul(self.PS[bc][0:64, 0:n], self.onesf[64:65, 0:64], rec, start=True, stop=True),
             reads=[rr, self.constR], writes=[self.PSR[bc]])
        return bc

    def finish_simple(self, acc, n, ydst, yres, add_ap=None):
        p = self.p
        bc = self.recip_bcast(acc, n, add_ap=add_ap)
        ti = self.tmp(0)
        t = self.TMP[ti][0:64, 0:n]
        p.op("act", lambda e: e.copy(out=t, in_=self.PS[acc][0:64, 0:n]), reads=[self.PSR[acc]], writes=[self.TMPR[ti]])
        p.op("dve", lambda e: e.tensor_tensor(out=ydst, in0=t, in1=self.PS[bc][0:64, 0:n], op=ALU.mult),
             reads=[self.TMPR[ti], self.PSR[bc]], writes=yres)

    def finish_c(self, accs, n, e, g):
        p = self.p
        ti = self.tmp_any()
        t = self.TMP[ti]
        tr = self.TMPR[ti]
        p.op("act", lambda e_: e_.copy(out=t[0:65, 0:n], in_=self.PS[accs[1]][0:65, 0:n]), reads=[self.PSR[accs[1]]],
             writes=[tr])
        p.op("dve", lambda e_: e_.tensor_tensor(out=t[0:65, 0:n], in0=t[0:65, 0:n], in1=self.PS[accs[0]][0:65, 0:n],
                                                op=ALU.add), reads=[tr, self.PSR[accs[0]]], writes=[tr])
        d = t[64:65, 0:n]
        p.op("dve", lambda e_: e_.reciprocal(out=d, in_=d), reads=[tr], writes=[tr])
        j = self.rb_rr
        self.rb_rr = (self.rb_rr + 1) % len(self.RB)
        self.rb_of[ti] = j
        rb = self.RB[j][:, 0:n]
        p.op("dve", lambda e_: e_.tensor_copy(out=rb, in_=d), reads=[tr], writes=[self.RBR[j]])
        self.fin2_simple(ti, n, e, g)

    def out_proj_load(self, l, heads_rows):
        p = self.p
        i = self.wb_rr
        self.wb_rr = (self.wb_rr + 1) % len(self.WB)
        for h, r0 in enumerate(heads_rows):
            dst = self.WB[i][64 * h:64 * h + 64, :]
            src = self.wout_d[l, r0:r0 + 64, :]
            p.dma("pool", lambda e, dst=dst, src=src: e.dma_start(out=dst, in_=src), self.wb_d[i],
                  writes=[self.WBR[i]])
        return i

    def out_proj_group(self, l, wb, g, ms, dve_only=False):
        p = self.p
        t0, n = TGS[g]
        stream = 0 if g < 4 else 1
        mr = self.MODR[l * 2 + stream]
        for m in ms:
            b = self.pbank("st")
            self.mm_group(self.PS[b][:, 0:n], [(self.WB[wb][:, m * P:(m + 1) * P], self.Y128[:, t0:t0 + n])],
                          reads=[self.YR[h][g] for h in range(2)] + [self.WBR[wb]], writes=[self.PSR[b]])
            xs = self.X[:, m, t0:t0 + n]
            ga = self.modap(l, stream, 5, m)
            if dve_only or m % 2 == 0:
                p.op("dve", lambda e, xs=xs, b=b, n=n, ga=ga: e.scalar_tensor_tensor(
                    out=xs, in0=self.PS[b][:, 0:n], scalar=ga, in1=xs, op0=ALU.mult, op1=ALU.add),
                     reads=[self.PSR[b], mr, self.XR[m][g]], writes=[self.XR[m][g]])
            else:
                ti = self.tmp(2)
                tt = self.TMP[ti][:, 0:n]
                p.op("act", lambda e, tt=tt, b=b, n=n, ga=ga: e.activation(out=tt, in_=self.PS[b][:, 0:n],
                                                                           func=AF.Identity, scale=ga),
                     reads=[self.PSR[b], mr], writes=[self.TMPR[ti]])
                p.op("pool", lambda e, xs=xs, tt=tt: e.tensor_tensor(out=xs, in0=xs, in1=tt, op=ALU.add),
                     reads=[self.TMPR[ti], self.XR[m][g]], writes=[self.XR[m][g]])

    def mixers(self, l, nmix):
        p = self.p
        win = self.win_d[l]
        qtgs = list(range(5)) if l == 0 else list(range(4))
        alltg = list(range(5))
        self.QR = [[Res(f"Q{g}_{e}") for e in range(2)] for g in range(5)]
        self.KR = [[Res(f"K{c}_{g}") for g in range(5)] for c in range(2)]
        self.VR = [Res(f"V{b}") for b in range(18)]
        self.YR = [[Res(f"Y{h}_{g}") for g in range(5)] for h in range(2)]
        self.YTR = [Res(f"Yt_{g}") for g in range(5)]
        Vv = self.V.rearrange("q b (h c) -> q b h c", c=65)

        def set_ones():
            p.op("pool", lambda e: e.memset(Vv[:, :, :, 64:65], 1.0), writes=self.VR)

        def qdst():
            return [(lambda g: self.QT[:, TGS[g][0]:TGS[g][0] + TGS[g][1]], lambda g: [self.QR[g][0], self.QR[g][1]])]

        def kdst(c):
            return (lambda g: self.KT[:, c, TGS[g][0]:TGS[g][0] + TGS[g][1]], lambda g: [self.KR[c][g]])

        nat = lambda c0: (lambda slot, kc: [(slice(0, P), self.WA[slot][:, kc, c0:c0 + P])])
        pair = lambda a: (lambda slot, kc: [(slice(0, 64), self.WA[slot][:, kc, a * 64:a * 64 + 64]),
                                            (slice(64, P), self.WA[slot][:, kc, (a + 2) * 64:(a + 2) * 64 + 64])])

        for mix in range(nmix):
            name = "ABCD"[mix]
            if name in self.skipmix:
                continue
            kind = 32 if name == "D" else 64
            qd = 8 if name == "D" else 16
            scale = (32 ** -0.5) if name == "D" else 0.125
            if name in "AB":
                self.pools = {"st": (0, 1, 2, 3, 4, 5), "acc": (6, 7)}
            else:
                self.pools = {"st": (0, 1, 2, 3), "acc": (4, 5, 6, 7)}
            self.pool_rr = {"st": 0, "acc": 0}
            for a in range(2):
                if name in "AB":
                    if a == 0:
                        set_ones()
                        wkv = self.load_wa(win[:, mix * 512 + 256: mix * 512 + 512])
                        wkvp = self.perm_weights(wkv, qd)
                        self.proj_fm(l, wkv, wkvp, nat(0), [kdst(0)], kind, alltg, norm_col=(2 if name == "B" else None))
                        self.proj_v(wkv, 128)
                    wq = self.load_wa(win[:, mix * 512: mix * 512 + 256])
                    wqp = self.perm_weights(wq, qd)
                    self.proj_fm(l, wq, wqp, pair(a), qdst(), kind, qtgs, norm_col=(0 if name == "B" else None))
                    heads = [a, a + 2]
                    kvh = [0, 1]
                elif name == "C":
                    set_ones()
                    wk = self.load_wa(win[:, 1280:1536])
                    self.proj_fm(l, wk, None, nat(a * P), [kdst(0)], kind, alltg)
                    wv = self.load_wa(win[:, 1536:1792])
                    self.proj_v(wv, a * P)
                    wq = self.load_wa(win[:, 1024:1280])
                    self.proj_fm(l, wq, None, nat(a * P), qdst(), kind, qtgs)
                    heads = [2 * a, 2 * a + 1]
                    kvh = [0, 1]
                    self.build_tc(l, a)
                else:
                    set_ones()
                    wk = self.load_wa(win[:, 2048:2304])
                    wkp = self.perm_weights(wk, qd)
                    self.proj_fm(l, wk, wkp, nat(a * P), [kdst(0), kdst(1)], kind, alltg, pad=True)
                    wv = self.load_wa(win[:, 2304:2560])
                    self.proj_v(wv, a * P)
                    wq = self.load_wa(win[:, 1792:2048])
                    wqp = self.perm_weights(wq, qd)
                    self.proj_fm(l, wq, wqp, nat(a * P), qdst(), kind, qtgs)
                    heads = [2 * a, 2 * a + 1]
                    kvh = [0, 1]
                wbs = self.out_proj_load(l, [mix * 256 + h * 64 for h in heads])
                pending = []

                def queue_outproj(g, last):
                    for ms in ((0, 1, 2, 3), (4, 5, 6, 7)):
                        pending.append(lambda g=g, ms=ms, last=last: self.out_proj_group(l, wbs, g, ms,
                                                                                        dve_only=not last))

                def run_pending():
                    while pending:
                        pending.pop(0)()

                for g in qtgs:
                    t0, n = TGS[g]
                    if g == 4:
                        kbs = [16, 17]
                    elif name == "A":
                        kbs = [kb for kb in range(4 * g - 1, 4 * g + 5) if 0 <= kb < 16] + [16, 17]
                    else:
                        kbs = list(range(18))

                    def mk(e, kb, c=0, mask=False, g=g, t0=t0, n=n):
                        pr = slice(64 * e, 64 * e + 64)
                        it = {"kT": self.KT[pr, c, kb * P:(kb + 1) * P], "q": self.QT[pr, t0:t0 + n],
                              "reads": [self.KR[c][min(kb // 4, 4)], self.QR[g][e]], "vblk": kb}
                        if mask:
                            o = kb - 4 * g
                            it["extra"] = (self.identb[:, :], self.strip[:, (4 - o) * P:(4 - o) * P + n])
                        return it

                    vsels = [(lambda blk, e=e: self.V[:, blk, kvh[e] * 65:kvh[e] * 65 + 65]) for e in range(2)]
                    if name in "AB" or (name == "C" and g == 4):
                        accs = [self.pbank("acc"), self.pbank("acc")]
                        streams = [{"items": [mk(e, kb, 0, mask=(name == "A" and g < 4 and kb < 16)) for kb in kbs],
                                    "acc": accs[e], "vsel": vsels[e]} for e in range(2)]
                        hk = list(pending)
                        del pending[:]
                        self.attn_multi(streams, n, scale, hooks=hk)
                        for e in range(2):
                            add_ap = None
                            if name == "A":
                                add_ap = self.SK[64:65, l * 4 + heads[e]:l * 4 + heads[e] + 1]
                            ti = self.fin1(accs[e], n, add_ap=add_ap)
                            pending.append(lambda ti=ti, n=n, e=e, g=g: self.fin2_simple(ti, n, e, g))
                        queue_outproj(g, g == qtgs[-1])
                    elif name == "C":
                        run_pending()
                        for e in range(2):
                            pr = slice(64 * e, 64 * e + 64)
                            accs = [self.pbank("acc"), self.pbank("acc")]
                            self.attn_c(accs, e, g, pr, vsels[e])
                            self.finish_c(accs, n, e, g)
                        queue_outproj(g, g == qtgs[-1])
                    else:
                        accs = [self.pbank("acc") for _ in range(4)]
                        streams = [{"items": [mk(e, kb, c) for kb in kbs], "acc": accs[c * 2 + e], "vsel": vsels[e]}
                                   for c in range(2) for e in range(2)]
                        hk = list(pending)
                        del pending[:]
                        self.attn_multi(streams, n, scale, hooks=hk)
                        for e in range(2):
                            i0 = self.fin1(accs[e], n)
                            i1 = self.fin1(accs[2 + e], n, mul_ap=self.lams[64:65, l:l + 1])
                            pending.append(lambda i0=i0, i1=i1, n=n: self.fin2_diff_a(i0, i1, n))
                            pending.append(lambda i0=i0, i1=i1, n=n, e=e, g=g:
                                           self.fin2_diff_b(l, i0, i1, n, e, g))
                        queue_outproj(g, g == qtgs[-1])
                run_pending()
                if mix == 0 and a == 0 and l == 0:
                    self.dump(0, self.QT[:, 0:2048], [])
                    self.dump(1, self.KT[:, 0, 0:2048], [])
                    self.dump(2, self.BIG[:, 3 * T:3 * T + 2048], [])
                    self.dump(3, self.Y[0:64, 0, 0:2048], [], np_=64)
                    self.dump(4, self.Y[0:64, 1, 0:2048], [], np_=64)
                    self.dump(5, self.H[:, 0, 0:2048], [])

    def finish_diff(self, l, acc0, acc1, n, ydst, yres):
        p = self.p
        bc0 = self.recip_bcast(acc0, n)
        i0 = self.tmp(0)
        t0_ = self.TMP[i0][0:64, 0:n]
        p.op("act", lambda e: e.copy(out=t0_, in_=self.PS[acc0][0:64, 0:n]), reads=[self.PSR[acc0]],
             writes=[self.TMPR[i0]])
        p.op("dve", lambda e: e.tensor_tensor(out=t0_, in0=t0_, in1=self.PS[bc0][0:64, 0:n], op=ALU.mult),
             reads=[self.TMPR[i0], self.PSR[bc0]], writes=[self.TMPR[i0]])
        bc1 = self.recip_bcast(acc1, n, mul_ap=self.lams[64:65, l:l + 1])
        i1 = self.tmp(1)
        t1_ = self.TMP[i1][0:64, 0:n]
        p.op("act", lambda e: e.copy(out=t1_, in_=self.PS[acc1][0:64, 0:n]), reads=[self.PSR[acc1]],
             writes=[self.TMPR[i1]])
        p.op("dve", lambda e: e.tensor_tensor(out=t1_, in0=t1_, in1=self.PS[bc1][0:64, 0:n], op=ALU.mult),
             reads=[self.TMPR[i1], self.PSR[bc1]], writes=[self.TMPR[i1]])
        p.op("pool", lambda e: e.tensor_tensor(out=t0_, in0=t0_, in1=t1_, op=ALU.add),
             reads=[self.TMPR[i0], self.TMPR[i1]], writes=[self.TMPR[i0]])
        p.op("pool", lambda e: e.tensor_tensor(out=t1_, in0=t0_, in1=t0_, op=ALU.mult),
             reads=[self.TMPR[i0], self.TMPR[i1]], writes=[self.TMPR[i1]])
        bs = self.pbank("st")
        p.op("pe", lambda e: e.matmul(self.PS[bs][0:64, 0:n], self.onesf[0:64, 0:64], t1_, start=True, stop=True),
             reads=[self.TMPR[i1], self.constR], writes=[self.PSR[bs]])
        p.op("act", lambda e: e.activation(out=t1_, in_=self.PS[bs][0:64, 0:n], func=AF.Sqrt, scale=1.0 / 64,
                                           bias=self.epsc[0:64, 0:1]),
             reads=[self.PSR[bs], self.constR], writes=[self.TMPR[i1]])
        p.op("dve", lambda e: e.reciprocal(out=t1_, in_=t1_), reads=[self.TMPR[i1]], writes=[self.TMPR[i1]])
        p.op("dve", lambda e: e.tensor_tensor(out=t0_, in0=t0_, in1=t1_, op=ALU.mult),
             reads=[self.TMPR[i0], self.TMPR[i1]], writes=[self.TMPR[i0]])
        p.op("act", lambda e: e.activation(out=ydst, in_=t0_, func=AF.Identity, scale=self.subg[:, l:l + 1]),
             reads=[self.TMPR[i0], self.constR], writes=yres)

    def build_tc(self, l, a):
        p = self.p
        p.dma("sp", lambda e: e.dma_start(out=self.STG[0][:, 0:960], in_=self.relT_d[l, a]), self.stg_d[0],
              writes=[self.STGR[0]])
        p.dma("sp", lambda e: e.dma_start(out=self.STG[1][:, 0:960], in_=self.cmask_d), self.stg_d[1],
              writes=[self.STGR[1]])
        p.op("dve", lambda e: e.scalar_tensor_tensor(out=self.Tc, in0=self.STG[0][:, 0:960], scalar=8.0,
                                                     in1=self.STG[1][:, 0:960], op0=ALU.mult, op1=ALU.add),
             reads=[self.STGR[0], self.STGR[1]], writes=[self.CSR[0]])

    def attn_c(self, accs, e, g, pr, vsel):
        p = self.p
        t0, n = TGS[g]
        scale = 0.125
        acc = accs[0]
        sts = []
        for kb in (16, 17):
            st = self.pbank("st")
            self.mm_group(self.PS[st][:, 0:n], [(self.KT[pr, 0, kb * P:(kb + 1) * P], self.QT[pr, t0:t0 + n])],
                          reads=[self.KR[0][4], self.QR[g][e]], writes=[self.PSR[st]])
            pi = self.next_pt()
            pt = self.PT[pi][:, 0:n]
            p.op("act", lambda e_, pt=pt, st=st: e_.activation(out=pt, in_=self.PS[st][:, 0:n], func=AF.Exp, scale=scale),
                 reads=[self.PSR[st]], writes=[self.PTR[pi]])
            sts.append((pi, kb))
        for i, (pi, kb) in enumerate(sts):
            lhsT = vsel(kb)
            rhs = self.PT[pi][:, 0:n]
            p.op("pe", lambda e_, lhsT=lhsT, rhs=rhs, i=i: e_.matmul(self.PS[acc][0:65, 0:n], lhsT, rhs, start=(i == 0),
                                                                     stop=False, skip_group_check=True),
                 reads=[self.PTR[pi], self.VR[kb]], writes=[self.PSR[acc]])
        Tcv = self.Tc[pr, :].rearrange("q (b c) -> q b c", c=64)
        pend = []

        def issue_row(rr):
            r = 8 * g + rr
            rs_ = min(max(r - 4, 0), 24)
            st = self.pbank("st")
            stv = self.PS[st]

            def fn(e_, r=r, rs_=rs_, stv=stv):
                ins = None
                first = {0: True, 1: True}
                for j in range(8):
                    rk = rs_ + j
                    par = rk % 2
                    ins = e_.matmul(stv[par * 64:(par + 1) * 64, j * 64:(j + 1) * 64],
                                    self.KT[pr, 0, rk * 64:(rk + 1) * 64], self.QT[pr, r * 64:(r + 1) * 64],
                                    start=first[par], stop=False, skip_group_check=True)
                    first[par] = False
                for par in range(2):
                    j0 = (par - rs_) % 2
                    b0 = rs_ + j0 - r + 7
                    ov = stv[par * 64:(par + 1) * 64, :].rearrange("q (j c) -> q j c", c=64)[:, j0:8:2, :]
                    ins = e_.matmul(ov, self.identb[pr, pr], Tcv[:, b0:b0 + 7:2, :], start=False, stop=True,
                                    skip_group_check=True)
                return ins

            kg = sorted(set(min((rs_ + j) // 8, 3) for j in range(8)))
            p.op("pe", fn, reads=[self.KR[0][k] for k in kg] + [self.QR[g][e], self.CSR[0], self.constR],
                 writes=[self.PSR[st]])
            pi = self.next_pt()
            pt = self.PT[pi][:, :]
            p.op("act", lambda e_, pt=pt, stv=stv: e_.activation(out=pt, in_=stv[:, :], func=AF.Exp, scale=scale),
                 reads=[self.PSR[st]], writes=[self.PTR[pi]])
            return (pi, rr, rs_)

        LA = 2
        for rr in range(min(LA, 8)):
            pend.append(issue_row(rr))
        for rr in range(8):
            if rr + LA < 8:
                pend.append(issue_row(rr + LA))
            pi, rr_, rs_ = pend.pop(0)

            def fn2(e_, pi=pi, rr_=rr_, rs_=rs_):
                ins = None
                for j in range(8):
                    rk = rs_ + j
                    par = rk % 2
                    ins = e_.matmul(self.PS[accs[par]][0:65, rr_ * 64:(rr_ + 1) * 64],
                                    self.V[par * 64:(par + 1) * 64, rk // 2, e * 65:e * 65 + 65],
                                    self.PT[pi][par * 64:(par + 1) * 64, j * 64:(j + 1) * 64],
                                    start=(par == 1 and rr_ == 0 and j < 2), stop=(rr_ == 7 and j >= 6),
                                    skip_group_check=True)
                return ins

            vb = sorted(set((rs_ + j) // 2 for j in range(8)))
            p.op("pe", fn2, reads=[self.PTR[pi]] + [self.VR[b] for b in vb],
                 writes=[self.PSR[accs[0]], self.PSR[accs[1]]])


_NC_CACHE = {}


def _get_nc(stage, debug=False, skipmix=""):
    if (stage, debug, skipmix) not in _NC_CACHE:
        _NC_CACHE[(stage, debug, skipmix)] = Builder(stage, debug, skipmix).build()
    return _NC_CACHE[(stage, debug, skipmix)]


_HC = {}


def _rope_tables(dim):
    t = np.arange(S, dtype=np.int32)
    row = (t // 64).astype(np.float32)
    col = (t % 64).astype(np.float32)
    half = dim // 2
    freqs = (np.float32(10000.0) ** (-np.arange(0, half, 2, dtype=np.float32) / np.float32(half))).astype(np.float32)
    ang_r = row[:, None] * freqs[None, :]
    ang_c = col[:, None] * freqs[None, :]
    ang = np.concatenate([ang_r, ang_r, ang_c, ang_c], axis=-1).astype(np.float32)
    cos = np.cos(ang).astype(np.float32)
    sin = np.sin(ang).astype(np.float32)
    qd = half // 2
    sign = np.where((np.arange(dim) % half) < qd, -1.0, 1.0).astype(np.float32)
    tab = np.zeros((P, 2, T), np.float32)
    reps = P // dim
    tab[:, 0, :S] = np.tile(cos.T, (reps, 1))
    tab[:, 1, :S] = np.tile((sin * sign[None, :]).T, (reps, 1))
    tab[:, 0, S:] = 1.0
    return tab


def _host_consts():
    if _HC:
        return _HC
    _HC["cs64"] = _rope_tables(64)
    _HC["cs32"] = _rope_tables(32)
    NEG = -30000.0
    kk = np.arange(P)[:, None]; qq = np.arange(P)[None, :]
    Mb = np.full((P, P), NEG, np.float32)
    Ub = np.where(kk <= qq, 0.0, NEG).astype(np.float32)
    Lb = np.where(kk >= qq, 0.0, NEG).astype(np.float32)
    Zb = np.zeros((P, P), np.float32)
    _HC["strip"] = np.ascontiguousarray(np.concatenate([Mb, Mb, Mb, Ub, Zb, Lb, Mb, Mb, Mb], axis=1))
    col = np.arange(64)
    c_start = np.clip(col - 8, 0, 48)
    ok = (col[:, None] >= c_start[None, :]) & (col[:, None] < c_start[None, :] + 16)
    cm = np.where(ok, 0.0, NEG).astype(np.float32)
    _HC["cmask"] = np.ascontiguousarray(np.tile(cm, (2, 15)))
    pm = np.zeros((P, 2), np.float32)
    pm[:, 0] = ((np.arange(P) % 64) < 32)
    pm[:, 1] = ((np.arange(P) % 64) >= 32)
    _HC["pmask"] = pm
    return _HC


def kernel(stage=99, debug=False, skipmix="", **inputs):
    nc = _get_nc(stage, debug, skipmix)
    f = lambda a: np.ascontiguousarray(np.asarray(a), dtype=np.float32)
    x = f(inputs["x"]); ctx = f(inputs["ctx"]); c = f(inputs["c"]); c_ctx = f(inputs["c_ctx"])
    common = {
        "final_norm_g": f(inputs["final_norm_g"]).reshape(1, D),
        "identf": np.eye(P, dtype=np.float32),
        "w_ada": f(inputs["w_ada"]),
        "b_adaT": np.ascontiguousarray(f(inputs["b_ada"]).reshape(2, 72, P).transpose(0, 2, 1)),
    }
    for n in ("w_ffn1_gate", "w_ffn1_up", "w_ffn1_down", "w_ffn2_gate", "w_ffn2_up", "w_ffn2_down", "w_in", "w_out",
              "sink_logit"):
        common[n] = f(inputs[n])
    common.update(_host_consts())
    perm64 = np.array([(d + 16) if (d % 32) < 16 else (d - 16) for d in range(64)])
    qg = f(inputs["q_norm_g"]); kg = f(inputs["k_norm_g"])
    qkg = np.stack([np.tile(qg, (1, 2)), np.tile(qg[:, perm64], (1, 2)),
                    np.tile(kg, (1, 2)), np.tile(kg[:, perm64], (1, 2))], axis=-1)
    common["qkg"] = np.ascontiguousarray(qkg)
    common["lamv"] = np.ascontiguousarray(np.stack([f(inputs["lam_q1"]), f(inputs["lam_k1"]), f(inputs["lam_q2"]),
                                                    f(inputs["lam_k2"])], axis=1))
    common["subg"] = np.ascontiguousarray(f(inputs["subln_g"]).T)
    rel = f(inputs["rel_pos_bias"])
    ck = np.arange(64)[:, None]; cq = np.arange(64)[None, :]
    dc = np.clip(ck - cq + 15, 0, 30)
    relT = rel[:, :, :, dc]
    relT = relT.transpose(0, 1, 3, 2, 4).reshape(2, 2, P, 960)
    common["relT"] = np.ascontiguousarray(relT)
    in_maps = []
    for b in range(8):
        cc = np.stack([c[b].reshape(KC, P).T, c_ctx.reshape(KC, P).T], axis=-1).reshape(P, 16)
        m = dict(common)
        m.update({"x": x[b], "ctx": ctx[b], "cc": np.ascontiguousarray(cc)})
        in_maps.append(m)
    res = run_bass_kernel_spmd(nc, in_maps, core_ids=list(range(8)))
    if debug:
        return np.stack([r["out"] for r in res.results], axis=0), res.results[0]["dbg"]
    return np.stack([r["out"] for r in res.results], axis=0)
```
